# Optimizing a Trainium2 kernel written in Bass

```python
import math
import jax, jax.numpy as jnp
from jax import lax
import numpy as np

D_MODEL = 1024
BATCH = 16
SEQ = 2048
DEPTH = 2

GRID_W = 64
CTX_LEN = 256

DN_HEADS = 4
DN_DK = 128
DN_DV = 128
DN_CONV = 3
DN_QKV = DN_HEADS * (2 * DN_DK + DN_DV)
RET_HEADS = 4
RET_DK = 128
RET_DV = 128
CHUNK = 64
EVEN_SIZES = (DN_QKV, DN_HEADS * DN_DV, 2 * DN_HEADS, 2 * DN_HEADS,
              RET_HEADS * RET_DK, RET_HEADS * RET_DK, RET_HEADS * RET_DV, RET_HEADS * RET_DV)
EVEN_IN = sum(EVEN_SIZES)
EVEN_MIX = DN_HEADS * DN_DV + RET_HEADS * RET_DV
DIFF_HEADS = 8
DIFF_HD = 64
DIFF_DV = 2 * DIFF_HD
ODD_IN = DIFF_HEADS * (4 * DIFF_HD + DIFF_DV)
ODD_MIX = DIFF_HEADS * DIFF_DV
Q_BLOCK = 128
D_FF = 2816
FFN_CONV = 3

ROPE_BASE = 10000.0
LN_EPS = 1e-5
NORM_EPS = 1e-6
DEEP_ALPHA = (2 * DEPTH) ** 0.25
DEEP_BETA = (8 * DEPTH) ** -0.25
N_EVEN = (DEPTH + 1) // 2
N_ODD = DEPTH // 2

kernel_name = 'hybrid_deltanet_retention_diffattn_dit'


def layer_norm(x, g, b):
    xf = x.astype(jnp.float32)
    mu = jnp.mean(xf, -1, keepdims=True)
    var = jnp.mean(jnp.square(xf - mu), -1, keepdims=True)
    y = (xf - mu) * lax.rsqrt(var + LN_EPS) * g.astype(jnp.float32) + b.astype(jnp.float32)
    return y.astype(x.dtype)


def rms_norm(x):
    xf = x.astype(jnp.float32)
    return xf * lax.rsqrt(jnp.mean(jnp.square(xf), -1, keepdims=True) + NORM_EPS)


def group_norm(x):
    xf = x.astype(jnp.float32)
    mu = jnp.mean(xf, -1, keepdims=True)
    var = jnp.mean(jnp.square(xf - mu), -1, keepdims=True)
    return (xf - mu) * lax.rsqrt(var + NORM_EPS)


def l2_normalize(x):
    xf = x.astype(jnp.float32)
    return xf * lax.rsqrt(jnp.sum(jnp.square(xf), -1, keepdims=True) + NORM_EPS)


def modulate(x, shift, scale):
    return x * (1 + scale) + shift


def to_heads(x, h):
    b, l, _ = x.shape
    return x.reshape(b, l, h, -1).transpose(0, 2, 1, 3)


def merge_heads(x):
    b, h, l, d = x.shape
    return x.transpose(0, 2, 1, 3).reshape(b, l, h * d)


def conv1d_centred(x, w):
    k = w.shape[0]
    p = k // 2
    l = x.shape[1]
    xp = jnp.pad(x, ((0, 0), (p, p), (0, 0)))
    out = xp[:, 0:l] * w[0]
    for i in range(1, k):
        out = out + xp[:, i:i + l] * w[i]
    return out


def dwconv2d_centred(x, w):
    k = w.shape[0]
    p = k // 2
    r, c = x.shape[1], x.shape[2]
    xp = jnp.pad(x, ((0, 0), (p, p), (p, p), (0, 0)))
    out = jnp.zeros_like(x)
    for i in range(k):
        for j in range(k):
            out = out + xp[:, i:i + r, j:j + c] * w[i, j]
    return out


def rope_1d(pos, dim):
    inv = ROPE_BASE ** (-jnp.arange(dim // 2, dtype=jnp.float32) / (dim // 2))
    return pos.astype(jnp.float32)[:, None] * inv[None]


def rope_2d(row, col, dim):
    n = dim // 4
    inv = ROPE_BASE ** (-jnp.arange(n, dtype=jnp.float32) / n)
    return jnp.concatenate([row.astype(jnp.float32)[:, None] * inv[None],
                            col.astype(jnp.float32)[:, None] * inv[None]], -1)


def apply_rotary(x, cos, sin):
    x1, x2 = jnp.split(x, 2, -1)
    return jnp.concatenate([x1 * cos - x2 * sin, x1 * sin + x2 * cos], -1)


def flip_seq(t):
    return jnp.flip(t, axis=2)


def gated_delta_chunked(q, k, v, log_a, beta, s0):
    f32 = jnp.float32
    q, k, v, log_a, beta = (t.astype(f32) for t in (q, k, v, log_a, beta))
    b, h, l, dk = q.shape
    dv = v.shape[-1]
    n = l // CHUNK
    q = q.reshape(b, h, n, CHUNK, dk)
    k = k.reshape(b, h, n, CHUNK, dk)
    v = v.reshape(b, h, n, CHUNK, dv)
    beta = beta.reshape(b, h, n, CHUNK)
    g = jnp.cumsum(log_a.reshape(b, h, n, CHUNK), -1)
    tri = jnp.tril(jnp.ones((CHUNK, CHUNK), bool))
    strict = jnp.tril(jnp.ones((CHUNK, CHUNK), bool), -1)
    dec_incl = jnp.exp(jnp.where(tri, g[..., :, None] - g[..., None, :], -jnp.inf))
    dec_strict = jnp.where(strict, dec_incl, 0.0)
    kb = k * beta[..., None]
    a_mat = jnp.einsum('bhnid,bhnjd->bhnij', kb, k) * dec_strict
    m_mat = a_mat + jnp.eye(CHUNK, dtype=f32)
    rhs = jnp.concatenate([v * beta[..., None], kb * jnp.exp(g)[..., None]], -1)
    sol = lax.linalg.triangular_solve(m_mat, rhs, left_side=True, lower=True, unit_diagonal=True)
    u_c, w_c = sol[..., :dv], sol[..., dv:]
    qk = jnp.einsum('bhnid,bhnjd->bhnij', q, k) * dec_incl
    q_g = q * jnp.exp(g)[..., None]
    k_g = k * jnp.exp(g[..., -1:] - g)[..., None]
    c_dec = jnp.exp(g[..., -1])
    xs = tuple(jnp.moveaxis(t, 2, 0) for t in (u_c, w_c, qk, q_g, k_g, c_dec))

    def step(s, inp):
        u_n, w_n, qk_n, qg_n, kg_n, cd_n = inp
        v_new = u_n - jnp.einsum('bhck,bhkv->bhcv', w_n, s)
        o = jnp.einsum('bhck,bhkv->bhcv', qg_n, s) + jnp.einsum('bhij,bhjv->bhiv', qk_n, v_new)
        s = cd_n[..., None, None] * s + jnp.einsum('bhck,bhcv->bhkv', kg_n, v_new)
        return s, o

    s, o = lax.scan(step, s0.astype(f32), xs)
    return jnp.moveaxis(o, 0, 2).reshape(b, h, l, dv), s


def retention_chunked(q, k, v, log_gamma, s0):
    f32 = jnp.float32
    q, k, v = (t.astype(f32) for t in (q, k, v))
    b, h, l, dk = q.shape
    dv = v.shape[-1]
    n = l // CHUNK
    q = q.reshape(b, h, n, CHUNK, dk)
    k = k.reshape(b, h, n, CHUNK, dk)
    v = v.reshape(b, h, n, CHUNK, dv)
    lg = log_gamma.astype(f32)[:, None]
    idx = jnp.arange(CHUNK, dtype=f32)
    rel = idx[:, None] - idx[None, :]
    dec = jnp.exp(jnp.where(rel >= 0, lg[:, :, None] * rel, -jnp.inf))
    o_in = jnp.einsum('bhnij,bhnjv->bhniv',
                      jnp.einsum('bhnid,bhnjd->bhnij', q, k) * dec[None, :, None], v)
    q_d = q * jnp.exp(lg * (idx + 1))[None, :, None, :, None]
    k_d = k * jnp.exp(lg * (CHUNK - 1 - idx))[None, :, None, :, None]
    c_dec = jnp.exp(lg[:, 0] * CHUNK)[None, :, None, None]
    xs = tuple(jnp.moveaxis(t, 2, 0) for t in (o_in, q_d, k_d, v))

    def step(s, inp):
        o_n, q_n, k_n, v_n = inp
        o = o_n + jnp.einsum('bhck,bhkv->bhcv', q_n, s)
        s = c_dec * s + jnp.einsum('bhck,bhcv->bhkv', k_n, v_n)
        return s, o

    s, o = lax.scan(step, s0.astype(f32), xs)
    return jnp.moveaxis(o, 0, 2).reshape(b, h, l, dv), s


def bidir_delta(q, k, v, log_a, beta, s_init):
    o_f, s_f = gated_delta_chunked(q, k, v, log_a[0], beta[0], s_init[0])
    o_b, s_b = gated_delta_chunked(flip_seq(q), flip_seq(k), flip_seq(v),
                                   flip_seq(log_a[1]), flip_seq(beta[1]), s_init[1])
    return o_f + flip_seq(o_b), (s_f, s_b)


def bidir_retention(q, k, v, log_gamma, s_init):
    o_f, s_f = retention_chunked(q, k, v, log_gamma[0], s_init[0])
    o_b, s_b = retention_chunked(flip_seq(q), flip_seq(k), flip_seq(v), log_gamma[1], s_init[1])
    return o_f + flip_seq(o_b), (s_f, s_b)


def even_mixer(h, hc, w_in, conv_w, a_log, dt_bias, norm_w, ret_decay, w_out, rot, need_ctx):
    f32 = jnp.float32
    splits = np.cumsum(EVEN_SIZES)[:-1].tolist()

    def project(t, rot_t):
        bt, lt = t.shape[:2]
        qkv, z, a_raw, b_raw, rq, rk, rv, rg = jnp.split(t @ w_in, splits, -1)
        qkv = jax.nn.silu(conv1d_centred(qkv, conv_w))
        q, k, v = jnp.split(qkv, [DN_HEADS * DN_DK, 2 * DN_HEADS * DN_DK], -1)
        q = l2_normalize(to_heads(q, DN_HEADS)) * DN_DK ** -0.5
        k = l2_normalize(to_heads(k, DN_HEADS))
        v = to_heads(v, DN_HEADS)
        a_raw = a_raw.reshape(bt, lt, 2, DN_HEADS).transpose(2, 0, 3, 1).astype(f32)
        b_raw = b_raw.reshape(bt, lt, 2, DN_HEADS).transpose(2, 0, 3, 1).astype(f32)
        log_a = -jnp.exp(a_log.astype(f32))[:, None, :, None] * jax.nn.softplus(
            a_raw + dt_bias.astype(f32)[:, None, :, None])
        beta = jax.nn.sigmoid(b_raw)
        rq = rq.reshape(bt, lt, RET_HEADS, RET_DK)
        rk = rk.reshape(bt, lt, RET_HEADS, RET_DK)
        if rot_t is not None:
            rq = apply_rotary(rq, *rot_t)
            rk = apply_rotary(rk, *rot_t)
        rq = rq.transpose(0, 2, 1, 3)
        rk = rk.transpose(0, 2, 1, 3) * RET_DK ** -0.5
        rv = to_heads(rv, RET_HEADS)
        return (q, k, v, log_a, beta), (rq, rk, rv), z, rg

    def finish(o_dn, z, o_ret, g):
        dn = rms_norm(o_dn) * norm_w.astype(f32) * jax.nn.silu(to_heads(z, DN_HEADS).astype(f32))
        ret = merge_heads(group_norm(o_ret)) * jax.nn.silu(g.astype(f32))
        y = jnp.concatenate([merge_heads(dn), ret], -1)
        return y.astype(w_out.dtype) @ w_out

    log_gamma = -jnp.exp(ret_decay.astype(f32))
    dn_c, ret_c, z_c, g_c = project(hc, None)
    dn_l, ret_l, z_l, g_l = project(h, rot)
    bsz = h.shape[0]
    s0_dn = jnp.zeros((bsz, DN_HEADS, DN_DK, DN_DV), f32)
    s0_ret = jnp.zeros((bsz, RET_HEADS, RET_DK, RET_DV), f32)
    o_dn_c, st_dn = bidir_delta(*dn_c, (s0_dn, s0_dn))
    o_ret_c, st_ret = bidir_retention(*ret_c, log_gamma, (s0_ret, s0_ret))
    o_dn_l, _ = bidir_delta(*dn_l, st_dn)
    o_ret_l, _ = bidir_retention(*ret_l, log_gamma, st_ret)
    y = finish(o_dn_l, z_l, o_ret_l, g_l).astype(h.dtype)
    yc = finish(o_dn_c, z_c, o_ret_c, g_c).astype(h.dtype) if need_ctx else None
    return y, yc


def diff_mix(q, k, v, lam):
    s = jnp.einsum('cbhqd,cbhkd->cbhqk', q, k).astype(jnp.float32) * DIFF_HD ** -0.5
    p = jax.nn.softmax(s, -1)
    a = p[0] - lam * p[1]
    return jnp.einsum('bhqk,bhkd->bhqd', a.astype(v.dtype), v)


def odd_mixer(h, hc, w_qkv, lam_p, subln_w, w_out, rot, lambda_init, need_ctx):
    f32 = jnp.float32

    def project(t, rot_t):
        bt, lt = t.shape[:2]
        q, k, v = jnp.split(t @ w_qkv, [DIFF_HEADS * 2 * DIFF_HD, 2 * DIFF_HEADS * 2 * DIFF_HD], -1)
        q = q.reshape(bt, lt, DIFF_HEADS, 2, DIFF_HD)
        k = k.reshape(bt, lt, DIFF_HEADS, 2, DIFF_HD)
        if rot_t is not None:
            q = apply_rotary(q, *rot_t)
            k = apply_rotary(k, *rot_t)
        return q.transpose(3, 0, 2, 1, 4), k.transpose(3, 0, 2, 1, 4), to_heads(v, DIFF_HEADS)

    lp = lam_p.astype(f32)
    lam = jnp.exp(jnp.sum(lp[0] * lp[1])) - jnp.exp(jnp.sum(lp[2] * lp[3])) + lambda_init
    qc, kc, vc = project(hc, None)
    ql, kl, vl = project(h, rot)
    k_all = jnp.concatenate([kl, kc], axis=3)
    v_all = jnp.concatenate([vl, vc], axis=2)
    bsz, nh, l = ql.shape[1], ql.shape[2], ql.shape[3]
    nb = l // Q_BLOCK
    qb = ql.reshape(2, bsz, nh, nb, Q_BLOCK, DIFF_HD).transpose(3, 0, 1, 2, 4, 5)
    ob = lax.map(lambda qq: diff_mix(qq, k_all, v_all, lam), qb)
    o_l = ob.transpose(1, 2, 0, 3, 4).reshape(bsz, nh, l, DIFF_DV)

    def finish(o):
        o = rms_norm(o) * subln_w.astype(f32) * (1.0 - lambda_init)
        return (merge_heads(o).astype(w_out.dtype) @ w_out).astype(h.dtype)

    y = finish(o_l)
    yc = finish(diff_mix(qc, kc, vc, lam)) if need_ctx else None
    return y, yc


def conv_ffn(h, rows, w_gate, w_up, w_conv, w_down):
    bt, lt, _ = h.shape
    a = h @ w_gate
    a = dwconv2d_centred(a.reshape(bt, rows, lt // rows, D_FF), w_conv).reshape(bt, lt, D_FF)
    return (jax.nn.silu(a) * (h @ w_up)) @ w_down


def setup_inputs(seed: int = 0) -> dict:
    key = jax.random.key(seed)
    ks = jax.random.split(key, 24)
    f32 = jnp.float32

    def nrm(k, shape, s):
        return jax.random.normal(k, shape, f32) * s

    x = nrm(ks[0], (BATCH, SEQ, D_MODEL), 1.0)
    c = nrm(ks[1], (BATCH, D_MODEL), 1.0)
    ctx = nrm(ks[2], (BATCH, CTX_LEN, D_MODEL), 1.0)
    c_ctx = nrm(ks[3], (D_MODEL,), 1.0)
    mod_w = nrm(ks[4], (DEPTH, D_MODEL, 6 * D_MODEL), 0.5 * D_MODEL ** -0.5)
    mod_b = nrm(ks[5], (DEPTH, 6 * D_MODEL), 0.02)
    ln_g = 1.0 + nrm(ks[6], (DEPTH, 2, D_MODEL), 0.02)
    ln_b = nrm(ks[7], (DEPTH, 2, D_MODEL), 0.02)
    e_w_in = nrm(ks[8], (N_EVEN, D_MODEL, EVEN_IN), D_MODEL ** -0.5)
    e_conv = nrm(ks[9], (N_EVEN, DN_CONV, DN_QKV), DN_CONV ** -0.5)
    e_a_log = jnp.log(jax.random.uniform(ks[10], (N_EVEN, 2, DN_HEADS), f32, 1.0, 16.0))
    dt = jnp.exp(jax.random.uniform(ks[11], (N_EVEN, 2, DN_HEADS), f32, math.log(1e-3), math.log(1e-1)))
    e_dt_bias = dt + jnp.log(-jnp.expm1(-dt))
    e_norm_w = 1.0 + nrm(ks[12], (N_EVEN, DN_DV), 0.02)
    base = jnp.log(-jnp.log1p(-jnp.power(2.0, -5.0 - jnp.arange(RET_HEADS, dtype=f32))))
    e_ret_decay = base + nrm(ks[13], (N_EVEN, 2, RET_HEADS), 0.05)
    e_w_out = nrm(ks[14], (N_EVEN, EVEN_MIX, D_MODEL), DEEP_BETA * EVEN_MIX ** -0.5)
    o_w_qkv = nrm(ks[15], (N_ODD, D_MODEL, ODD_IN), D_MODEL ** -0.5)
    o_lambda = nrm(ks[16], (N_ODD, 4, DIFF_HD), 0.1)
    o_subln_w = 1.0 + nrm(ks[17], (N_ODD, DIFF_DV), 0.02)
    o_w_out = nrm(ks[18], (N_ODD, ODD_MIX, D_MODEL), DEEP_BETA * ODD_MIX ** -0.5)
    f_w_gate = nrm(ks[19], (DEPTH, D_MODEL, D_FF), D_MODEL ** -0.5)
    f_w_up = nrm(ks[20], (DEPTH, D_MODEL, D_FF), D_MODEL ** -0.5)
    f_conv = nrm(ks[21], (DEPTH, FFN_CONV, FFN_CONV, D_FF), 1.0 / FFN_CONV)
    f_w_down = nrm(ks[22], (DEPTH, D_FF, D_MODEL), DEEP_BETA * D_FF ** -0.5)
    return {'x': x, 'c': c, 'ctx': ctx, 'c_ctx': c_ctx, 'mod_w': mod_w, 'mod_b': mod_b,
            'ln_g': ln_g, 'ln_b': ln_b, 'e_w_in': e_w_in, 'e_conv': e_conv, 'e_a_log': e_a_log,
            'e_dt_bias': e_dt_bias, 'e_norm_w': e_norm_w, 'e_ret_decay': e_ret_decay,
            'e_w_out': e_w_out, 'o_w_qkv': o_w_qkv, 'o_lambda': o_lambda, 'o_subln_w': o_subln_w,
            'o_w_out': o_w_out, 'f_w_gate': f_w_gate, 'f_w_up': f_w_up, 'f_conv': f_conv,
            'f_w_down': f_w_down}


def reference(x, c, ctx, c_ctx, mod_w, mod_b, ln_g, ln_b, e_w_in, e_conv, e_a_log, e_dt_bias,
              e_norm_w, e_ret_decay, e_w_out, o_w_qkv, o_lambda, o_subln_w, o_w_out,
              f_w_gate, f_w_up, f_conv, f_w_down):
    l = x.shape[1]
    rows = l // GRID_W
    pos = jnp.arange(l)
    ret_ang = rope_1d(pos, RET_DK)
    diff_ang = rope_2d(pos // GRID_W, pos % GRID_W, DIFF_HD)
    ret_rot = (jnp.cos(ret_ang)[None, :, None, :].astype(x.dtype),
               jnp.sin(ret_ang)[None, :, None, :].astype(x.dtype))
    diff_rot = (jnp.cos(diff_ang)[None, :, None, None, :].astype(x.dtype),
                jnp.sin(diff_ang)[None, :, None, None, :].astype(x.dtype))
    c_act = jax.nn.silu(c)
    cc_act = jax.nn.silu(c_ctx)
    for li in range(DEPTH):
        last = li == DEPTH - 1
        mod = (c_act @ mod_w[li] + mod_b[li])[:, None, :]
        modc = cc_act @ mod_w[li] + mod_b[li]
        sh_a, sc_a, g_a, sh_f, sc_f, g_f = jnp.split(mod, 6, -1)
        csh_a, csc_a, cg_a, csh_f, csc_f, cg_f = jnp.split(modc, 6, -1)
        h = modulate(x, sh_a, sc_a)
        hc = modulate(ctx, csh_a, csc_a)
        i = li // 2
        if li % 2 == 0:
            y, yc = even_mixer(h, hc, e_w_in[i], e_conv[i], e_a_log[i], e_dt_bias[i], e_norm_w[i],
                               e_ret_decay[i], e_w_out[i], ret_rot, not last)
        else:
            lambda_init = 0.8 - 0.6 * math.exp(-0.3 * li)
            y, yc = odd_mixer(h, hc, o_w_qkv[i], o_lambda[i], o_subln_w[i], o_w_out[i], diff_rot,
                              lambda_init, not last)
        x = layer_norm(DEEP_ALPHA * x + g_a * y, ln_g[li, 0], ln_b[li, 0])
        hf = modulate(x, sh_f, sc_f)
        x = layer_norm(DEEP_ALPHA * x + g_f * conv_ffn(hf, rows, f_w_gate[li], f_w_up[li], f_conv[li], f_w_down[li]),
                       ln_g[li, 1], ln_b[li, 1])
        if not last:
            ctx = layer_norm(DEEP_ALPHA * ctx + cg_a * yc, ln_g[li, 0], ln_b[li, 0])
            hcf = modulate(ctx, csh_f, csc_f)
            ctx = layer_norm(DEEP_ALPHA * ctx + cg_f * conv_ffn(hcf, 1, f_w_gate[li], f_w_up[li], f_conv[li], f_w_down[li]),
                             ln_g[li, 1], ln_b[li, 1])
    return x
```

```python
import math
from contextlib import ExitStack

import numpy as np
import concourse.bass as bass
import concourse.mybir as mybir
from concourse.bass_utils import run_bass_kernel_spmd

F32 = mybir.dt.float32
BF16 = mybir.dt.bfloat16
AF = mybir.ActivationFunctionType
ALU = mybir.AluOpType
AX = mybir.AxisListType

SAME_ENGINE_SYNC = True
N_DMA_SEMS = 24
N_HW_SEMS = 16

NB = 2
L = 2048
CT = 256
T = L + CT
D = 1024
DFF = 2816
NT = T // 128
ALPHA = 4.0 ** 0.25
LN_EPS = 1e-5
NORM_EPS = 1e-6
LAMBDA_INIT = 0.8 - 0.6 * math.exp(-0.3)
EVEN_IN = 4112
NEG = -30000.0
SB_BASE = 16640


class Op:
    __slots__ = ("eng", "fn", "deps", "signal", "count", "dma", "sem", "target", "idx")


class Prog:
    ENGS = ("pe", "act", "dve", "pool", "sp")

    def __init__(self, nc):
        self.nc = nc
        self.ops = []
        self.last_w = {}
        self.readers = {}
        self.dma_rr = 0
        self.dma_rr_sw = 0
        self.dma_sem_total = [0] * N_DMA_SEMS
        self.dma_sem_lastop = [None] * N_DMA_SEMS
        self.last_eng = {}
        self.barrier_deps = []

    def barrier(self):
        deps = list(self.last_eng.values())
        deps += [o for o in self.dma_sem_lastop if o is not None]
        self.barrier_deps = deps
        self.last_w = {}
        self.readers = {}

    def _add(self, eng, fn, reads, writes, dma):
        op = Op()
        op.eng = eng
        op.fn = fn
        op.signal = False
        op.count = 0
        op.dma = dma
        op.sem = None
        op.target = 0
        op.idx = len(self.ops)
        deps = {}
        ps_reads = [r for r in reads if isinstance(r, tuple) and r and r[0] == "ps"]
        if ps_reads:
            writes = list(writes) + [r for r in ps_reads if r not in writes]
        for d in self.barrier_deps:
            deps[d.idx] = d
        for r in reads:
            w = self.last_w.get(r)
            if w is not None:
                deps[w.idx] = w
        for r in writes:
            w = self.last_w.get(r)
            if w is not None:
                deps[w.idx] = w
            for rd in self.readers.get(r, ()):
                deps[rd.idx] = rd
        if dma:
            if eng == "pool":
                i = N_HW_SEMS + self.dma_rr_sw
                self.dma_rr_sw = (self.dma_rr_sw + 1) % (N_DMA_SEMS - N_HW_SEMS)
            else:
                i = self.dma_rr
                self.dma_rr = (self.dma_rr + 1) % N_HW_SEMS
            prev = self.dma_sem_lastop[i]
            if prev is not None:
                deps[prev.idx] = prev
            self.dma_sem_total[i] += 16
            op.sem = i
            op.target = self.dma_sem_total[i]
            self.dma_sem_lastop[i] = op
        else:
            self.last_eng[eng] = op
        op.deps = list(deps.values())
        for r in reads:
            self.readers.setdefault(r, []).append(op)
        for r in writes:
            self.last_w[r] = op
            self.readers[r] = []
        self.ops.append(op)
        return op

    def op(self, eng, fn, reads=(), writes=()):
        return self._add(eng, fn, reads, writes, False)

    def dma(self, eng, fn, reads=(), writes=()):
        return self._add(eng, fn, reads, writes, True)

    @staticmethod
    def _skip(d, op):
        return d.eng == op.eng and (d.eng == "pe" or not SAME_ENGINE_SYNC) and not op.dma

    def emit(self):
        nc = self.nc
        for op in self.ops:
            for d in op.deps:
                if d.dma or self._skip(d, op):
                    continue
                d.signal = True
        cnt = {e: 0 for e in self.ENGS}
        for op in self.ops:
            if op.dma:
                continue
            if op.signal:
                cnt[op.eng] += 1
                op.count = cnt[op.eng]
        streams = {e: [o for o in self.ops if o.eng == e] for e in self.ENGS}
        with ExitStack() as es:
            esem = {e: es.enter_context(nc.semaphore("s_" + e)) for e in self.ENGS}
            dsem = [es.enter_context(nc.semaphore("d%d" % i)) for i in range(N_DMA_SEMS)]
            block = es.enter_context(nc.Block())
            all_dma_final = [(dsem[i], self.dma_sem_total[i]) for i in range(N_DMA_SEMS)
                             if self.dma_sem_total[i] > 0]

            def run_stream(e, eng):
                waited = {}
                for op in streams[e]:
                    for d in op.deps:
                        if d.dma:
                            key, val, sem = ("d", d.sem), d.target, dsem[d.sem]
                        else:
                            if self._skip(d, op):
                                continue
                            key, val, sem = ("e", d.eng), d.count, esem[d.eng]
                        if waited.get(key, 0) >= val:
                            continue
                        waited[key] = val
                        eng.wait_ge(sem, val)
                    ins = op.fn(eng)
                    if op.dma:
                        ins.then_inc(dsem[op.sem], 16)
                    elif op.signal:
                        ins.then_inc(esem[e], 1)
                if e == "sp":
                    for sem, val in all_dma_final:
                        eng.wait_ge(sem, val)

            @block.tensor
            def _(eng):
                run_stream("pe", eng)

            @block.scalar
            def _(eng):
                run_stream("act", eng)

            @block.vector
            def _(eng):
                run_stream("dve", eng)

            @block.gpsimd
            def _(eng):
                run_stream("pool", eng)

            @block.sync
            def _(eng):
                run_stream("sp", eng)


class Ctx:
    def __init__(self, nc):
        self.nc = nc
        self.P = Prog(nc)
        self.sb_off = SB_BASE
        self.sb_n = 0
        self.banks = [nc.alloc_psum_tensor("bank%d" % i, [128, 512], F32) for i in range(8)]
        self.bank_rr = 0
        self.dram = {}

    def sb_reset(self):
        self.P.barrier()
        self.sb_off = SB_BASE

    def sb(self, name, shape, dt):
        nbytes = int(np.prod(shape[1:])) * (2 if dt == BF16 else 4)
        nbytes = (nbytes + 63) // 64 * 64
        self.sb_n += 1
        t = self.nc.alloc_sbuf_tensor_at("%s_%d" % (name, self.sb_n), list(shape), dt, offset=self.sb_off)
        self.sb_off += nbytes
        assert self.sb_off <= 229376, (name, self.sb_off)
        return t

    def sb_at(self, name, shape, dt, off):
        self.sb_n += 1
        return self.nc.alloc_sbuf_tensor_at("%s_%d" % (name, self.sb_n), list(shape), dt, offset=off)

    def bank(self):
        i = self.bank_rr
        self.bank_rr = (self.bank_rr + 1) % 8
        return i

    def mm(self, out, lhsT, rhs, start, stop, reads, writes):
        self.P.op("pe", lambda e: e.matmul(out, lhsT=lhsT, rhs=rhs, start=start, stop=stop), reads, writes)

    def tr(self, out, in_, ident, reads, writes):
        self.P.op("pe", lambda e: e.transpose(out, in_, ident), reads, writes)

    def act(self, out, in_, func, reads, writes, bias=None, scale=None):
        kw = {}
        if bias is not None:
            kw["bias"] = bias
        if scale is not None:
            kw["scale"] = scale
        self.P.op("act", lambda e: e.activation(out=out, in_=in_, func=func, **kw), reads, writes)

    def tt(self, eng, out, in0, in1, op, reads, writes):
        self.P.op(eng, lambda e: e.tensor_tensor(out=out, in0=in0, in1=in1, op=op), reads, writes)

    def ts(self, eng, out, in0, s1, s2, op0, op1, reads, writes):
        if op1 is None:
            self.P.op(eng, lambda e: e.tensor_scalar(out=out, in0=in0, scalar1=s1, scalar2=None, op0=op0), reads, writes)
        else:
            self.P.op(eng, lambda e: e.tensor_scalar(out=out, in0=in0, scalar1=s1, scalar2=s2, op0=op0, op1=op1), reads, writes)

    def stt(self, eng, out, in0, scalar, in1, op0, op1, reads, writes):
        self.P.op(eng, lambda e: e.scalar_tensor_tensor(out=out, in0=in0, scalar=scalar, in1=in1, op0=op0, op1=op1), reads, writes)

    def cp(self, eng, out, in_, reads, writes):
        if eng == "act":
            self.P.op("act", lambda e: e.copy(out=out, in_=in_), reads, writes)
        else:
            self.P.op(eng, lambda e: e.tensor_copy(out=out, in_=in_), reads, writes)

    def ld(self, out, in_, reads, writes, q="sp", slow=False):
        if slow:
            self.P.dma(q, lambda e: e.dma_start(out=out, in_=in_, allow_slow_non_contiguous=True), reads, writes)
        else:
            self.P.dma(q, lambda e: e.dma_start(out=out, in_=in_), reads, writes)


def pp_view(vec_ap):
    return vec_ap.rearrange("(k p) -> p k", p=128)


def stage_mod(C):
    C.sb_reset()
    I = C.I
    cT = C.sb("cT", [128, 8, 3], F32)
    for r in range(3):
        for k0 in range(0, 8, 4):
            C.ld(cT[:, k0:k0 + 4, r], pp_view(I["cvec"][r])[:, k0:k0 + 4], [], ["cT"], slow=True)
    C.act(cT[:], cT[:], AF.Silu, ["cT"], ["cT"])
    wt = [C.sb("modw%d" % i, [128, 8, 512], F32) for i in range(2)]
    mb = C.sb("modb", [3, 6144], F32)
    res = C.sb("modres", [3, 6144], F32)
    n = 0
    for li in range(2):
        C.ld(mb[:], I["mod_b"][li].partition_broadcast(3), [], ["mb"])
        wv = I["mod_w"][li].rearrange("(k p) n -> p k n", p=128)
        for j in range(12):
            w = wt[n % 2]
            wk = ("modw", n % 2)
            n += 1
            C.ld(w[:], wv[:, :, j * 512:(j + 1) * 512], [], [wk])
            bk = C.bank()
            for k in range(8):
                C.mm(C.banks[bk][0:3, :], cT[:, k, :], w[:, k, :], k == 0, k == 7, ["cT", wk], [("ps", bk)])
            C.tt("dve", res[:, j * 512:(j + 1) * 512], C.banks[bk][0:3, :], mb[:, j * 512:(j + 1) * 512], ALU.add,
                 [("ps", bk), "mb"], ["modres"])
        C.ld(C.dram["modv"][li], res[:], ["modres"], [("modv", li)])


def load_mod_pp(C, li, r, which, name):
    base = 0 if which == "a" else 3 * D
    sh = C.sb(name + "sh", [128, 8], F32)
    sc = C.sb(name + "sc", [128, 8], F32)
    mv = C.dram["modv"][li, r]
    for k0 in range(0, 8, 4):
        C.ld(sh[:, k0:k0 + 4], pp_view(mv[base:base + D])[:, k0:k0 + 4], [("modv", li)], [name + "sh"], slow=True)
        C.ld(sc[:, k0:k0 + 4], pp_view(mv[base + D:base + 2 * D])[:, k0:k0 + 4], [("modv", li)], [name + "sc"], slow=True)
    C.ts("dve", sc[:], sc[:], 1.0, None, ALU.add, None, [name + "sc"], [name + "sc"])
    return sh, sc


def emit_modT(C, xt, xres, sh, sc, mres, hdst, hres, tag, slot):
    ht = C.hT_tiles[slot]
    hk = ("hTt", slot)
    for half in range(2):
        bk = C.bank()
        for q in range(4):
            k = half * 4 + q
            C.tr(C.banks[bk][:, q * 128:(q + 1) * 128], xt[:, k * 128:(k + 1) * 128], C.ident[:], [xres, "ident"], [("ps", bk)])
        for q in range(4):
            k = half * 4 + q
            C.act(ht[:, k, :], C.banks[bk][:, q * 128:(q + 1) * 128], AF.Identity, [("ps", bk)] + mres, [hk],
                  bias=sh[:, k:k + 1], scale=sc[:, k:k + 1])
    C.ld(hdst, ht[:], [hk], [hres])


def x_in_ap(C, src, b, t):
    if src == "input":
        if t < 16:
            return C.I["x"][b, t * 128:(t + 1) * 128, :]
        return C.I["ctx"][b, (t - 16) * 128:(t - 15) * 128, :]
    return C.dram[src][b, t * 128:(t + 1) * 128, :]


def hT_dst(C, b, t):
    return C.dram["HT"][b].rearrange("(k p) t -> p k t", p=128)[:, :, t * 128:(t + 1) * 128]


def stage_modT0(C):
    C.sb_reset()
    C.ident = C.sb("ident", [128, 128], F32)
    C.ld(C.ident[:], C.I["ident"], [], ["ident"])
    C.hT_tiles = [C.sb("hTt%d" % i, [128, 8, 128], BF16) for i in range(2)]
    xts = [C.sb("xt%d" % i, [128, D], F32) for i in range(3)]
    n = 0
    for b in range(NB):
        sh, sc = load_mod_pp(C, 0, b, "a", "m%d" % b)
        shc, scc = load_mod_pp(C, 0, 2, "a", "mc%d" % b)
        for t in range(NT):
            xt = xts[n % 3]
            xk = ("xt", n % 3)
            C.ld(xt[:], x_in_ap(C, "input", b, t), [], [xk])
            if t < 16:
                emit_modT(C, xt, xk, sh, sc, ["m%dsh" % b, "m%dsc" % b], hT_dst(C, b, t), ("HT", b, t), "m0", n % 2)
            else:
                emit_modT(C, xt, xk, shc, scc, ["mc%dsh" % b, "mc%dsc" % b], hT_dst(C, b, t), ("HT", b, t), "m0", n % 2)
            n += 1


def stage_projln(C, li, which, ysrc, KC, W, xsrc, xdst, ntiles, nxt):
    C.sb_reset()
    C.ident = C.sb("ident", [128, 128], F32)
    C.ld(C.ident[:], C.I["ident"], [], ["ident"])
    C.hT_tiles = [C.sb("hTt%d" % i, [128, 8, 128], BF16) for i in range(2)]
    wsb = C.sb("wsb", [128, KC, D], BF16)
    wv = W.rearrange("(k p) n -> p k n", p=128)
    for k0 in range(0, KC, 2):
        k1 = min(KC, k0 + 2)
        C.ld(wsb[:, k0:k1, :], wv[:, k0:k1, :], [], [("wsb", k0)], q="pool")
    wres = [("wsb", k0) for k0 in range(0, KC, 2)]
    lnrow = 0 if which == "a" else 1
    lng = C.sb("lng", [128, D], F32)
    lnb = C.sb("lnb", [128, D], F32)
    C.ld(lng[:], C.I["ln_g"][li, lnrow].partition_broadcast(128), [], ["lng"])
    C.ld(lnb[:], C.I["ln_b"][li, lnrow].partition_broadcast(128), [], ["lnb"])
    goff = 2 * D if which == "a" else 5 * D
    gates = {}
    for r in ([0, 1, 2] if ntiles > 16 else [0, 1]):
        g = C.sb("gate%d" % r, [128, D], F32)
        C.ld(g[:], C.dram["modv"][li, r, goff:goff + D].partition_broadcast(128), [("modv", li)], [("gate", r)])
        gates[r] = g
    mods = {}
    if nxt is not None:
        for r in ([0, 1, 2] if ntiles > 16 else [0, 1]):
            mods[r] = load_mod_pp(C, nxt[0], r, nxt[1], "nm%d" % r)
    yts = [C.sb("yt%d" % i, [128, KC, 512], BF16) for i in range(2)]
    xts = [C.sb("xt%d" % i, [128, D], F32) for i in range(2)]
    zs = [C.sb("z%d" % i, [128, D], F32) for i in range(2)]
    xos = [C.sb("xo%d" % i, [128, D], F32) for i in range(2)]
    st = [C.sb("st%d" % i, [128, 2, 6], F32) for i in range(2)]
    mv = [C.sb("mv%d" % i, [128, 4], F32) for i in range(2)]
    n = 0
    ng = 0
    for b in range(NB):
        yv = ysrc[b].rearrange("(k p) t -> p k t", p=128)
        for g0 in range(0, ntiles, 4):
            g1 = min(ntiles, g0 + 4)
            yt = yts[ng % 2]
            yk = ("yt", ng % 2)
            ng += 1
            ntok = (g1 - g0) * 128
            for k0 in range(0, KC, 8):
                k1 = min(KC, k0 + 8)
                C.ld(yt[:, k0:k1, 0:ntok], yv[:, k0:k1, g0 * 128:g1 * 128], [("Y", b)], [yk])
            for t in range(g0, g1):
                s = n % 2
                n += 1
                r = b if t < 16 else 2
                xt, z, xo = xts[s], zs[s], xos[s]
                xk, zk, xok, stk = ("xt", s), ("z", s), ("xo", s), ("st", s)
                C.ld(xt[:], x_in_ap(C, xsrc, b, t), [("X", xsrc, b)], [xk])
                bks = [C.bank(), C.bank()]
                for nn in range(2):
                    for k in range(KC):
                        C.mm(C.banks[bks[nn]][:, :], yt[:, k, (t - g0) * 128:(t - g0 + 1) * 128],
                             wsb[:, k, nn * 512:(nn + 1) * 512], k == 0, k == KC - 1, [yk] + wres, [("ps", bks[nn])])
                for nn in range(2):
                    C.tt("dve", z[:, nn * 512:(nn + 1) * 512], C.banks[bks[nn]][:, :], gates[r][:, nn * 512:(nn + 1) * 512],
                         ALU.mult, [("ps", bks[nn]), ("gate", r)], [zk])
                C.stt("dve", z[:], xt[:], ALPHA, z[:], ALU.mult, ALU.add, [xk, zk], [zk])
                for nn in range(2):
                    C.P.op("dve", (lambda o, i: (lambda e: e.bn_stats(out=o, in_=i)))(st[s][:, nn, :], z[:, nn * 512:(nn + 1) * 512]),
                           [zk], [stk])
                C.P.op("dve", (lambda o, i: (lambda e: e.bn_aggr(out=o, in_=i)))(mv[s][:, 0:2], st[s][:].rearrange("p a b -> p (a b)")),
                       [stk], [stk])
                C.ts("dve", mv[s][:, 2:3], mv[s][:, 1:2], LN_EPS, None, ALU.add, None, [stk], [stk])
                C.act(mv[s][:, 2:3], mv[s][:, 2:3], AF.Sqrt, [stk], [stk])
                C.P.op("dve", (lambda o: (lambda e: e.reciprocal(out=o, in_=o)))(mv[s][:, 2:3]), [stk], [stk])
                C.stt("dve", mv[s][:, 3:4], mv[s][:, 0:1], -1.0, mv[s][:, 2:3], ALU.mult, ALU.mult, [stk], [stk])
                C.act(xo[:], z[:], AF.Identity, [zk, stk], [xok], bias=mv[s][:, 3:4], scale=mv[s][:, 2:3])
                C.tt("dve", xo[:], xo[:], lng[:], ALU.mult, [xok, "lng"], [xok])
                C.tt("dve", xo[:], xo[:], lnb[:], ALU.add, [xok, "lnb"], [xok])
                if xdst == "out":
                    C.ld(C.I["out"][b, t * 128:(t + 1) * 128, :], xo[:], [xok], [("OUT", b, t)])
                else:
                    C.ld(C.dram[xdst][b, t * 128:(t + 1) * 128, :], xo[:], [xok], [("Xw", xdst, b, t)])
                if nxt is not None:
                    sh, sc = mods[r]
                    emit_modT(C, xo, xok, sh, sc, ["nm%dsh" % r, "nm%dsc" % r], hT_dst(C, b, t), ("HT", b, t), "pl", s)


def stage_ffn1(C, li, with_ctx):
    C.sb_reset()
    ntok = T if with_ctx else L
    hsb = C.sb("hsb", [128, 8, T], BF16)
    cw = C.sb("cw", [128, 22, 9], F32)
    for i in range(3):
        for j in range(3):
            for c0 in range(0, 22, 4):
                c1 = min(22, c0 + 4)
                C.ld(cw[:, c0:c1, 3 * i + j], C.I["f_conv"][li, i, j].rearrange("(c p) -> p c", p=128)[:, c0:c1], [], ["cw"], slow=True)
    identf = C.sb("identf", [128, 128], F32)
    identb = C.sb("identb", [128, 128], BF16)
    C.ld(identf[:], C.I["ident"], [], ["identf"])
    C.cp("dve", identb[:], identf[:], ["identf"], ["identb"])
    wgs = [C.sb("wg%d" % i, [128, 8, 128], BF16) for i in range(2)]
    wus = [C.sb("wu%d" % i, [128, 8, 128], BF16) for i in range(2)]
    dgs = [C.sb("dg%d" % i, [128, 9, 128], BF16) for i in range(2)]
    gpads = [C.sb("gpad%d" % i, [128, 34, 66], BF16) for i in range(2)]
    gctxs = [C.sb("gctx%d" % i, [128, CT + 2], BF16) for i in range(2)]
    us = [C.sb("u%d" % i, [128, T], F32) for i in range(2)]
    svs = [C.sb("sv%d" % i, [128, T], F32) for i in range(2)]
    gbs = [C.sb("gb%d" % i, [128, T], BF16) for i in range(2)]
    for i in range(2):
        C.P.op("pool", (lambda o: (lambda e: e.memset(o, 0.0)))(gpads[i][:].rearrange("p a b -> p (a b)")), [], [("gpad", i)])
        C.P.op("pool", (lambda o: (lambda e: e.memset(o, 0.0)))(gctxs[i][:]), [], [("gctx", i)])
    wgv = C.I["f_w_gate"][li].rearrange("(k p) n -> p k n", p=128)
    wuv = C.I["f_w_up"][li].rearrange("(k p) n -> p k n", p=128)
    n = 0
    for b in range(NB):
        hv = C.dram["HT"][b].rearrange("(k p) t -> p k t", p=128)
        for k in range(8):
            C.ld(hsb[:, k, 0:ntok], hv[:, k, 0:ntok], [("HT", b)], ["hsb"])
        for c in range(22):
            s = n % 2
            n += 1
            wg, wu, dg, gpad, gctx, u, sv, gb = wgs[s], wus[s], dgs[s], gpads[s], gctxs[s], us[s], svs[s], gbs[s]
            C.ld(wg[:], wgv[:, :, c * 128:(c + 1) * 128], [], [("wg", s)], q="pool")
            C.ld(wu[:], wuv[:, :, c * 128:(c + 1) * 128], [], [("wu", s)], q="pool")
            for tap in range(9):
                C.ts("dve", dg[:, tap, :], identb[:], cw[:, c, tap:tap + 1], None, ALU.mult, None, ["identb", "cw"], [("dg", s)])
            for tt_ in range(4):
                bk = C.bank()
                for k in range(8):
                    C.mm(C.banks[bk][:, :], wg[:, k, :], hsb[:, k, tt_ * 512:(tt_ + 1) * 512], k == 0, k == 7, ["hsb", ("wg", s)], [("ps", bk)])
                C.cp("act", gpad[:, 1 + 8 * tt_:9 + 8 * tt_, 1:65], C.banks[bk][:, :].rearrange("p (r w) -> p r w", w=64),
                     [("ps", bk)], [("gpad", s)])
            if with_ctx:
                bk = C.bank()
                for k in range(8):
                    C.mm(C.banks[bk][:, 0:CT], wg[:, k, :], hsb[:, k, L:T], k == 0, k == 7, ["hsb", ("wg", s)], [("ps", bk)])
                C.cp("act", gctx[:, 1:CT + 1], C.banks[bk][:, 0:CT], [("ps", bk)], [("gctx", s)])
            tiles = [(i * 512, 512) for i in range(4)] + ([(L, CT)] if with_ctx else [])
            for (t0, tn) in tiles:
                bk = C.bank()
                for k in range(8):
                    C.mm(C.banks[bk][:, 0:tn], wu[:, k, :], hsb[:, k, t0:t0 + tn], k == 0, k == 7, ["hsb", ("wu", s)], [("ps", bk)])
                C.cp("act", u[:, t0:t0 + tn], C.banks[bk][:, 0:tn], [("ps", bk)], [("u", s)])
            for tt_ in range(4):
                bk = C.bank()
                for tap in range(9):
                    i, j = tap // 3, tap % 3
                    C.mm(C.banks[bk][:, :], dg[:, tap, :], gpad[:, 8 * tt_ + i:8 * tt_ + i + 8, j:j + 64], tap == 0, tap == 8,
                         [("dg", s), ("gpad", s)], [("ps", bk)])
                C.act(sv[:, tt_ * 512:(tt_ + 1) * 512], C.banks[bk][:, :], AF.Silu, [("ps", bk)], [("sv", s)])
            if with_ctx:
                bk = C.bank()
                for j in range(3):
                    C.mm(C.banks[bk][:, 0:CT], dg[:, 3 + j, :], gctx[:, j:j + CT], j == 0, j == 2, [("dg", s), ("gctx", s)], [("ps", bk)])
                C.act(sv[:, L:T], C.banks[bk][:, 0:CT], AF.Silu, [("ps", bk)], [("sv", s)])
            C.tt("dve", gb[:, 0:ntok], sv[:, 0:ntok], u[:, 0:ntok], ALU.mult, [("sv", s), ("u", s)], [("gb", s)])
            C.ld(C.dram["GT"][b, c * 128:(c + 1) * 128, 0:ntok], gb[:, 0:ntok], [("gb", s)], [("G", b, c)])


def rot_weights(C, wr, w, hd_half, res_in, res_out):
    nblk = 128 // (2 * hd_half)
    for c in range(nblk):
        b0 = c * 2 * hd_half
        C.ts("pool", wr[:, :, b0:b0 + hd_half], w[:, :, b0 + hd_half:b0 + 2 * hd_half], -1.0, None, ALU.mult, None, [res_in], [res_out])
        C.cp("pool", wr[:, :, b0 + hd_half:b0 + 2 * hd_half], w[:, :, b0:b0 + hd_half], [res_in], [res_out])


def stage_odd(C):
    C.sb_reset()
    I = C.I
    hsb = C.sb("hsb", [128, 8, T], BF16)
    vsb = C.sb("vsb", [128, NT, D], BF16)
    cosT = C.sb("cosT", [128, L], F32)
    sinT = C.sb("sinT", [128, L], F32)
    C.ld(cosT[:], I["rope2_cos"], [], ["cosT"])
    C.ld(sinT[:], I["rope2_sin"], [], ["sinT"])
    ones_b = C.sb("ones_b", [128, 128], BF16)
    ones_f = C.sb("ones_f", [128, 128], F32)
    C.P.op("pool", lambda e: e.memset(ones_b[:], 1.0), [], ["ones_b"])
    C.P.op("pool", lambda e: e.memset(ones_f[:], 1.0), [], ["ones_f"])
    lpb = C.sb("lpb", [128, 4, 64], F32)
    C.ld(lpb[:].rearrange("p a b -> p (a b)"), I["o_lambda"][0].rearrange("a b -> (a b)").partition_broadcast(128), [], ["lpb"])
    lt = C.sb("lt", [128, 2, 64], F32)
    lam = C.sb("lam", [128, 4], F32)
    C.tt("dve", lt[:, 0, :], lpb[:, 0, :], lpb[:, 1, :], ALU.mult, ["lpb"], ["lt"])
    C.tt("dve", lt[:, 1, :], lpb[:, 2, :], lpb[:, 3, :], ALU.mult, ["lpb"], ["lt"])
    C.P.op("dve", lambda e: e.reduce_sum(out=lam[:, 0:2], in_=lt[:], axis=AX.X), ["lt"], ["lam"])
    C.act(lam[:, 0:2], lam[:, 0:2], AF.Exp, ["lam"], ["lam"])
    C.tt("dve", lam[:, 2:3], lam[:, 1:2], lam[:, 0:1], ALU.subtract, ["lam"], ["lam"])
    C.ts("dve", lam[:, 2:3], lam[:, 2:3], -LAMBDA_INIT, None, ALU.add, None, ["lam"], ["lam"])
    sw = C.sb("sw", [128, 1], F32)
    C.ld(sw[:], I["o_subln_w"][0].rearrange("(p o) -> p o", o=1), [], ["sw"], slow=True)
    C.ts("dve", sw[:], sw[:], 1.0 - LAMBDA_INIT, None, ALU.mult, None, ["sw"], ["sw"])
    wv_all = C.sb("wv_all", [128, 8, D], BF16)
    wqs = [C.sb("wq%d" % i, [128, 8, 128], BF16) for i in range(2)]
    wqr = [C.sb("wqr%d" % i, [128, 8, 128], BF16) for i in range(2)]
    wks = [C.sb("wk%d" % i, [128, 8, 128], BF16) for i in range(2)]
    wkr = [C.sb("wkr%d" % i, [128, 8, 128], BF16) for i in range(2)]
    qT = [C.sb("qT%d" % i, [128, L], BF16) for i in range(2)]
    kT = [C.sb("kT%d" % i, [128, T], BF16) for i in range(2)]
    t1 = [C.sb("t1_%d" % i, [128, 512], F32) for i in range(2)]
    t2 = [C.sb("t2_%d" % i, [128, 512], F32) for i in range(2)]
    pT = [C.sb("pT%d" % i, [128, 512], BF16) for i in range(6)]
    accs = [C.sb("acc%d" % i, [128, 512], F32) for i in range(2)]
    ep = [C.sb("ep%d" % i, [128, 512], F32) for i in range(5)]
    yb = [C.sb("yb%d" % i, [128, 512], BF16) for i in range(2)]
    wqkv = I["o_w_qkv"][0].rearrange("(k p) n -> p k n", p=128)
    npt = 0
    nrot = 0
    ny = 0
    for b in range(NB):
        hv = C.dram["HT"][b].rearrange("(k p) t -> p k t", p=128)
        for k in range(8):
            C.ld(hsb[:, k, :], hv[:, k, :], [("HT", b)], ["hsb"])
        for k0 in range(0, 8, 2):
            C.ld(wv_all[:, k0:k0 + 2, :], wqkv[:, k0:k0 + 2, 2 * D:3 * D], [], ["wv_all"], q="pool")
        for t in range(NT):
            for nn in range(2):
                bk = C.bank()
                for k in range(8):
                    C.mm(C.banks[bk][:, :], hsb[:, k, t * 128:(t + 1) * 128], wv_all[:, k, nn * 512:(nn + 1) * 512], k == 0, k == 7,
                         ["hsb", "wv_all"], [("ps", bk)])
                C.cp("act" if nn == 0 else "dve", vsb[:, t, nn * 512:(nn + 1) * 512], C.banks[bk][:, :], [("ps", bk)], ["vsb"])
        for hd in range(8):
            s = hd % 2
            C.ld(wqs[s][:], wqkv[:, :, hd * 128:(hd + 1) * 128], [], [("wq", s)], q="pool")
            C.ld(wks[s][:], wqkv[:, :, D + hd * 128:D + (hd + 1) * 128], [], [("wk", s)], q="pool")
            rot_weights(C, wqr[s], wqs[s], 32, ("wq", s), ("wqr", s))
            rot_weights(C, wkr[s], wks[s], 32, ("wk", s), ("wkr", s))
            for (w, wr, dst, dres, wres) in ((wqs[s], wqr[s], qT[s], ("qT", s), [("wq", s), ("wqr", s)]),
                                            (wks[s], wkr[s], kT[s], ("kT", s), [("wk", s), ("wkr", s)])):
                for tt_ in range(4):
                    t0 = tt_ * 512
                    b1, b2 = C.bank(), C.bank()
                    for k in range(8):
                        C.mm(C.banks[b1][:, :], w[:, k, :], hsb[:, k, t0:t0 + 512], k == 0, k == 7, ["hsb"] + wres, [("ps", b1)])
                    for k in range(8):
                        C.mm(C.banks[b2][:, :], wr[:, k, :], hsb[:, k, t0:t0 + 512], k == 0, k == 7, ["hsb"] + wres, [("ps", b2)])
                    r = nrot % 2
                    nrot += 1
                    C.tt("dve", t1[r][:], C.banks[b1][:, :], cosT[:, t0:t0 + 512], ALU.mult, [("ps", b1), "cosT"], [("t1", r)])
                    C.tt("dve", t2[r][:], C.banks[b2][:, :], sinT[:, t0:t0 + 512], ALU.mult, [("ps", b2), "sinT"], [("t2", r)])
                    C.tt("dve", dst[:, t0:t0 + 512], t1[r][:], t2[r][:], ALU.add, [("t1", r), ("t2", r)], [dres])
            bk = C.bank()
            for k in range(8):
                C.mm(C.banks[bk][:, 0:CT], wks[s][:, k, :], hsb[:, k, L:T], k == 0, k == 7, ["hsb", ("wk", s)], [("ps", bk)])
            C.cp("act", kT[s][:, L:T], C.banks[bk][:, 0:CT], [("ps", bk)], [("kT", s)])
            for qb in range(4):
                po = [0, 1]
                pss = [6, 7]

                def emit_S(kt, cs=(0, 1)):
                    for c in cs:
                        bs = 2 + (kt % 2) * 2 + c
                        C.mm(C.banks[bs][:, :], kT[s][c * 64:(c + 1) * 64, kt * 128:(kt + 1) * 128],
                             qT[s][c * 64:(c + 1) * 64, qb * 512:(qb + 1) * 512], True, True, [("kT", s), ("qT", s)], [("ps", bs)])

                emit_S(0, (0,))
                C.mm(C.banks[7][0:8, 0:8], ones_f[:, 0:8], ones_f[:, 0:8], True, True, ["ones_f"], [("ps", 7)])
                emit_S(0, (1,))
                for kt in range(NT):
                    pis = []
                    for c in range(2):
                        bs = 2 + (kt % 2) * 2 + c
                        pi = npt % 6
                        npt += 1
                        pis.append(pi)
                        C.act(pT[pi][:], C.banks[bs][:, :], AF.Exp, [("ps", bs)], [("pT", pi)], scale=0.125)
                    for c in range(2):
                        pi = pis[c]
                        if kt + 1 < NT:
                            emit_S(kt + 1, (c,))
                        C.mm(C.banks[po[c]][:, :], vsb[:, kt, hd * 128:(hd + 1) * 128], pT[pi][:], kt == 0, kt == NT - 1,
                             ["vsb", ("pT", pi)], [("ps", po[c])])
                    for c in range(2):
                        pi = pis[c]
                        if kt == 0:
                            C.cp("dve", accs[c][:], pT[pi][:], [("pT", pi)], [("acc", c)])
                        else:
                            C.tt("dve", accs[c][:], accs[c][:], pT[pi][:], ALU.add, [("pT", pi), ("acc", c)], [("acc", c)])
                for c in range(2):
                    C.mm(C.banks[pss[c]][:, :], ones_f[:], accs[c][:], True, True, ["ones_f", ("acc", c)], [("ps", pss[c])])
                e0, e1, e2, e3, e4 = ep
                C.P.op("dve", (lambda o, i: (lambda e: e.reciprocal(out=o, in_=i)))(e0[:], C.banks[pss[0]][:, :]), [("ps", pss[0])], ["e0"])
                C.tt("dve", e1[:], C.banks[po[0]][:, :], e0[:], ALU.mult, [("ps", po[0]), "e0"], ["e1"])
                C.P.op("dve", (lambda o, i: (lambda e: e.reciprocal(out=o, in_=i)))(e2[:], C.banks[pss[1]][:, :]), [("ps", pss[1])], ["e2"])
                C.tt("dve", e3[:], C.banks[po[1]][:, :], e2[:], ALU.mult, [("ps", po[1]), "e2"], ["e3"])
                C.stt("dve", e1[:], e3[:], lam[:, 2:3], e1[:], ALU.mult, ALU.add, ["e3", "e1", "lam"], ["e1"])
                C.act(e4[:], e1[:], AF.Square, ["e1"], ["e4"])
                bq = 2
                C.mm(C.banks[bq][:, :], ones_f[:], e4[:], True, True, ["ones_f", "e4"], [("ps", bq)])
                C.ts("dve", e0[:], C.banks[bq][:, :], 1.0 / 128.0, NORM_EPS, ALU.mult, ALU.add, [("ps", bq)], ["e0"])
                C.act(e0[:], e0[:], AF.Ln, ["e0"], ["e0"])
                C.act(e0[:], e0[:], AF.Exp, ["e0"], ["e0"], scale=-0.5)
                yi = ny % 2
                ny += 1
                C.stt("dve", yb[yi][:], e1[:], sw[:, 0:1], e0[:], ALU.mult, ALU.mult, ["e1", "e0", "sw"], [("yb", yi)])
                C.ld(C.dram["YT"][b, hd * 128:(hd + 1) * 128, qb * 512:(qb + 1) * 512], yb[yi][:], [("yb", yi)], [("Yw", b, hd, qb)])


NCH = T // 64
DK_SCALE = 128.0 ** -0.5


import os


def stage_even(C):
    EV_NB = int(os.environ.get('EV_NB', NB))
    EV_STOP = int(os.environ.get('EV_STOP', 99))
    EV_SUB = int(os.environ.get('EV_SUB', 99))
    EV_HEADS = [int(x) for x in os.environ.get('EV_HEADS', '0,1,2,3,4,5,6,7').split(',')]
    C.sb_reset()
    I = C.I
    P = C.P
    mk = C.sb("mk", [64, 8, 64], F32)
    for i in range(7):
        C.ld(mk[:, i, :], I["masks"][i], [], ["mk"])
    U = [mk[:, 0, :], mk[:, 1, :]]
    POS = [mk[:, 2, :], mk[:, 3, :]]
    STRICT = [mk[:, 4, :], mk[:, 5, :]]
    I64 = mk[:, 6, :]
    ident = C.sb("ident", [128, 128], F32)
    C.ld(ident[:], I["ident"], [], ["ident"])
    ones64 = C.sb("ones64", [64, 128], F32)
    ones128 = C.sb("ones128", [128, 128], F32)
    P.op("pool", lambda e: e.memset(ones64[:], 1.0), [], ["ones64"])
    P.op("pool", lambda e: e.memset(ones128[:], 1.0), [], ["ones128"])
    cosT = C.sb("cosT", [128, L], F32)
    sinT = C.sb("sinT", [128, L], F32)
    C.ld(cosT[:], I["rope1_cos"], [], ["cosT"])
    C.ld(sinT[:], I["rope1_sin"], [], ["sinT"])
    cwe = C.sb("cwe", [128, 12, 3], F32)
    for tap in range(3):
        for g0 in range(0, 12, 4):
            C.ld(cwe[:, g0:g0 + 4, tap], I["e_conv"][0, tap].rearrange("(g p) -> p g", p=128)[:, g0:g0 + 4], [], ["cwe"], slow=True)
    nw = C.sb("nw", [128, 1], F32)
    C.ld(nw[:], I["e_norm_w"][0].rearrange("(p o) -> p o", o=1), [], ["nw"], slow=True)
    sc8 = C.sb("sc8", [8, 4], F32)
    C.ld(sc8[:, 0:1], I["e_a_log"][0].rearrange("d (h o) -> (d h) o", o=1), [], ["sc8"], slow=True)
    C.ld(sc8[:, 1:2], I["e_dt_bias"][0].rearrange("d (h o) -> (d h) o", o=1), [], ["sc8"], slow=True)
    C.ld(sc8[:, 2:3], I["e_ret_decay"][0].rearrange("d (h o) -> (d h) o", o=1), [], ["sc8"], slow=True)
    C.act(sc8[:, 0:1], sc8[:, 0:1], AF.Exp, ["sc8"], ["sc8"])
    C.ts("dve", sc8[:, 0:1], sc8[:, 0:1], -1.0, None, ALU.mult, None, ["sc8"], ["sc8"])
    C.act(sc8[:, 2:3], sc8[:, 2:3], AF.Exp, ["sc8"], ["sc8"])
    C.ts("dve", sc8[:, 2:3], sc8[:, 2:3], -1.0, None, ALU.mult, None, ["sc8"], ["sc8"])

    hsb_off = C.sb_off
    hsb = C.sb("hsb", [128, 8, T], BF16)
    wab = C.sb("wab", [128, 8, 16], BF16)
    wts = [C.sb("wt%d" % i, [128, 8, 128], BF16) for i in range(4)]
    wrs = [C.sb("wr%d" % i, [128, 8, 128], BF16) for i in range(2)]
    raw = C.sb("raw", [128, T], F32)
    qT = C.sb("qT", [128, T], F32)
    kT = C.sb("kT", [128, T], F32)
    vT = C.sb("vT", [128, T], F32)
    szT = C.sb("szT", [128, T], BF16)
    tmp = C.sb("tmp", [128, 512], F32)
    tmp2 = C.sb("tmp2", [128, 512], F32)
    ktok = C.sb("ktok", [64, NCH, 128], F32)
    vtok = C.sb("vtok", [64, NCH, 128], F32)
    oacc_off = C.sb_off
    oacc = C.sb("oacc", [64, NCH, 128], F32)
    ab = C.sb_at("ab", [8, T], F32, oacc_off)
    ab2 = C.sb_at("ab2", [8, T], F32, oacc_off + T * 4)
    TTs = C.sb_at("TTs", [64, 2 * NCH, 64], F32, hsb_off)
    QKTs = C.sb_at("QKTs", [64, 2 * NCH, 64], F32, hsb_off + 2 * NCH * 64 * 4)
    yT = C.sb("yT", [128, T], BF16)
    la = [C.sb("la%d" % d, [64, NCH], F32) for d in range(2)]
    bet = [C.sb("bet%d" % d, [64, NCH], F32) for d in range(2)]
    G = [C.sb("G%d" % d, [64, NCH], F32) for d in range(2)]
    EG = [C.sb("EG%d" % d, [64, NCH], F32) for d in range(2)]
    EKG = [C.sb("EKG%d" % d, [64, NCH], F32) for d in range(2)]
    NBEG = [C.sb("NBEG%d" % d, [64, NCH], F32) for d in range(2)]
    CD = [C.sb("CD%d" % d, [128, NCH], F32) for d in range(2)]
    S = [C.sb("S%d" % d, [128, 128], F32) for d in range(2)]
    st = C.sb("st", [64, 4 * NCH], F32)
    lat = C.sb("lat", [NCH, 128], F32)
    NSM = 72
    sm = [C.sb("sm%d" % i, [64, 64], F32) for i in range(NSM)]
    NPM = 24
    Pm = [C.sb("Pm%d" % i, [64, 64], F32) for i in range(NPM)]
    vb = [C.sb("vb%d" % i, [64, 128], F32) for i in range(4)]
    smi = [0]

    tb = [0, 0]
    psA = {"banks": [2, 3, 4, 5], "bi": -1, "j": 8, "round": -1}
    psB = {"banks": [6, 7], "bi": -1, "j": 8, "round": -1}
    rnd = [0]

    def _pslot(st):
        if st["round"] != rnd[0] or st["j"] >= 8:
            st["round"] = rnd[0]
            st["bi"] = (st["bi"] + 1) % len(st["banks"])
            st["j"] = 0
        bk, j = st["banks"][st["bi"]], st["j"]
        st["j"] += 1
        return C.banks[bk][0:64, j * 64:(j + 1) * 64], ("ps", bk)

    def pslot():
        return _pslot(psA)

    def pslotB():
        return _pslot(psB)

    def pmat():
        i = tb[1] % NPM
        tb[1] += 1
        return Pm[i][:], ("Pm", i)

    def small():
        i = smi[0] % NSM
        smi[0] += 1
        return sm[i][:], ("sm", i)

    win = I["e_w_in"][0].rearrange("(k p) n -> p k n", p=128)
    LAD = C.dram["LAD"]

    def proj(w, t0, tn, wres):
        bk = C.bank()
        for k in range(8):
            C.mm(C.banks[bk][:, 0:tn], w[:, k, :], hsb[:, k, t0:t0 + tn], k == 0, k == 7, ["hsb"] + wres, [("ps", bk)])
        return bk

    tiles = [(i * 512, 512) for i in range(4)] + [(L, CT)]

    for b in range(EV_NB):
        hv = C.dram["HT"][b].rearrange("(k p) t -> p k t", p=128)
        P.barrier()
        for k in range(8):
            C.ld(hsb[:, k, :], hv[:, k, :], [("HT", b)], ["hsb"])
        C.ld(wab[:], win[:, :, 2048:2064], [], ["wab"], q="pool")
        for (t0, tn) in tiles:
            bk = C.bank()
            for k in range(8):
                C.mm(C.banks[bk][0:8, 0:tn], wab[:, k, 0:8], hsb[:, k, t0:t0 + tn], k == 0, k == 7, ["hsb", "wab"], [("ps", bk)])
            C.act(ab[:, t0:t0 + tn], C.banks[bk][0:8, 0:tn], AF.Exp, [("ps", bk), "sc8"], ["ab"], bias=sc8[:, 1:2])
            bk = C.bank()
            for k in range(8):
                C.mm(C.banks[bk][0:8, 0:tn], wab[:, k, 8:16], hsb[:, k, t0:t0 + tn], k == 0, k == 7, ["hsb", "wab"], [("ps", bk)])
            C.act(ab2[:, t0:t0 + tn], C.banks[bk][0:8, 0:tn], AF.Sigmoid, [("ps", bk)], ["ab2"])
        C.ts("dve", ab[:], ab[:], 1.0, None, ALU.add, None, ["ab"], ["ab"])
        C.act(ab[:], ab[:], AF.Ln, ["ab"], ["ab"])
        C.ts("dve", ab[:], ab[:], sc8[:, 0:1], None, ALU.mult, None, ["ab", "sc8"], ["ab"])
        C.ld(LAD[b, 0:8, :], ab[:], ["ab"], [("LAD", b, 0)])
        C.ld(LAD[b, 8:16, :], ab2[:], ["ab2"], [("LAD", b, 1)])
        C.ts("dve", ab2[:], ab2[:], 0.0, None, ALU.mult, None, ["ab2"], ["ab2"])
        C.ts("dve", ab2[:], ab2[:], sc8[:, 2:3], None, ALU.add, None, ["ab2", "sc8"], ["ab2"])
        C.ld(LAD[b, 16:24, :], ab2[:], ["ab2"], [("LAD", b, 2)])
        ladres = [("LAD", b, 0), ("LAD", b, 1), ("LAD", b, 2)]
        if EV_STOP <= 1:
            return

        for hd in EV_HEADS:
            P.barrier()
            if hd != EV_HEADS[0]:
                for k in range(8):
                    C.ld(hsb[:, k, :], hv[:, k, :], [("HT", b)], ["hsb"])
            delta = hd < 4
            h4 = hd % 4
            if delta:
                cols = [h4 * 128, 512 + h4 * 128, 1024 + h4 * 128, 1536 + h4 * 128]
            else:
                cols = [2064 + h4 * 128, 2576 + h4 * 128, 3088 + h4 * 128, 3600 + h4 * 128]
            for i in range(4):
                C.ld(wts[i][:], win[:, :, cols[i]:cols[i] + 128], [], [("wt", i)], q="pool")
            dsts = [qT, kT, vT]
            dres = ["qT", "kT", "vT"]
            if delta:
                for i in range(3):
                    for (t0, tn) in tiles:
                        bk = proj(wts[i], t0, tn, [("wt", i)])
                        C.cp("act", raw[:, t0:t0 + tn], C.banks[bk][:, 0:tn], [("ps", bk)], ["raw"])
                    gidx = i * 4 + h4
                    d_ = dsts[i]
                    for (s0, sn) in ((0, L), (L, CT)):
                        C.ts("dve", d_[:, s0:s0 + sn], raw[:, s0:s0 + sn], cwe[:, gidx, 1:2], None, ALU.mult, None, ["raw", "cwe"], [dres[i]])
                        C.stt("dve", d_[:, s0 + 1:s0 + sn], raw[:, s0:s0 + sn - 1], cwe[:, gidx, 0:1], d_[:, s0 + 1:s0 + sn], ALU.mult, ALU.add,
                              ["raw", "cwe", dres[i]], [dres[i]])
                        C.stt("dve", d_[:, s0:s0 + sn - 1], raw[:, s0 + 1:s0 + sn], cwe[:, gidx, 2:3], d_[:, s0:s0 + sn - 1], ALU.mult, ALU.add,
                              ["raw", "cwe", dres[i]], [dres[i]])
                    C.act(d_[:], d_[:], AF.Silu, [dres[i]], [dres[i]])
                    if i < 2:
                        for (t0, tn) in tiles:
                            C.act(tmp[:, 0:tn], d_[:, t0:t0 + tn], AF.Square, [dres[i]], ["tmp"])
                            bk = C.bank()
                            C.mm(C.banks[bk][:, 0:tn], ones128[:], tmp[:, 0:tn], True, True, ["ones128", "tmp"], [("ps", bk)])
                            C.ts("dve", tmp2[:, 0:tn], C.banks[bk][:, 0:tn], NORM_EPS, None, ALU.add, None, [("ps", bk)], ["tmp2"])
                            C.act(tmp2[:, 0:tn], tmp2[:, 0:tn], AF.Sqrt, ["tmp2"], ["tmp2"])
                            P.op("dve", (lambda o: (lambda e: e.reciprocal(out=o, in_=o)))(tmp2[:, 0:tn]), ["tmp2"], ["tmp2"])
                            C.stt("dve", d_[:, t0:t0 + tn], d_[:, t0:t0 + tn], DK_SCALE if i == 0 else 1.0, tmp2[:, 0:tn], ALU.mult, ALU.mult,
                                  [dres[i], "tmp2"], [dres[i]])
            else:
                for i in range(2):
                    rot_weights(C, wrs[i], wts[i], 64, ("wt", i), ("wr", i))
                    d_ = dsts[i]
                    sc_ = 1.0 if i == 0 else DK_SCALE
                    for (t0, tn) in tiles[:4]:
                        b1 = proj(wts[i], t0, tn, [("wt", i)])
                        b2 = proj(wrs[i], t0, tn, [("wr", i)])
                        C.tt("dve", tmp[:], C.banks[b1][:, :], cosT[:, t0:t0 + 512], ALU.mult, [("ps", b1), "cosT"], ["tmp"])
                        C.tt("dve", tmp2[:], C.banks[b2][:, :], sinT[:, t0:t0 + 512], ALU.mult, [("ps", b2), "sinT"], ["tmp2"])
                        if i == 0:
                            C.tt("dve", d_[:, t0:t0 + 512], tmp[:], tmp2[:], ALU.add, ["tmp", "tmp2"], [dres[i]])
                        else:
                            C.tt("dve", tmp[:], tmp[:], tmp2[:], ALU.add, ["tmp", "tmp2"], ["tmp"])
                            C.act(d_[:, t0:t0 + 512], tmp[:], AF.Identity, ["tmp"], [dres[i]], scale=sc_)
                    bk = proj(wts[i], L, CT, [("wt", i)])
                    C.act(d_[:, L:T], C.banks[bk][:, 0:CT], AF.Identity, [("ps", bk)], [dres[i]], scale=sc_)
                for (t0, tn) in tiles:
                    bk = proj(wts[2], t0, tn, [("wt", 2)])
                    C.cp("act", vT[:, t0:t0 + tn], C.banks[bk][:, 0:tn], [("ps", bk)], ["vT"])
            for (t0, tn) in tiles:
                bk = proj(wts[3], t0, tn, [("wt", 3)])
                C.act(szT[:, t0:t0 + tn], C.banks[bk][:, 0:tn], AF.Silu, [("ps", bk)], ["szT"])
            if EV_STOP <= 2:
                return
            for (src, sres, dst, dr) in ((kT, "kT", ktok, "ktok"), (vT, "vT", vtok, "vtok")):
                for n0 in range(0, NCH, 4):
                    bk = C.bank()
                    for q in range(4):
                        n = n0 + q
                        C.tr(C.banks[bk][0:64, q * 128:(q + 1) * 128], src[:, n * 64:(n + 1) * 64], ident[:], [sres, "ident"], [("ps", bk)])
                    C.cp("act" if (n0 // 4) % 2 == 0 else "dve", dst[:, n0:n0 + 4, :].rearrange("p a b -> p (a b)"), C.banks[bk][0:64, :],
                         [("ps", bk)], [dr])
            if EV_STOP <= 3:
                return
            for d in range(2):
                row = (d * 4 + h4) if delta else (16 + d * 4 + h4)
                C.ld(lat[:, 0:64], LAD[b, row].rearrange("(n i) -> n i", i=64), ladres, ["lat"])
                if delta:
                    C.ld(lat[:, 64:128], LAD[b, 8 + d * 4 + h4].rearrange("(n i) -> n i", i=64), ladres, ["lat"])
                bkt = C.bank()
                C.tr(C.banks[bkt][0:64, 0:NCH], lat[:, 0:64], ident[0:NCH, 0:NCH], ["lat", "ident"], [("ps", bkt)])
                if delta:
                    C.tr(C.banks[bkt][0:64, 64:64 + NCH], lat[:, 64:128], ident[0:NCH, 0:NCH], ["lat", "ident"], [("ps", bkt)])
                C.cp("dve", la[d][:], C.banks[bkt][0:64, 0:NCH], [("ps", bkt)], [("la", d)])
                if delta:
                    C.cp("dve", bet[d][:], C.banks[bkt][0:64, 64:64 + NCH], [("ps", bkt)], [("bet", d)])
                bk = C.bank()
                C.mm(C.banks[bk][0:64, 0:NCH], U[d], la[d][:], True, True, ["mk", ("la", d)], [("ps", bk)])
                C.cp("dve", G[d][:], C.banks[bk][0:64, 0:NCH], [("ps", bk)], [("G", d)])
                C.act(EG[d][:], C.banks[bk][0:64, 0:NCH], AF.Exp, [("ps", bk)], [("EG", d)])
                bk2 = C.bank()
                C.mm(C.banks[bk2][:, 0:NCH], ones64[:], la[d][:], True, True, ["ones64", ("la", d)], [("ps", bk2)])
                C.act(CD[d][:], C.banks[bk2][:, 0:NCH], AF.Exp, [("ps", bk2)], [("CD", d)])
                C.tt("dve", EKG[d][:], C.banks[bk2][0:64, 0:NCH], G[d][:], ALU.subtract, [("ps", bk2), ("G", d)], [("EKG", d)])
                C.act(EKG[d][:], EKG[d][:], AF.Exp, [("EKG", d)], [("EKG", d)])
                if delta:
                    C.stt("dve", NBEG[d][:], bet[d][:], -1.0, EG[d][:], ALU.mult, ALU.mult, [("bet", d), ("EG", d)], [("NBEG", d)])
            if EV_STOP <= 4:
                return
            P.barrier()
            for n0 in range(0, NCH, 4):
                P.op("pool", (lambda o: (lambda e: e.memset(o, 0.0)))(oacc[:, n0:n0 + 4, :].rearrange("p a b -> p (a b)")), [],
                     [("oacc", n) for n in range(n0, n0 + 4)])
            def chain(n, d, kk_ap, kk_r, qk_ap, qk_r):
                labc, labr = small()
                C.act(labc, ones64[:, 0:64], AF.Identity, ["ones64", ("la", d)], [labr], scale=la[d][:, n:n + 1])
                yield
                pg, pgr = pslot()
                C.mm(pg, labc, U[d], True, False, [labr, "mk"], [pgr])
                C.mm(pg, I64, POS[d], False, True, ["mk"], [pgr])
                yield
                dec, decr = small()
                C.act(dec, pg, AF.Exp, [pgr, ("G", d)], [decr], bias=G[d][:, n:n + 1], scale=-1.0)
                yield
                qk, qkr = small()
                C.tt("dve", qk, qk_ap, dec, ALU.mult, [qk_r, decr], [qkr])
                if delta:
                    decs, decsr = small()
                    C.tt("pool", decs, dec, STRICT[d], ALU.mult, [decr, "mk"], [decsr])
                yield
                pt, ptr = pslot()
                C.tr(pt, qk, I64, [qkr, "mk"], [ptr])
                if delta:
                    Y, Yr = small()
                    C.stt("dve", Y, kk_ap, bet[d][:, n:n + 1], decs, ALU.mult, ALU.mult, [kk_r, ("bet", d), decsr], [Yr])
                yield
                C.cp("act", QKTs[:, 2 * n + d, :], pt, [ptr], [("QKT", n, d)])
                if not delta:
                    return
                px, pxr = pslot()
                C.tr(px, Y, I64, [Yr, "mk"], [pxr])
                yield
                X, Xr = small()
                C.cp("act", X, px, [pxr], [Xr])
                yield
                Pc, Pr = pmat()
                C.tt("pool", Pc, I64, X, ALU.subtract, ["mk", Xr], [Pr])
                yield
                for k in range(5):
                    py, pyr = pslot()
                    C.mm(py, X, Y, True, True, [Xr, Yr], [pyr])
                    if k < 4:
                        px2, px2r = pslotB()
                        C.mm(px2, Y, X, True, True, [Xr, Yr], [px2r])
                    yield
                    Y2, Y2r = small()
                    C.cp("act", Y2, py, [pyr], [Y2r])
                    if k < 4:
                        X2, X2r = small()
                        C.cp("dve", X2, px2, [px2r], [X2r])
                    yield
                    pp, ppr = pslotB()
                    C.mm(pp, Y2, Pc, True, True, [Y2r, Pr], [ppr])
                    yield
                    if k < 4:
                        Pn, Pnr = pmat()
                    else:
                        Pn, Pnr = TTs[:, 2 * n + d, :], ("TT", n, d)
                    C.tt("dve", Pn, pp, Pc, ALU.add, [Pr, ppr], [Pnr])
                    Pc, Pr = Pn, Pnr
                    Y, Yr = Y2, Y2r
                    if k < 4:
                        X, Xr = X2, X2r
                    yield

            GC = int(os.environ.get("EV_GC", 4))
            for gi, n0 in enumerate(range(0, NCH, GC)):
                gens = []
                for q in range(GC):
                    n = n0 + q
                    ksl = kT[:, n * 64:(n + 1) * 64]
                    qsl = qT[:, n * 64:(n + 1) * 64]
                    bkq = gi % 2
                    kk_ap, kk_r = C.banks[bkq][0:64, q * 64:(q + 1) * 64], ("ps", bkq)
                    qk_ap, qk_r = C.banks[bkq][0:64, (4 + q) * 64:(5 + q) * 64], ("ps", bkq)
                    if delta:
                        C.mm(kk_ap, ksl, ksl, True, True, ["kT"], [kk_r])
                    C.mm(qk_ap, qsl, ksl, True, True, ["kT", "qT"], [qk_r])
                    for d in range(2):
                        gens.append(chain(n, d, kk_ap, kk_r, qk_ap, qk_r))
                lev = 0
                while gens:
                    nxt = []
                    lev += 1
                    rnd[0] += 1
                    if lev > int(os.environ.get("EV_LEV", 999)):
                        break
                    for g in gens:
                        try:
                            next(g)
                            nxt.append(g)
                        except StopIteration:
                            pass
                    gens = nxt
            if EV_STOP <= 5:
                return
            P.barrier()
            for d in range(2):
                P.op("pool", (lambda o: (lambda e: e.memset(o, 0.0)))(S[d][:]), [], [("S", d)])
            order = [list(range(32, 36)) + list(range(0, 32)), list(range(35, 31, -1)) + list(range(31, -1, -1))]
            nvb = 0
            for step in range(NCH):
                for d in range(2):
                    n = order[d][step]
                    ksl = kT[:, n * 64:(n + 1) * 64]
                    qsl = qT[:, n * 64:(n + 1) * 64]
                    Sd, Sr = S[d][:], ("S", d)
                    vn = vb[nvb % 4]
                    vnr = ("vb", nvb % 4)
                    nvb += 1
                    vs = vb[nvb % 4]
                    vsr = ("vb", nvb % 4)
                    nvb += 1
                    if delta:
                        b1 = C.bank()
                        C.mm(C.banks[b1][0:64, 0:128], ksl, Sd, True, True, ["kT", Sr], [("ps", b1)])
                        C.act(vs[:], vtok[:, n, :], AF.Identity, ["vtok", ("bet", d)], [vsr], scale=bet[d][:, n:n + 1])
                        C.stt("dve", vs[:], C.banks[b1][0:64, 0:128], NBEG[d][:, n:n + 1], vs[:], ALU.mult, ALU.add,
                              [("ps", b1), ("NBEG", d), vsr], [vsr])
                        b2 = C.bank()
                        C.mm(C.banks[b2][0:64, 0:128], TTs[:, 2 * n + d, :], vs[:], True, True, [("TT", n, d), vsr], [("ps", b2)])
                        C.cp("act", vn[:], C.banks[b2][0:64, 0:128], [("ps", b2)], [vnr])
                        C.ts("dve", vs[:], C.banks[b2][0:64, 0:128], EKG[d][:, n:n + 1], None, ALU.mult, None, [("ps", b2), ("EKG", d)], [vsr])
                        vnap = vn[:]
                    else:
                        vnap = vtok[:, n, :]
                        vnr = "vtok"
                        C.act(vs[:], vtok[:, n, :], AF.Identity, ["vtok", ("EKG", d)], [vsr], scale=EKG[d][:, n:n + 1])
                    b3 = C.bank()
                    C.mm(C.banks[b3][0:64, 0:128], qsl, Sd, True, True, ["qT", Sr], [("ps", b3)])
                    b4 = C.bank()
                    C.mm(C.banks[b4][0:64, 0:128], QKTs[:, 2 * n + d, :], vnap, True, True, [("QKT", n, d), vnr], [("ps", b4)])
                    ores = ("oacc", n)
                    C.stt("dve", oacc[:, n, :], C.banks[b3][0:64, 0:128], EG[d][:, n:n + 1], oacc[:, n, :], ALU.mult, ALU.add,
                          [("ps", b3), ("EG", d), ores], [ores])
                    C.tt("dve", oacc[:, n, :], C.banks[b4][0:64, 0:128], oacc[:, n, :], ALU.add, [ores, ("ps", b4)], [ores])
                    b5 = C.bank()
                    C.mm(C.banks[b5][:, 0:128], ktok[:, n, :], vs[:], True, True, ["ktok", vsr], [("ps", b5)])
                    C.act(Sd, Sd, AF.Identity, [Sr, ("CD", d)], [Sr], scale=CD[d][:, n:n + 1])
                    C.tt("dve", Sd, C.banks[b5][:, 0:128], Sd, ALU.add, [Sr, ("ps", b5)], [Sr])
            if EV_STOP <= 6:
                return
            ores_all = [("oacc", n) for n in range(NCH)]
            for n0 in range(0, NCH, 4):
                sl = oacc[:, n0:n0 + 4, :].rearrange("p a b -> p (a b)")
                C.act(tmp[0:64, :], sl, AF.Square, ores_all[n0:n0 + 4], ["tmp"])
                P.op("dve", (lambda o, i: (lambda e: e.reduce_sum(out=o, in_=i, axis=AX.X)))(
                    st[:, n0:n0 + 4], tmp[0:64, :].rearrange("p (a b) -> p a b", b=128)), ["tmp"], ["st"])
                P.op("dve", (lambda o, i: (lambda e: e.reduce_sum(out=o, in_=i, axis=AX.X)))(
                    st[:, NCH + n0:NCH + n0 + 4], oacc[:, n0:n0 + 4, :]), ores_all[n0:n0 + 4], ["st"])
            ssq = st[:, 0:NCH]
            ssum = st[:, NCH:2 * NCH]
            rstd = st[:, 2 * NCH:3 * NCH]
            mean = st[:, 3 * NCH:4 * NCH]
            C.ts("dve", mean, ssum, 1.0 / 128.0, None, ALU.mult, None, ["st"], ["st"])
            if delta:
                C.ts("dve", rstd, ssq, 1.0 / 128.0, NORM_EPS, ALU.mult, ALU.add, ["st"], ["st"])
            else:
                C.tt("dve", rstd, mean, mean, ALU.mult, ["st"], ["st"])
                C.stt("dve", rstd, ssq, 1.0 / 128.0, rstd, ALU.mult, ALU.subtract, ["st"], ["st"])
                C.ts("dve", rstd, rstd, NORM_EPS, None, ALU.add, None, ["st"], ["st"])
            C.act(rstd, rstd, AF.Sqrt, ["st"], ["st"])
            P.op("dve", (lambda o: (lambda e: e.reciprocal(out=o, in_=o)))(rstd), ["st"], ["st"])
            for n in range(NCH):
                if delta:
                    C.ts("dve", oacc[:, n, :], oacc[:, n, :], rstd[:, n:n + 1], None, ALU.mult, None,
                         [("oacc", n), "st"], [("oacc", n)])
                else:
                    C.ts("dve", oacc[:, n, :], oacc[:, n, :], mean[:, n:n + 1], rstd[:, n:n + 1], ALU.subtract, ALU.mult,
                         [("oacc", n), "st"], [("oacc", n)])
            for n0 in range(0, NCH, 8):
                nn = min(8, NCH - n0)
                bk = C.bank()
                for q in range(nn):
                    C.tr(C.banks[bk][:, q * 64:(q + 1) * 64], oacc[:, n0 + q, :], I64, [("oacc", n0 + q), "mk"], [("ps", bk)])
                t0 = n0 * 64
                tn = nn * 64
                if delta:
                    C.stt("dve", yT[:, t0:t0 + tn], C.banks[bk][:, 0:tn], nw[:, 0:1], szT[:, t0:t0 + tn], ALU.mult, ALU.mult,
                          [("ps", bk), "nw", "szT"], ["yT"])
                else:
                    C.tt("dve", yT[:, t0:t0 + tn], C.banks[bk][:, 0:tn], szT[:, t0:t0 + tn], ALU.mult, [("ps", bk), "szT"], ["yT"])
            C.ld(C.dram["YT"][b, hd * 128:(hd + 1) * 128, :], yT[:], ["yT"], [("Yw", b, hd)])


def build(stages, dbg=()):
    nc = bass.Bass("TRN2", target_bir_lowering=False)
    C = Ctx(nc)
    I = {}

    def inp(name, shape):
        I[name] = nc.dram_tensor(name, list(shape), F32, kind="ExternalInput").ap()

    inp("x", (NB, L, D)); inp("ctx", (NB, CT, D)); inp("cvec", (3, D))
    inp("mod_w", (2, D, 6 * D)); inp("mod_b", (2, 6 * D)); inp("ln_g", (2, 2, D)); inp("ln_b", (2, 2, D))
    inp("e_w_in", (1, D, EVEN_IN)); inp("e_conv", (1, 3, 1536)); inp("e_a_log", (1, 2, 4)); inp("e_dt_bias", (1, 2, 4))
    inp("e_norm_w", (1, 128)); inp("e_ret_decay", (1, 2, 4)); inp("e_w_out", (1, D, D))
    inp("o_w_qkv", (1, D, 3 * D)); inp("o_lambda", (1, 4, 64)); inp("o_subln_w", (1, 128)); inp("o_w_out", (1, D, D))
    inp("f_w_gate", (2, D, DFF)); inp("f_w_up", (2, D, DFF)); inp("f_conv", (2, 3, 3, DFF)); inp("f_w_down", (2, DFF, D))
    inp("ident", (128, 128)); inp("rope2_cos", (128, L)); inp("rope2_sin", (128, L))
    inp("rope1_cos", (128, L)); inp("rope1_sin", (128, L)); inp("masks", (8, 64, 64))
    I["out"] = nc.dram_tensor("out", [NB, L, D], F32, kind="ExternalOutput").ap()
    C.I = I
    def kind_of(nm):
        if nm in dbg:
            return "ExternalOutput"
        if nm + "in" in dbg:
            return "ExternalInput"
        return "Internal"

    C.dram["modv"] = nc.dram_tensor("modv", [2, 3, 6 * D], F32, kind=kind_of("modv")).ap()
    for nm in ("XA", "XB"):
        C.dram[nm] = nc.dram_tensor(nm, [NB, T, D], F32, kind=kind_of(nm)).ap()
    C.dram["HT"] = nc.dram_tensor("HT", [NB, D, T], BF16, kind=kind_of("HT")).ap()
    C.dram["YT"] = nc.dram_tensor("YT", [NB, D, T], BF16, kind=kind_of("YT")).ap()
    C.dram["GT"] = nc.dram_tensor("GT", [NB, DFF, T], BF16, kind=kind_of("GT")).ap()
    C.dram["LAD"] = nc.dram_tensor("LAD", [NB, 24, T], F32, kind=kind_of("LAD")).ap()
    S = stages
    if "mod" in S:
        stage_mod(C)
    if "modT0" in S:
        stage_modT0(C)
    if "even" in S:
        stage_even(C)
    if "pa0" in S:
        stage_projln(C, 0, "a", C.dram["YT"], 8, I["e_w_out"][0], "input", "XA", NT, (0, "f"))
    if "ffn0" in S:
        stage_ffn1(C, 0, True)
    if "pf0" in S:
        stage_projln(C, 0, "f", C.dram["GT"], 22, I["f_w_down"][0], "XA", "XB", NT, (1, "a"))
    if "odd" in S:
        stage_odd(C)
    if "pa1" in S:
        stage_projln(C, 1, "a", C.dram["YT"], 8, I["o_w_out"][0], "XB", "XA", 16, (1, "f"))
    if "ffn1" in S:
        stage_ffn1(C, 1, False)
    if "pf1" in S:
        stage_projln(C, 1, "f", C.dram["GT"], 22, I["f_w_down"][1], "XA", "out", 16, None)
    C.P.emit()
    return nc


def make_consts():
    c = {}
    c["ident"] = np.eye(128, dtype=np.float32)
    pos = np.arange(L)
    inv2 = 10000.0 ** (-np.arange(16, dtype=np.float64) / 16)
    ang2 = np.concatenate([(pos // 64)[:, None] * inv2[None], (pos % 64)[:, None] * inv2[None]], -1)
    idx2 = (np.arange(128) % 64) % 32
    c["rope2_cos"] = np.ascontiguousarray(np.cos(ang2)[:, idx2].T.astype(np.float32))
    c["rope2_sin"] = np.ascontiguousarray(np.sin(ang2)[:, idx2].T.astype(np.float32))
    inv1 = 10000.0 ** (-np.arange(64, dtype=np.float64) / 64)
    ang1 = pos[:, None] * inv1[None]
    idx1 = np.arange(128) % 64
    c["rope1_cos"] = np.ascontiguousarray(np.cos(ang1)[:, idx1].T.astype(np.float32))
    c["rope1_sin"] = np.ascontiguousarray(np.sin(ang1)[:, idx1].T.astype(np.float32))
    c["masks"] = make_masks()
    return c


def make_masks():
    m = np.zeros((8, 64, 64), np.float32)
    a = np.arange(64)
    le = (a[:, None] <= a[None, :]).astype(np.float32)
    m[0] = le
    m[1] = le.T
    m[2] = np.where(a[None, :] <= a[:, None], 0.0, -NEG)
    m[3] = np.where(a[None, :] >= a[:, None], 0.0, -NEG)
    m[4] = (a[None, :] < a[:, None]).astype(np.float32)
    m[5] = (a[None, :] > a[:, None]).astype(np.float32)
    m[6] = np.eye(64, dtype=np.float32)
    m[7] = 1.0
    return m


ALL_STAGES = ["mod", "modT0", "even", "pa0", "ffn0", "pf0", "odd", "pa1", "ffn1", "pf1"]
_NC_CACHE = {}


def kernel(**inputs):
    n = 8
    if "nc" not in _NC_CACHE:
        _NC_CACHE["nc"] = build(ALL_STAGES)
    nc = _NC_CACHE["nc"]
    consts = make_consts()
    shared = {k: np.ascontiguousarray(np.asarray(v, dtype=np.float32)) for k, v in inputs.items()
              if k not in ("x", "c", "ctx", "c_ctx")}
    x = np.asarray(inputs["x"], dtype=np.float32)
    ctx = np.asarray(inputs["ctx"], dtype=np.float32)
    c = np.asarray(inputs["c"], dtype=np.float32)
    c_ctx = np.asarray(inputs["c_ctx"], dtype=np.float32)
    in_maps = []
    for i in range(n):
        m = dict(shared)
        m.update(consts)
        m["x"] = np.ascontiguousarray(x[NB * i:NB * (i + 1)])
        m["ctx"] = np.ascontiguousarray(ctx[NB * i:NB * (i + 1)])
        m["cvec"] = np.ascontiguousarray(np.stack([c[NB * i], c[NB * i + 1], c_ctx]))
        in_maps.append(m)
    res = run_bass_kernel_spmd(nc, in_maps, core_ids=list(range(n)))
    return np.concatenate([r["out"] for r in res.results], axis=0).astype(np.float32)
```

```python
import math
from contextlib import ExitStack

import numpy as np
import concourse.bass as bass
import concourse.mybir as mybir
from concourse.bass_utils import run_bass_kernel_spmd

F32 = mybir.dt.float32
BF16 = mybir.dt.bfloat16
AF = mybir.ActivationFunctionType
ALU = mybir.AluOpType
AX = mybir.AxisListType

SAME_ENGINE_SYNC = True
N_DMA_SEMS = 24
N_HW_SEMS = 16

NB = 2
L = 2048
CT = 256
T = L + CT
D = 1024
DFF = 2816
NT = T // 128
ALPHA = 4.0 ** 0.25
LN_EPS = 1e-5
NORM_EPS = 1e-6
LAMBDA_INIT = 0.8 - 0.6 * math.exp(-0.3)
EVEN_IN = 4112
NEG = -30000.0
SB_BASE = 16640


class Op:
    __slots__ = ("eng", "fn", "deps", "signal", "count", "dma", "sem", "target", "idx")


class Prog:
    ENGS = ("pe", "act", "dve", "pool", "sp")

    def __init__(self, nc):
        self.nc = nc
        self.ops = []
        self.last_w = {}
        self.readers = {}
        self.dma_rr = 0
        self.dma_rr_sw = 0
        self.dma_sem_total = [0] * N_DMA_SEMS
        self.dma_sem_lastop = [None] * N_DMA_SEMS
        self.last_eng = {}
        self.barrier_deps = []

    def barrier(self):
        deps = list(self.last_eng.values())
        deps += [o for o in self.dma_sem_lastop if o is not None]
        self.barrier_deps = deps
        self.last_w = {}
        self.readers = {}

    def _add(self, eng, fn, reads, writes, dma):
        op = Op()
        op.eng = eng
        op.fn = fn
        op.signal = False
        op.count = 0
        op.dma = dma
        op.sem = None
        op.target = 0
        op.idx = len(self.ops)
        deps = {}
        ps_reads = [r for r in reads if isinstance(r, tuple) and r and r[0] == "ps"]
        if ps_reads:
            writes = list(writes) + [r for r in ps_reads if r not in writes]
        for d in self.barrier_deps:
            deps[d.idx] = d
        for r in reads:
            w = self.last_w.get(r)
            if w is not None:
                deps[w.idx] = w
        for r in writes:
            w = self.last_w.get(r)
            if w is not None:
                deps[w.idx] = w
            for rd in self.readers.get(r, ()):
                deps[rd.idx] = rd
        if dma:
            if eng == "pool":
                i = N_HW_SEMS + self.dma_rr_sw
                self.dma_rr_sw = (self.dma_rr_sw + 1) % (N_DMA_SEMS - N_HW_SEMS)
            else:
                i = self.dma_rr
                self.dma_rr = (self.dma_rr + 1) % N_HW_SEMS
            prev = self.dma_sem_lastop[i]
            if prev is not None:
                deps[prev.idx] = prev
            self.dma_sem_total[i] += 16
            op.sem = i
            op.target = self.dma_sem_total[i]
            self.dma_sem_lastop[i] = op
        else:
            self.last_eng[eng] = op
        op.deps = list(deps.values())
        for r in reads:
            self.readers.setdefault(r, []).append(op)
        for r in writes:
            self.last_w[r] = op
            self.readers[r] = []
        self.ops.append(op)
        return op

    def op(self, eng, fn, reads=(), writes=()):
        return self._add(eng, fn, reads, writes, False)

    def dma(self, eng, fn, reads=(), writes=()):
        return self._add(eng, fn, reads, writes, True)

    @staticmethod
    def _skip(d, op):
        return d.eng == op.eng and (d.eng == "pe" or not SAME_ENGINE_SYNC) and not op.dma

    def emit(self):
        nc = self.nc
        for op in self.ops:
            for d in op.deps:
                if d.dma or self._skip(d, op):
                    continue
                d.signal = True
        cnt = {e: 0 for e in self.ENGS}
        for op in self.ops:
            if op.dma:
                continue
            if op.signal:
                cnt[op.eng] += 1
                op.count = cnt[op.eng]
        streams = {e: [o for o in self.ops if o.eng == e] for e in self.ENGS}
        with ExitStack() as es:
            esem = {e: es.enter_context(nc.semaphore("s_" + e)) for e in self.ENGS}
            dsem = [es.enter_context(nc.semaphore("d%d" % i)) for i in range(N_DMA_SEMS)]
            block = es.enter_context(nc.Block())
            all_dma_final = [(dsem[i], self.dma_sem_total[i]) for i in range(N_DMA_SEMS)
                             if self.dma_sem_total[i] > 0]

            def run_stream(e, eng):
                waited = {}
                for op in streams[e]:
                    for d in op.deps:
                        if d.dma:
                            key, val, sem = ("d", d.sem), d.target, dsem[d.sem]
                        else:
                            if self._skip(d, op):
                                continue
                            key, val, sem = ("e", d.eng), d.count, esem[d.eng]
                        if waited.get(key, 0) >= val:
                            continue
                        waited[key] = val
                        eng.wait_ge(sem, val)
                    ins = op.fn(eng)
                    if op.dma:
                        ins.then_inc(dsem[op.sem], 16)
                    elif op.signal:
                        ins.then_inc(esem[e], 1)
                if e == "sp":
                    for sem, val in all_dma_final:
                        eng.wait_ge(sem, val)

            @block.tensor
            def _(eng):
                run_stream("pe", eng)

            @block.scalar
            def _(eng):
                run_stream("act", eng)

            @block.vector
            def _(eng):
                run_stream("dve", eng)

            @block.gpsimd
            def _(eng):
                run_stream("pool", eng)

            @block.sync
            def _(eng):
                run_stream("sp", eng)


class Ctx:
    def __init__(self, nc):
        self.nc = nc
        self.P = Prog(nc)
        self.sb_off = SB_BASE
        self.sb_n = 0
        self.banks = [nc.alloc_psum_tensor("bank%d" % i, [128, 512], F32) for i in range(8)]
        self.bank_rr = 0
        self.dram = {}

    def sb_reset(self):
        self.P.barrier()
        self.sb_off = SB_BASE

    def sb(self, name, shape, dt):
        nbytes = int(np.prod(shape[1:])) * (2 if dt == BF16 else 4)
        nbytes = (nbytes + 63) // 64 * 64
        self.sb_n += 1
        t = self.nc.alloc_sbuf_tensor_at("%s_%d" % (name, self.sb_n), list(shape), dt, offset=self.sb_off)
        self.sb_off += nbytes
        assert self.sb_off <= 229376, (name, self.sb_off)
        return t

    def sb_at(self, name, shape, dt, off):
        self.sb_n += 1
        return self.nc.alloc_sbuf_tensor_at("%s_%d" % (name, self.sb_n), list(shape), dt, offset=off)

    def bank(self):
        i = self.bank_rr
        self.bank_rr = (self.bank_rr + 1) % 8
        return i

    def mm(self, out, lhsT, rhs, start, stop, reads, writes):
        self.P.op("pe", lambda e: e.matmul(out, lhsT=lhsT, rhs=rhs, start=start, stop=stop), reads, writes)

    def tr(self, out, in_, ident, reads, writes):
        self.P.op("pe", lambda e: e.transpose(out, in_, ident), reads, writes)

    def act(self, out, in_, func, reads, writes, bias=None, scale=None):
        kw = {}
        if bias is not None:
            kw["bias"] = bias
        if scale is not None:
            kw["scale"] = scale
        self.P.op("act", lambda e: e.activation(out=out, in_=in_, func=func, **kw), reads, writes)

    def tt(self, eng, out, in0, in1, op, reads, writes):
        self.P.op(eng, lambda e: e.tensor_tensor(out=out, in0=in0, in1=in1, op=op), reads, writes)

    def ts(self, eng, out, in0, s1, s2, op0, op1, reads, writes):
        if op1 is None:
            self.P.op(eng, lambda e: e.tensor_scalar(out=out, in0=in0, scalar1=s1, scalar2=None, op0=op0), reads, writes)
        else:
            self.P.op(eng, lambda e: e.tensor_scalar(out=out, in0=in0, scalar1=s1, scalar2=s2, op0=op0, op1=op1), reads, writes)

    def stt(self, eng, out, in0, scalar, in1, op0, op1, reads, writes):
        self.P.op(eng, lambda e: e.scalar_tensor_tensor(out=out, in0=in0, scalar=scalar, in1=in1, op0=op0, op1=op1), reads, writes)

    def cp(self, eng, out, in_, reads, writes):
        if eng == "act":
            self.P.op("act", lambda e: e.copy(out=out, in_=in_), reads, writes)
        else:
            self.P.op(eng, lambda e: e.tensor_copy(out=out, in_=in_), reads, writes)

    def ld(self, out, in_, reads, writes, q="sp", slow=False):
        if slow:
            self.P.dma(q, lambda e: e.dma_start(out=out, in_=in_, allow_slow_non_contiguous=True), reads, writes)
        else:
            self.P.dma(q, lambda e: e.dma_start(out=out, in_=in_), reads, writes)


def pp_view(vec_ap):
    return vec_ap.rearrange("(k p) -> p k", p=128)


def stage_mod(C):
    C.sb_reset()
    I = C.I
    cT = C.sb("cT", [128, 8, 3], F32)
    for r in range(3):
        for k0 in range(0, 8, 4):
            C.ld(cT[:, k0:k0 + 4, r], pp_view(I["cvec"][r])[:, k0:k0 + 4], [], ["cT"], slow=True)
    C.act(cT[:], cT[:], AF.Silu, ["cT"], ["cT"])
    wt = [C.sb("modw%d" % i, [128, 8, 512], F32) for i in range(2)]
    mb = C.sb("modb", [3, 6144], F32)
    res = C.sb("modres", [3, 6144], F32)
    n = 0
    for li in range(2):
        C.ld(mb[:], I["mod_b"][li].partition_broadcast(3), [], ["mb"])
        wv = I["mod_w"][li].rearrange("(k p) n -> p k n", p=128)
        for j in range(12):
            w = wt[n % 2]
            wk = ("modw", n % 2)
            n += 1
            C.ld(w[:], wv[:, :, j * 512:(j + 1) * 512], [], [wk])
            bk = C.bank()
            for k in range(8):
                C.mm(C.banks[bk][0:3, :], cT[:, k, :], w[:, k, :], k == 0, k == 7, ["cT", wk], [("ps", bk)])
            C.tt("dve", res[:, j * 512:(j + 1) * 512], C.banks[bk][0:3, :], mb[:, j * 512:(j + 1) * 512], ALU.add,
                 [("ps", bk), "mb"], ["modres"])
        C.ld(C.dram["modv"][li], res[:], ["modres"], [("modv", li)])


def load_mod_pp(C, li, r, which, name):
    base = 0 if which == "a" else 3 * D
    sh = C.sb(name + "sh", [128, 8], F32)
    sc = C.sb(name + "sc", [128, 8], F32)
    mv = C.dram["modv"][li, r]
    for k0 in range(0, 8, 4):
        C.ld(sh[:, k0:k0 + 4], pp_view(mv[base:base + D])[:, k0:k0 + 4], [("modv", li)], [name + "sh"], slow=True)
        C.ld(sc[:, k0:k0 + 4], pp_view(mv[base + D:base + 2 * D])[:, k0:k0 + 4], [("modv", li)], [name + "sc"], slow=True)
    C.ts("dve", sc[:], sc[:], 1.0, None, ALU.add, None, [name + "sc"], [name + "sc"])
    return sh, sc


def emit_modT(C, xt, xres, sh, sc, mres, hdst, hres, tag, slot):
    ht = C.hT_tiles[slot]
    hk = ("hTt", slot)
    for half in range(2):
        bk = C.bank()
        for q in range(4):
            k = half * 4 + q
            C.tr(C.banks[bk][:, q * 128:(q + 1) * 128], xt[:, k * 128:(k + 1) * 128], C.ident[:], [xres, "ident"], [("ps", bk)])
        for q in range(4):
            k = half * 4 + q
            C.act(ht[:, k, :], C.banks[bk][:, q * 128:(q + 1) * 128], AF.Identity, [("ps", bk)] + mres, [hk],
                  bias=sh[:, k:k + 1], scale=sc[:, k:k + 1])
    C.ld(hdst, ht[:], [hk], [hres])


def x_in_ap(C, src, b, t):
    if src == "input":
        if t < 16:
            return C.I["x"][b, t * 128:(t + 1) * 128, :]
        return C.I["ctx"][b, (t - 16) * 128:(t - 15) * 128, :]
    return C.dram[src][b, t * 128:(t + 1) * 128, :]


def hT_dst(C, b, t):
    return C.dram["HT"][b].rearrange("(k p) t -> p k t", p=128)[:, :, t * 128:(t + 1) * 128]


def stage_modT0(C):
    C.sb_reset()
    C.ident = C.sb("ident", [128, 128], F32)
    C.ld(C.ident[:], C.I["ident"], [], ["ident"])
    C.hT_tiles = [C.sb("hTt%d" % i, [128, 8, 128], BF16) for i in range(2)]
    xts = [C.sb("xt%d" % i, [128, D], F32) for i in range(3)]
    n = 0
    for b in range(NB):
        sh, sc = load_mod_pp(C, 0, b, "a", "m%d" % b)
        shc, scc = load_mod_pp(C, 0, 2, "a", "mc%d" % b)
        for t in range(NT):
            xt = xts[n % 3]
            xk = ("xt", n % 3)
            C.ld(xt[:], x_in_ap(C, "input", b, t), [], [xk])
            if t < 16:
                emit_modT(C, xt, xk, sh, sc, ["m%dsh" % b, "m%dsc" % b], hT_dst(C, b, t), ("HT", b, t), "m0", n % 2)
            else:
                emit_modT(C, xt, xk, shc, scc, ["mc%dsh" % b, "mc%dsc" % b], hT_dst(C, b, t), ("HT", b, t), "m0", n % 2)
            n += 1


def stage_projln(C, li, which, ysrc, KC, W, xsrc, xdst, ntiles, nxt):
    C.sb_reset()
    C.ident = C.sb("ident", [128, 128], F32)
    C.ld(C.ident[:], C.I["ident"], [], ["ident"])
    C.hT_tiles = [C.sb("hTt%d" % i, [128, 8, 128], BF16) for i in range(2)]
    wsb = C.sb("wsb", [128, KC, D], BF16)
    wv = W.rearrange("(k p) n -> p k n", p=128)
    for k0 in range(0, KC, 2):
        k1 = min(KC, k0 + 2)
        C.ld(wsb[:, k0:k1, :], wv[:, k0:k1, :], [], [("wsb", k0)], q="pool")
    wres = [("wsb", k0) for k0 in range(0, KC, 2)]
    lnrow = 0 if which == "a" else 1
    lng = C.sb("lng", [128, D], F32)
    lnb = C.sb("lnb", [128, D], F32)
    C.ld(lng[:], C.I["ln_g"][li, lnrow].partition_broadcast(128), [], ["lng"])
    C.ld(lnb[:], C.I["ln_b"][li, lnrow].partition_broadcast(128), [], ["lnb"])
    goff = 2 * D if which == "a" else 5 * D
    gates = {}
    for r in ([0, 1, 2] if ntiles > 16 else [0, 1]):
        g = C.sb("gate%d" % r, [128, D], F32)
        C.ld(g[:], C.dram["modv"][li, r, goff:goff + D].partition_broadcast(128), [("modv", li)], [("gate", r)])
        gates[r] = g
    mods = {}
    if nxt is not None:
        for r in ([0, 1, 2] if ntiles > 16 else [0, 1]):
            mods[r] = load_mod_pp(C, nxt[0], r, nxt[1], "nm%d" % r)
    yts = [C.sb("yt%d" % i, [128, KC, 512], BF16) for i in range(2)]
    xts = [C.sb("xt%d" % i, [128, D], F32) for i in range(2)]
    zs = [C.sb("z%d" % i, [128, D], F32) for i in range(2)]
    xos = [C.sb("xo%d" % i, [128, D], F32) for i in range(2)]
    st = [C.sb("st%d" % i, [128, 2, 6], F32) for i in range(2)]
    mv = [C.sb("mv%d" % i, [128, 4], F32) for i in range(2)]
    n = 0
    ng = 0
    for b in range(NB):
        yv = ysrc[b].rearrange("(k p) t -> p k t", p=128)
        for g0 in range(0, ntiles, 4):
            g1 = min(ntiles, g0 + 4)
            yt = yts[ng % 2]
            yk = ("yt", ng % 2)
            ng += 1
            ntok = (g1 - g0) * 128
            for k0 in range(0, KC, 8):
                k1 = min(KC, k0 + 8)
                C.ld(yt[:, k0:k1, 0:ntok], yv[:, k0:k1, g0 * 128:g1 * 128], [("Y", b)], [yk])
            for t in range(g0, g1):
                s = n % 2
                n += 1
                r = b if t < 16 else 2
                xt, z, xo = xts[s], zs[s], xos[s]
                xk, zk, xok, stk = ("xt", s), ("z", s), ("xo", s), ("st", s)
                C.ld(xt[:], x_in_ap(C, xsrc, b, t), [("X", xsrc, b)], [xk])
                bks = [C.bank(), C.bank()]
                for nn in range(2):
                    for k in range(KC):
                        C.mm(C.banks[bks[nn]][:, :], yt[:, k, (t - g0) * 128:(t - g0 + 1) * 128],
                             wsb[:, k, nn * 512:(nn + 1) * 512], k == 0, k == KC - 1, [yk] + wres, [("ps", bks[nn])])
                for nn in range(2):
                    C.tt("dve", z[:, nn * 512:(nn + 1) * 512], C.banks[bks[nn]][:, :], gates[r][:, nn * 512:(nn + 1) * 512],
                         ALU.mult, [("ps", bks[nn]), ("gate", r)], [zk])
                C.stt("dve", z[:], xt[:], ALPHA, z[:], ALU.mult, ALU.add, [xk, zk], [zk])
                for nn in range(2):
                    C.P.op("dve", (lambda o, i: (lambda e: e.bn_stats(out=o, in_=i)))(st[s][:, nn, :], z[:, nn * 512:(nn + 1) * 512]),
                           [zk], [stk])
                C.P.op("dve", (lambda o, i: (lambda e: e.bn_aggr(out=o, in_=i)))(mv[s][:, 0:2], st[s][:].rearrange("p a b -> p (a b)")),
                       [stk], [stk])
                C.ts("dve", mv[s][:, 2:3], mv[s][:, 1:2], LN_EPS, None, ALU.add, None, [stk], [stk])
                C.act(mv[s][:, 2:3], mv[s][:, 2:3], AF.Sqrt, [stk], [stk])
                C.P.op("dve", (lambda o: (lambda e: e.reciprocal(out=o, in_=o)))(mv[s][:, 2:3]), [stk], [stk])
                C.stt("dve", mv[s][:, 3:4], mv[s][:, 0:1], -1.0, mv[s][:, 2:3], ALU.mult, ALU.mult, [stk], [stk])
                C.act(xo[:], z[:], AF.Identity, [zk, stk], [xok], bias=mv[s][:, 3:4], scale=mv[s][:, 2:3])
                C.tt("dve", xo[:], xo[:], lng[:], ALU.mult, [xok, "lng"], [xok])
                C.tt("dve", xo[:], xo[:], lnb[:], ALU.add, [xok, "lnb"], [xok])
                if xdst == "out":
                    C.ld(C.I["out"][b, t * 128:(t + 1) * 128, :], xo[:], [xok], [("OUT", b, t)])
                else:
                    C.ld(C.dram[xdst][b, t * 128:(t + 1) * 128, :], xo[:], [xok], [("Xw", xdst, b, t)])
                if nxt is not None:
                    sh, sc = mods[r]
                    emit_modT(C, xo, xok, sh, sc, ["nm%dsh" % r, "nm%dsc" % r], hT_dst(C, b, t), ("HT", b, t), "pl", s)


def stage_ffn1(C, li, with_ctx):
    C.sb_reset()
    ntok = T if with_ctx else L
    hsb = C.sb("hsb", [128, 8, T], BF16)
    cw = C.sb("cw", [128, 22, 9], F32)
    for i in range(3):
        for j in range(3):
            for c0 in range(0, 22, 4):
                c1 = min(22, c0 + 4)
                C.ld(cw[:, c0:c1, 3 * i + j], C.I["f_conv"][li, i, j].rearrange("(c p) -> p c", p=128)[:, c0:c1], [], ["cw"], slow=True)
    identf = C.sb("identf", [128, 128], F32)
    identb = C.sb("identb", [128, 128], BF16)
    C.ld(identf[:], C.I["ident"], [], ["identf"])
    C.cp("dve", identb[:], identf[:], ["identf"], ["identb"])
    wgs = [C.sb("wg%d" % i, [128, 8, 128], BF16) for i in range(2)]
    wus = [C.sb("wu%d" % i, [128, 8, 128], BF16) for i in range(2)]
    dgs = [C.sb("dg%d" % i, [128, 9, 128], BF16) for i in range(2)]
    gpads = [C.sb("gpad%d" % i, [128, 34, 66], BF16) for i in range(2)]
    gctxs = [C.sb("gctx%d" % i, [128, CT + 2], BF16) for i in range(2)]
    us = [C.sb("u%d" % i, [128, T], F32) for i in range(2)]
    svs = [C.sb("sv%d" % i, [128, T], F32) for i in range(2)]
    gbs = [C.sb("gb%d" % i, [128, T], BF16) for i in range(2)]
    for i in range(2):
        C.P.op("pool", (lambda o: (lambda e: e.memset(o, 0.0)))(gpads[i][:].rearrange("p a b -> p (a b)")), [], [("gpad", i)])
        C.P.op("pool", (lambda o: (lambda e: e.memset(o, 0.0)))(gctxs[i][:]), [], [("gctx", i)])
    wgv = C.I["f_w_gate"][li].rearrange("(k p) n -> p k n", p=128)
    wuv = C.I["f_w_up"][li].rearrange("(k p) n -> p k n", p=128)
    n = 0
    for b in range(NB):
        hv = C.dram["HT"][b].rearrange("(k p) t -> p k t", p=128)
        for k in range(8):
            C.ld(hsb[:, k, 0:ntok], hv[:, k, 0:ntok], [("HT", b)], ["hsb"])
        for c in range(22):
            s = n % 2
            n += 1
            wg, wu, dg, gpad, gctx, u, sv, gb = wgs[s], wus[s], dgs[s], gpads[s], gctxs[s], us[s], svs[s], gbs[s]
            C.ld(wg[:], wgv[:, :, c * 128:(c + 1) * 128], [], [("wg", s)], q="pool")
            C.ld(wu[:], wuv[:, :, c * 128:(c + 1) * 128], [], [("wu", s)], q="pool")
            for tap in range(9):
                C.ts("dve", dg[:, tap, :], identb[:], cw[:, c, tap:tap + 1], None, ALU.mult, None, ["identb", "cw"], [("dg", s)])
            for tt_ in range(4):
                bk = C.bank()
                for k in range(8):
                    C.mm(C.banks[bk][:, :], wg[:, k, :], hsb[:, k, tt_ * 512:(tt_ + 1) * 512], k == 0, k == 7, ["hsb", ("wg", s)], [("ps", bk)])
                C.cp("act", gpad[:, 1 + 8 * tt_:9 + 8 * tt_, 1:65], C.banks[bk][:, :].rearrange("p (r w) -> p r w", w=64),
                     [("ps", bk)], [("gpad", s)])
            if with_ctx:
                bk = C.bank()
                for k in range(8):
                    C.mm(C.banks[bk][:, 0:CT], wg[:, k, :], hsb[:, k, L:T], k == 0, k == 7, ["hsb", ("wg", s)], [("ps", bk)])
                C.cp("act", gctx[:, 1:CT + 1], C.banks[bk][:, 0:CT], [("ps", bk)], [("gctx", s)])
            tiles = [(i * 512, 512) for i in range(4)] + ([(L, CT)] if with_ctx else [])
            for (t0, tn) in tiles:
                bk = C.bank()
                for k in range(8):
                    C.mm(C.banks[bk][:, 0:tn], wu[:, k, :], hsb[:, k, t0:t0 + tn], k == 0, k == 7, ["hsb", ("wu", s)], [("ps", bk)])
                C.cp("act", u[:, t0:t0 + tn], C.banks[bk][:, 0:tn], [("ps", bk)], [("u", s)])
            for tt_ in range(4):
                bk = C.bank()
                for tap in range(9):
                    i, j = tap // 3, tap % 3
                    C.mm(C.banks[bk][:, :], dg[:, tap, :], gpad[:, 8 * tt_ + i:8 * tt_ + i + 8, j:j + 64], tap == 0, tap == 8,
                         [("dg", s), ("gpad", s)], [("ps", bk)])
                C.act(sv[:, tt_ * 512:(tt_ + 1) * 512], C.banks[bk][:, :], AF.Silu, [("ps", bk)], [("sv", s)])
            if with_ctx:
                bk = C.bank()
                for j in range(3):
                    C.mm(C.banks[bk][:, 0:CT], dg[:, 3 + j, :], gctx[:, j:j + CT], j == 0, j == 2, [("dg", s), ("gctx", s)], [("ps", bk)])
                C.act(sv[:, L:T], C.banks[bk][:, 0:CT], AF.Silu, [("ps", bk)], [("sv", s)])
            C.tt("dve", gb[:, 0:ntok], sv[:, 0:ntok], u[:, 0:ntok], ALU.mult, [("sv", s), ("u", s)], [("gb", s)])
            C.ld(C.dram["GT"][b, c * 128:(c + 1) * 128, 0:ntok], gb[:, 0:ntok], [("gb", s)], [("G", b, c)])


def rot_weights(C, wr, w, hd_half, res_in, res_out):
    nblk = 128 // (2 * hd_half)
    for c in range(nblk):
        b0 = c * 2 * hd_half
        C.ts("pool", wr[:, :, b0:b0 + hd_half], w[:, :, b0 + hd_half:b0 + 2 * hd_half], -1.0, None, ALU.mult, None, [res_in], [res_out])
        C.cp("pool", wr[:, :, b0 + hd_half:b0 + 2 * hd_half], w[:, :, b0:b0 + hd_half], [res_in], [res_out])


def stage_odd(C):
    C.sb_reset()
    I = C.I
    hsb = C.sb("hsb", [128, 8, T], BF16)
    vsb = C.sb("vsb", [128, NT, D], BF16)
    cosT = C.sb("cosT", [128, L], F32)
    sinT = C.sb("sinT", [128, L], F32)
    C.ld(cosT[:], I["rope2_cos"], [], ["cosT"])
    C.ld(sinT[:], I["rope2_sin"], [], ["sinT"])
    ones_b = C.sb("ones_b", [128, 128], BF16)
    ones_f = C.sb("ones_f", [128, 128], F32)
    C.P.op("pool", lambda e: e.memset(ones_b[:], 1.0), [], ["ones_b"])
    C.P.op("pool", lambda e: e.memset(ones_f[:], 1.0), [], ["ones_f"])
    lpb = C.sb("lpb", [128, 4, 64], F32)
    C.ld(lpb[:].rearrange("p a b -> p (a b)"), I["o_lambda"][0].rearrange("a b -> (a b)").partition_broadcast(128), [], ["lpb"])
    lt = C.sb("lt", [128, 2, 64], F32)
    lam = C.sb("lam", [128, 4], F32)
    C.tt("dve", lt[:, 0, :], lpb[:, 0, :], lpb[:, 1, :], ALU.mult, ["lpb"], ["lt"])
    C.tt("dve", lt[:, 1, :], lpb[:, 2, :], lpb[:, 3, :], ALU.mult, ["lpb"], ["lt"])
    C.P.op("dve", lambda e: e.reduce_sum(out=lam[:, 0:2], in_=lt[:], axis=AX.X), ["lt"], ["lam"])
    C.act(lam[:, 0:2], lam[:, 0:2], AF.Exp, ["lam"], ["lam"])
    C.tt("dve", lam[:, 2:3], lam[:, 1:2], lam[:, 0:1], ALU.subtract, ["lam"], ["lam"])
    C.ts("dve", lam[:, 2:3], lam[:, 2:3], -LAMBDA_INIT, None, ALU.add, None, ["lam"], ["lam"])
    sw = C.sb("sw", [128, 1], F32)
    C.ld(sw[:], I["o_subln_w"][0].rearrange("(p o) -> p o", o=1), [], ["sw"], slow=True)
    C.ts("dve", sw[:], sw[:], 1.0 - LAMBDA_INIT, None, ALU.mult, None, ["sw"], ["sw"])
    wv_all = C.sb("wv_all", [128, 8, D], BF16)
    wqs = [C.sb("wq%d" % i, [128, 8, 128], BF16) for i in range(2)]
    wqr = [C.sb("wqr%d" % i, [128, 8, 128], BF16) for i in range(2)]
    wks = [C.sb("wk%d" % i, [128, 8, 128], BF16) for i in range(2)]
    wkr = [C.sb("wkr%d" % i, [128, 8, 128], BF16) for i in range(2)]
    qT = [C.sb("qT%d" % i, [128, L], BF16) for i in range(2)]
    kT = [C.sb("kT%d" % i, [128, T], BF16) for i in range(2)]
    t1 = [C.sb("t1_%d" % i, [128, 512], F32) for i in range(2)]
    t2 = [C.sb("t2_%d" % i, [128, 512], F32) for i in range(2)]
    pT = [C.sb("pT%d" % i, [128, 512], BF16) for i in range(6)]
    accs = [C.sb("acc%d" % i, [128, 512], F32) for i in range(2)]
    ep = [C.sb("ep%d" % i, [128, 512], F32) for i in range(5)]
    yb = [C.sb("yb%d" % i, [128, 512], BF16) for i in range(2)]
    wqkv = I["o_w_qkv"][0].rearrange("(k p) n -> p k n", p=128)
    npt = 0
    nrot = 0
    ny = 0
    for b in range(NB):
        hv = C.dram["HT"][b].rearrange("(k p) t -> p k t", p=128)
        for k in range(8):
            C.ld(hsb[:, k, :], hv[:, k, :], [("HT", b)], ["hsb"])
        for k0 in range(0, 8, 2):
            C.ld(wv_all[:, k0:k0 + 2, :], wqkv[:, k0:k0 + 2, 2 * D:3 * D], [], ["wv_all"], q="pool")
        for t in range(NT):
            for nn in range(2):
                bk = C.bank()
                for k in range(8):
                    C.mm(C.banks[bk][:, :], hsb[:, k, t * 128:(t + 1) * 128], wv_all[:, k, nn * 512:(nn + 1) * 512], k == 0, k == 7,
                         ["hsb", "wv_all"], [("ps", bk)])
                C.cp("act" if nn == 0 else "dve", vsb[:, t, nn * 512:(nn + 1) * 512], C.banks[bk][:, :], [("ps", bk)], ["vsb"])
        for hd in range(8):
            s = hd % 2
            C.ld(wqs[s][:], wqkv[:, :, hd * 128:(hd + 1) * 128], [], [("wq", s)], q="pool")
            C.ld(wks[s][:], wqkv[:, :, D + hd * 128:D + (hd + 1) * 128], [], [("wk", s)], q="pool")
            rot_weights(C, wqr[s], wqs[s], 32, ("wq", s), ("wqr", s))
            rot_weights(C, wkr[s], wks[s], 32, ("wk", s), ("wkr", s))
            for (w, wr, dst, dres, wres) in ((wqs[s], wqr[s], qT[s], ("qT", s), [("wq", s), ("wqr", s)]),
                                            (wks[s], wkr[s], kT[s], ("kT", s), [("wk", s), ("wkr", s)])):
                for tt_ in range(4):
                    t0 = tt_ * 512
                    b1, b2 = C.bank(), C.bank()
                    for k in range(8):
                        C.mm(C.banks[b1][:, :], w[:, k, :], hsb[:, k, t0:t0 + 512], k == 0, k == 7, ["hsb"] + wres, [("ps", b1)])
                    for k in range(8):
                        C.mm(C.banks[b2][:, :], wr[:, k, :], hsb[:, k, t0:t0 + 512], k == 0, k == 7, ["hsb"] + wres, [("ps", b2)])
                    r = nrot % 2
                    nrot += 1
                    C.tt("dve", t1[r][:], C.banks[b1][:, :], cosT[:, t0:t0 + 512], ALU.mult, [("ps", b1), "cosT"], [("t1", r)])
                    C.tt("dve", t2[r][:], C.banks[b2][:, :], sinT[:, t0:t0 + 512], ALU.mult, [("ps", b2), "sinT"], [("t2", r)])
                    C.tt("dve", dst[:, t0:t0 + 512], t1[r][:], t2[r][:], ALU.add, [("t1", r), ("t2", r)], [dres])
            bk = C.bank()
            for k in range(8):
                C.mm(C.banks[bk][:, 0:CT], wks[s][:, k, :], hsb[:, k, L:T], k == 0, k == 7, ["hsb", ("wk", s)], [("ps", bk)])
            C.cp("act", kT[s][:, L:T], C.banks[bk][:, 0:CT], [("ps", bk)], [("kT", s)])
            for qb in range(4):
                po = [0, 1]
                pss = [6, 7]

                def emit_S(kt, cs=(0, 1)):
                    for c in cs:
                        bs = 2 + (kt % 2) * 2 + c
                        C.mm(C.banks[bs][:, :], kT[s][c * 64:(c + 1) * 64, kt * 128:(kt + 1) * 128],
                             qT[s][c * 64:(c + 1) * 64, qb * 512:(qb + 1) * 512], True, True, [("kT", s), ("qT", s)], [("ps", bs)])

                emit_S(0, (0,))
                C.mm(C.banks[7][0:8, 0:8], ones_f[:, 0:8], ones_f[:, 0:8], True, True, ["ones_f"], [("ps", 7)])
                emit_S(0, (1,))
                for kt in range(NT):
                    pis = []
                    for c in range(2):
                        bs = 2 + (kt % 2) * 2 + c
                        pi = npt % 6
                        npt += 1
                        pis.append(pi)
                        C.act(pT[pi][:], C.banks[bs][:, :], AF.Exp, [("ps", bs)], [("pT", pi)], scale=0.125)
                    for c in range(2):
                        pi = pis[c]
                        if kt + 1 < NT:
                            emit_S(kt + 1, (c,))
                        C.mm(C.banks[po[c]][:, :], vsb[:, kt, hd * 128:(hd + 1) * 128], pT[pi][:], kt == 0, kt == NT - 1,
                             ["vsb", ("pT", pi)], [("ps", po[c])])
                    for c in range(2):
                        pi = pis[c]
                        if kt == 0:
                            C.cp("dve", accs[c][:], pT[pi][:], [("pT", pi)], [("acc", c)])
                        else:
                            C.tt("dve", accs[c][:], accs[c][:], pT[pi][:], ALU.add, [("pT", pi), ("acc", c)], [("acc", c)])
                for c in range(2):
                    C.mm(C.banks[pss[c]][:, :], ones_f[:], accs[c][:], True, True, ["ones_f", ("acc", c)], [("ps", pss[c])])
                e0, e1, e2, e3, e4 = ep
                C.P.op("dve", (lambda o, i: (lambda e: e.reciprocal(out=o, in_=i)))(e0[:], C.banks[pss[0]][:, :]), [("ps", pss[0])], ["e0"])
                C.tt("dve", e1[:], C.banks[po[0]][:, :], e0[:], ALU.mult, [("ps", po[0]), "e0"], ["e1"])
                C.P.op("dve", (lambda o, i: (lambda e: e.reciprocal(out=o, in_=i)))(e2[:], C.banks[pss[1]][:, :]), [("ps", pss[1])], ["e2"])
                C.tt("dve", e3[:], C.banks[po[1]][:, :], e2[:], ALU.mult, [("ps", po[1]), "e2"], ["e3"])
                C.stt("dve", e1[:], e3[:], lam[:, 2:3], e1[:], ALU.mult, ALU.add, ["e3", "e1", "lam"], ["e1"])
                C.act(e4[:], e1[:], AF.Square, ["e1"], ["e4"])
                bq = 2
                C.mm(C.banks[bq][:, :], ones_f[:], e4[:], True, True, ["ones_f", "e4"], [("ps", bq)])
                C.ts("dve", e0[:], C.banks[bq][:, :], 1.0 / 128.0, NORM_EPS, ALU.mult, ALU.add, [("ps", bq)], ["e0"])
                C.act(e0[:], e0[:], AF.Ln, ["e0"], ["e0"])
                C.act(e0[:], e0[:], AF.Exp, ["e0"], ["e0"], scale=-0.5)
                yi = ny % 2
                ny += 1
                C.stt("dve", yb[yi][:], e1[:], sw[:, 0:1], e0[:], ALU.mult, ALU.mult, ["e1", "e0", "sw"], [("yb", yi)])
                C.ld(C.dram["YT"][b, hd * 128:(hd + 1) * 128, qb * 512:(qb + 1) * 512], yb[yi][:], [("yb", yi)], [("Yw", b, hd, qb)])


NCH = T // 64
DK_SCALE = 128.0 ** -0.5


import os


def stage_even(C):
    EV_NB = int(os.environ.get('EV_NB', NB))
    EV_STOP = int(os.environ.get('EV_STOP', 99))
    EV_SUB = int(os.environ.get('EV_SUB', 99))
    EV_HEADS = [int(x) for x in os.environ.get('EV_HEADS', '0,1,2,3,4,5,6,7').split(',')]
    C.sb_reset()
    I = C.I
    P = C.P
    mk = C.sb("mk", [64, 8, 64], F32)
    for i in range(7):
        C.ld(mk[:, i, :], I["masks"][i], [], ["mk"])
    U = [mk[:, 0, :], mk[:, 1, :]]
    POS = [mk[:, 2, :], mk[:, 3, :]]
    STRICT = [mk[:, 4, :], mk[:, 5, :]]
    I64 = mk[:, 6, :]
    ident = C.sb("ident", [128, 128], F32)
    C.ld(ident[:], I["ident"], [], ["ident"])
    ones64 = C.sb("ones64", [64, 128], F32)
    ones128 = C.sb("ones128", [128, 128], F32)
    P.op("pool", lambda e: e.memset(ones64[:], 1.0), [], ["ones64"])
    P.op("pool", lambda e: e.memset(ones128[:], 1.0), [], ["ones128"])
    cosT = C.sb("cosT", [128, L], F32)
    sinT = C.sb("sinT", [128, L], F32)
    C.ld(cosT[:], I["rope1_cos"], [], ["cosT"])
    C.ld(sinT[:], I["rope1_sin"], [], ["sinT"])
    cwe = C.sb("cwe", [128, 12, 3], F32)
    for tap in range(3):
        for g0 in range(0, 12, 4):
            C.ld(cwe[:, g0:g0 + 4, tap], I["e_conv"][0, tap].rearrange("(g p) -> p g", p=128)[:, g0:g0 + 4], [], ["cwe"], slow=True)
    nw = C.sb("nw", [128, 1], F32)
    C.ld(nw[:], I["e_norm_w"][0].rearrange("(p o) -> p o", o=1), [], ["nw"], slow=True)
    sc8 = C.sb("sc8", [8, 4], F32)
    C.ld(sc8[:, 0:1], I["e_a_log"][0].rearrange("d (h o) -> (d h) o", o=1), [], ["sc8"], slow=True)
    C.ld(sc8[:, 1:2], I["e_dt_bias"][0].rearrange("d (h o) -> (d h) o", o=1), [], ["sc8"], slow=True)
    C.ld(sc8[:, 2:3], I["e_ret_decay"][0].rearrange("d (h o) -> (d h) o", o=1), [], ["sc8"], slow=True)
    C.act(sc8[:, 0:1], sc8[:, 0:1], AF.Exp, ["sc8"], ["sc8"])
    C.ts("dve", sc8[:, 0:1], sc8[:, 0:1], -1.0, None, ALU.mult, None, ["sc8"], ["sc8"])
    C.act(sc8[:, 2:3], sc8[:, 2:3], AF.Exp, ["sc8"], ["sc8"])
    C.ts("dve", sc8[:, 2:3], sc8[:, 2:3], -1.0, None, ALU.mult, None, ["sc8"], ["sc8"])

    hsb_off = C.sb_off
    hsb = C.sb("hsb", [128, 8, T], BF16)
    wab = C.sb("wab", [128, 8, 16], BF16)
    wts = [C.sb("wt%d" % i, [128, 8, 128], BF16) for i in range(4)]
    wrs = [C.sb("wr%d" % i, [128, 8, 128], BF16) for i in range(2)]
    raw = C.sb("raw", [128, T], F32)
    qT = C.sb("qT", [128, T], F32)
    kT = C.sb("kT", [128, T], F32)
    vT = C.sb("vT", [128, T], F32)
    szT = C.sb("szT", [128, T], BF16)
    tmp = C.sb("tmp", [128, 512], F32)
    tmp2 = C.sb("tmp2", [128, 512], F32)
    ktok = C.sb("ktok", [64, NCH, 128], F32)
    vtok = C.sb("vtok", [64, NCH, 128], F32)
    oacc_off = C.sb_off
    oacc = C.sb("oacc", [64, NCH, 128], F32)
    ab = C.sb_at("ab", [8, T], F32, oacc_off)
    ab2 = C.sb_at("ab2", [8, T], F32, oacc_off + T * 4)
    TTs = C.sb_at("TTs", [64, 2 * NCH, 64], F32, hsb_off)
    QKTs = C.sb_at("QKTs", [64, 2 * NCH, 64], F32, hsb_off + 2 * NCH * 64 * 4)
    yT = C.sb("yT", [128, T], BF16)
    la = [C.sb("la%d" % d, [64, NCH], F32) for d in range(2)]
    bet = [C.sb("bet%d" % d, [64, NCH], F32) for d in range(2)]
    G = [C.sb("G%d" % d, [64, NCH], F32) for d in range(2)]
    EG = [C.sb("EG%d" % d, [64, NCH], F32) for d in range(2)]
    EKG = [C.sb("EKG%d" % d, [64, NCH], F32) for d in range(2)]
    NBEG = [C.sb("NBEG%d" % d, [64, NCH], F32) for d in range(2)]
    CD = [C.sb("CD%d" % d, [128, NCH], F32) for d in range(2)]
    S = [C.sb("S%d" % d, [128, 128], F32) for d in range(2)]
    st = C.sb("st", [64, 4 * NCH], F32)
    lat = C.sb("lat", [NCH, 128], F32)
    NSM = 72
    sm = [C.sb("sm%d" % i, [64, 64], F32) for i in range(NSM)]
    NPM = 24
    Pm = [C.sb("Pm%d" % i, [64, 64], F32) for i in range(NPM)]
    NVB = 8
    vb = [C.sb("vb%d" % i, [64, 128], F32) for i in range(NVB)]
    smi = [0]

    tb = [0, 0]
    psA = {"banks": [2, 3, 4, 5], "bi": -1, "j": 8, "round": -1}
    psB = {"banks": [6, 7], "bi": -1, "j": 8, "round": -1}
    rnd = [0]

    def _pslot(st):
        if st["round"] != rnd[0] or st["j"] >= 8:
            st["round"] = rnd[0]
            st["bi"] = (st["bi"] + 1) % len(st["banks"])
            st["j"] = 0
        bk, j = st["banks"][st["bi"]], st["j"]
        st["j"] += 1
        return C.banks[bk][0:64, j * 64:(j + 1) * 64], ("ps", bk)

    def pslot():
        return _pslot(psA)

    def pslotB():
        return _pslot(psB)

    def pmat():
        i = tb[1] % NPM
        tb[1] += 1
        return Pm[i][:], ("Pm", i)

    def small():
        i = smi[0] % NSM
        smi[0] += 1
        return sm[i][:], ("sm", i)

    win = I["e_w_in"][0].rearrange("(k p) n -> p k n", p=128)
    LAD = C.dram["LAD"]

    def proj(w, t0, tn, wres):
        bk = C.bank()
        for k in range(8):
            C.mm(C.banks[bk][:, 0:tn], w[:, k, :], hsb[:, k, t0:t0 + tn], k == 0, k == 7, ["hsb"] + wres, [("ps", bk)])
        return bk

    tiles = [(i * 512, 512) for i in range(4)] + [(L, CT)]

    for b in range(EV_NB):
        hv = C.dram["HT"][b].rearrange("(k p) t -> p k t", p=128)
        P.barrier()
        for k in range(8):
            C.ld(hsb[:, k, :], hv[:, k, :], [("HT", b)], ["hsb"])
        C.ld(wab[:], win[:, :, 2048:2064], [], ["wab"], q="pool")
        for (t0, tn) in tiles:
            bk = C.bank()
            for k in range(8):
                C.mm(C.banks[bk][0:8, 0:tn], wab[:, k, 0:8], hsb[:, k, t0:t0 + tn], k == 0, k == 7, ["hsb", "wab"], [("ps", bk)])
            C.act(ab[:, t0:t0 + tn], C.banks[bk][0:8, 0:tn], AF.Exp, [("ps", bk), "sc8"], ["ab"], bias=sc8[:, 1:2])
            bk = C.bank()
            for k in range(8):
                C.mm(C.banks[bk][0:8, 0:tn], wab[:, k, 8:16], hsb[:, k, t0:t0 + tn], k == 0, k == 7, ["hsb", "wab"], [("ps", bk)])
            C.act(ab2[:, t0:t0 + tn], C.banks[bk][0:8, 0:tn], AF.Sigmoid, [("ps", bk)], ["ab2"])
        C.ts("dve", ab[:], ab[:], 1.0, None, ALU.add, None, ["ab"], ["ab"])
        C.act(ab[:], ab[:], AF.Ln, ["ab"], ["ab"])
        C.ts("dve", ab[:], ab[:], sc8[:, 0:1], None, ALU.mult, None, ["ab", "sc8"], ["ab"])
        C.ld(LAD[b, 0:8, :], ab[:], ["ab"], [("LAD", b, 0)])
        C.ld(LAD[b, 8:16, :], ab2[:], ["ab2"], [("LAD", b, 1)])
        C.ts("dve", ab2[:], ab2[:], 0.0, None, ALU.mult, None, ["ab2"], ["ab2"])
        C.ts("dve", ab2[:], ab2[:], sc8[:, 2:3], None, ALU.add, None, ["ab2", "sc8"], ["ab2"])
        C.ld(LAD[b, 16:24, :], ab2[:], ["ab2"], [("LAD", b, 2)])
        ladres = [("LAD", b, 0), ("LAD", b, 1), ("LAD", b, 2)]
        if EV_STOP <= 1:
            return

        for hd in EV_HEADS:
            P.barrier()
            if hd != EV_HEADS[0]:
                for k in range(8):
                    C.ld(hsb[:, k, :], hv[:, k, :], [("HT", b)], ["hsb"])
            delta = hd < 4
            h4 = hd % 4
            if delta:
                cols = [h4 * 128, 512 + h4 * 128, 1024 + h4 * 128, 1536 + h4 * 128]
            else:
                cols = [2064 + h4 * 128, 2576 + h4 * 128, 3088 + h4 * 128, 3600 + h4 * 128]
            for i in range(4):
                C.ld(wts[i][:], win[:, :, cols[i]:cols[i] + 128], [], [("wt", i)], q="pool")
            dsts = [qT, kT, vT]
            dres = ["qT", "kT", "vT"]
            if delta:
                for i in range(3):
                    for (t0, tn) in tiles:
                        bk = proj(wts[i], t0, tn, [("wt", i)])
                        C.cp("act", raw[:, t0:t0 + tn], C.banks[bk][:, 0:tn], [("ps", bk)], ["raw"])
                    gidx = i * 4 + h4
                    d_ = dsts[i]
                    for (s0, sn) in ((0, L), (L, CT)):
                        C.ts("dve", d_[:, s0:s0 + sn], raw[:, s0:s0 + sn], cwe[:, gidx, 1:2], None, ALU.mult, None, ["raw", "cwe"], [dres[i]])
                        C.stt("dve", d_[:, s0 + 1:s0 + sn], raw[:, s0:s0 + sn - 1], cwe[:, gidx, 0:1], d_[:, s0 + 1:s0 + sn], ALU.mult, ALU.add,
                              ["raw", "cwe", dres[i]], [dres[i]])
                        C.stt("dve", d_[:, s0:s0 + sn - 1], raw[:, s0 + 1:s0 + sn], cwe[:, gidx, 2:3], d_[:, s0:s0 + sn - 1], ALU.mult, ALU.add,
                              ["raw", "cwe", dres[i]], [dres[i]])
                    C.act(d_[:], d_[:], AF.Silu, [dres[i]], [dres[i]])
                    if i < 2:
                        for (t0, tn) in tiles:
                            C.act(tmp[:, 0:tn], d_[:, t0:t0 + tn], AF.Square, [dres[i]], ["tmp"])
                            bk = C.bank()
                            C.mm(C.banks[bk][:, 0:tn], ones128[:], tmp[:, 0:tn], True, True, ["ones128", "tmp"], [("ps", bk)])
                            C.ts("dve", tmp2[:, 0:tn], C.banks[bk][:, 0:tn], NORM_EPS, None, ALU.add, None, [("ps", bk)], ["tmp2"])
                            C.act(tmp2[:, 0:tn], tmp2[:, 0:tn], AF.Sqrt, ["tmp2"], ["tmp2"])
                            P.op("dve", (lambda o: (lambda e: e.reciprocal(out=o, in_=o)))(tmp2[:, 0:tn]), ["tmp2"], ["tmp2"])
                            C.stt("dve", d_[:, t0:t0 + tn], d_[:, t0:t0 + tn], DK_SCALE if i == 0 else 1.0, tmp2[:, 0:tn], ALU.mult, ALU.mult,
                                  [dres[i], "tmp2"], [dres[i]])
            else:
                for i in range(2):
                    rot_weights(C, wrs[i], wts[i], 64, ("wt", i), ("wr", i))
                    d_ = dsts[i]
                    sc_ = 1.0 if i == 0 else DK_SCALE
                    for (t0, tn) in tiles[:4]:
                        b1 = proj(wts[i], t0, tn, [("wt", i)])
                        b2 = proj(wrs[i], t0, tn, [("wr", i)])
                        C.tt("dve", tmp[:], C.banks[b1][:, :], cosT[:, t0:t0 + 512], ALU.mult, [("ps", b1), "cosT"], ["tmp"])
                        C.tt("dve", tmp2[:], C.banks[b2][:, :], sinT[:, t0:t0 + 512], ALU.mult, [("ps", b2), "sinT"], ["tmp2"])
                        if i == 0:
                            C.tt("dve", d_[:, t0:t0 + 512], tmp[:], tmp2[:], ALU.add, ["tmp", "tmp2"], [dres[i]])
                        else:
                            C.tt("dve", tmp[:], tmp[:], tmp2[:], ALU.add, ["tmp", "tmp2"], ["tmp"])
                            C.act(d_[:, t0:t0 + 512], tmp[:], AF.Identity, ["tmp"], [dres[i]], scale=sc_)
                    bk = proj(wts[i], L, CT, [("wt", i)])
                    C.act(d_[:, L:T], C.banks[bk][:, 0:CT], AF.Identity, [("ps", bk)], [dres[i]], scale=sc_)
                for (t0, tn) in tiles:
                    bk = proj(wts[2], t0, tn, [("wt", 2)])
                    C.cp("act", vT[:, t0:t0 + tn], C.banks[bk][:, 0:tn], [("ps", bk)], ["vT"])
            for (t0, tn) in tiles:
                bk = proj(wts[3], t0, tn, [("wt", 3)])
                C.act(szT[:, t0:t0 + tn], C.banks[bk][:, 0:tn], AF.Silu, [("ps", bk)], ["szT"])
            if EV_STOP <= 2:
                return
            for (src, sres, dst, dr) in ((kT, "kT", ktok, "ktok"), (vT, "vT", vtok, "vtok")):
                for n0 in range(0, NCH, 4):
                    bk = C.bank()
                    for q in range(4):
                        n = n0 + q
                        C.tr(C.banks[bk][0:64, q * 128:(q + 1) * 128], src[:, n * 64:(n + 1) * 64], ident[:], [sres, "ident"], [("ps", bk)])
                    C.cp("act" if (n0 // 4) % 2 == 0 else "dve", dst[:, n0:n0 + 4, :].rearrange("p a b -> p (a b)"), C.banks[bk][0:64, :],
                         [("ps", bk)], [dr])
            if EV_STOP <= 3:
                return
            for d in range(2):
                row = (d * 4 + h4) if delta else (16 + d * 4 + h4)
                C.ld(lat[:, 0:64], LAD[b, row].rearrange("(n i) -> n i", i=64), ladres, ["lat"])
                if delta:
                    C.ld(lat[:, 64:128], LAD[b, 8 + d * 4 + h4].rearrange("(n i) -> n i", i=64), ladres, ["lat"])
                bkt = C.bank()
                C.tr(C.banks[bkt][0:64, 0:NCH], lat[:, 0:64], ident[0:NCH, 0:NCH], ["lat", "ident"], [("ps", bkt)])
                if delta:
                    C.tr(C.banks[bkt][0:64, 64:64 + NCH], lat[:, 64:128], ident[0:NCH, 0:NCH], ["lat", "ident"], [("ps", bkt)])
                C.cp("dve", la[d][:], C.banks[bkt][0:64, 0:NCH], [("ps", bkt)], [("la", d)])
                if delta:
                    C.cp("dve", bet[d][:], C.banks[bkt][0:64, 64:64 + NCH], [("ps", bkt)], [("bet", d)])
                bk = C.bank()
                C.mm(C.banks[bk][0:64, 0:NCH], U[d], la[d][:], True, True, ["mk", ("la", d)], [("ps", bk)])
                C.cp("dve", G[d][:], C.banks[bk][0:64, 0:NCH], [("ps", bk)], [("G", d)])
                C.act(EG[d][:], C.banks[bk][0:64, 0:NCH], AF.Exp, [("ps", bk)], [("EG", d)])
                bk2 = C.bank()
                C.mm(C.banks[bk2][:, 0:NCH], ones64[:], la[d][:], True, True, ["ones64", ("la", d)], [("ps", bk2)])
                C.act(CD[d][:], C.banks[bk2][:, 0:NCH], AF.Exp, [("ps", bk2)], [("CD", d)])
                C.tt("dve", EKG[d][:], C.banks[bk2][0:64, 0:NCH], G[d][:], ALU.subtract, [("ps", bk2), ("G", d)], [("EKG", d)])
                C.act(EKG[d][:], EKG[d][:], AF.Exp, [("EKG", d)], [("EKG", d)])
                if delta:
                    C.stt("dve", NBEG[d][:], bet[d][:], -1.0, EG[d][:], ALU.mult, ALU.mult, [("bet", d), ("EG", d)], [("NBEG", d)])
            if EV_STOP <= 4:
                return
            P.barrier()
            for n0 in range(0, NCH, 4):
                P.op("pool", (lambda o: (lambda e: e.memset(o, 0.0)))(oacc[:, n0:n0 + 4, :].rearrange("p a b -> p (a b)")), [],
                     [("oacc", n) for n in range(n0, n0 + 4)])
            def chain(n, d, kk_ap, kk_r, qk_ap, qk_r):
                labc, labr = small()
                C.act(labc, ones64[:, 0:64], AF.Identity, ["ones64", ("la", d)], [labr], scale=la[d][:, n:n + 1])
                yield
                pg, pgr = pslot()
                C.mm(pg, labc, U[d], True, False, [labr, "mk"], [pgr])
                C.mm(pg, I64, POS[d], False, True, ["mk"], [pgr])
                yield
                dec, decr = small()
                C.act(dec, pg, AF.Exp, [pgr, ("G", d)], [decr], bias=G[d][:, n:n + 1], scale=-1.0)
                yield
                qk, qkr = small()
                C.tt("dve", qk, qk_ap, dec, ALU.mult, [qk_r, decr], [qkr])
                if delta:
                    decs, decsr = small()
                    C.tt("pool", decs, dec, STRICT[d], ALU.mult, [decr, "mk"], [decsr])
                yield
                pt, ptr = pslot()
                C.tr(pt, qk, I64, [qkr, "mk"], [ptr])
                if delta:
                    Y, Yr = small()
                    C.stt("dve", Y, kk_ap, bet[d][:, n:n + 1], decs, ALU.mult, ALU.mult, [kk_r, ("bet", d), decsr], [Yr])
                yield
                C.cp("act", QKTs[:, 2 * n + d, :], pt, [ptr], [("QKT", n, d)])
                if not delta:
                    return
                px, pxr = pslot()
                C.tr(px, Y, I64, [Yr, "mk"], [pxr])
                yield
                X, Xr = small()
                C.cp("act", X, px, [pxr], [Xr])
                yield
                Pc, Pr = pmat()
                C.tt("pool", Pc, I64, X, ALU.subtract, ["mk", Xr], [Pr])
                yield
                for k in range(5):
                    py, pyr = pslot()
                    C.mm(py, X, Y, True, True, [Xr, Yr], [pyr])
                    if k < 4:
                        px2, px2r = pslotB()
                        C.mm(px2, Y, X, True, True, [Xr, Yr], [px2r])
                    yield
                    Y2, Y2r = small()
                    C.cp("act", Y2, py, [pyr], [Y2r])
                    if k < 4:
                        X2, X2r = small()
                        C.cp("dve", X2, px2, [px2r], [X2r])
                    yield
                    pp, ppr = pslotB()
                    C.mm(pp, Y2, Pc, True, True, [Y2r, Pr], [ppr])
                    yield
                    if k < 4:
                        Pn, Pnr = pmat()
                    else:
                        Pn, Pnr = TTs[:, 2 * n + d, :], ("TT", n, d)
                    C.tt("dve", Pn, pp, Pc, ALU.add, [Pr, ppr], [Pnr])
                    Pc, Pr = Pn, Pnr
                    Y, Yr = Y2, Y2r
                    if k < 4:
                        X, Xr = X2, X2r
                    yield

            GC = int(os.environ.get("EV_GC", 4))
            for gi, n0 in enumerate(range(0, NCH, GC)):
                gens = []
                for q in range(GC):
                    n = n0 + q
                    ksl = kT[:, n * 64:(n + 1) * 64]
                    qsl = qT[:, n * 64:(n + 1) * 64]
                    bkq = gi % 2
                    kk_ap, kk_r = C.banks[bkq][0:64, q * 64:(q + 1) * 64], ("ps", bkq)
                    qk_ap, qk_r = C.banks[bkq][0:64, (4 + q) * 64:(5 + q) * 64], ("ps", bkq)
                    if delta:
                        C.mm(kk_ap, ksl, ksl, True, True, ["kT"], [kk_r])
                    C.mm(qk_ap, qsl, ksl, True, True, ["kT", "qT"], [qk_r])
                    for d in range(2):
                        gens.append(chain(n, d, kk_ap, kk_r, qk_ap, qk_r))
                lev = 0
                while gens:
                    nxt = []
                    lev += 1
                    rnd[0] += 1
                    if lev > int(os.environ.get("EV_LEV", 999)):
                        break
                    for g in gens:
                        try:
                            next(g)
                            nxt.append(g)
                        except StopIteration:
                            pass
                    gens = nxt
            if EV_STOP <= 5:
                return
            P.barrier()
            for d in range(2):
                P.op("pool", (lambda o: (lambda e: e.memset(o, 0.0)))(S[d][:]), [], [("S", d)])
            order = [list(range(32, 36)) + list(range(0, 32)), list(range(35, 31, -1)) + list(range(31, -1, -1))]
            nvbc = [0]

            def scan_step(step, d):
                n = order[d][step]
                ksl = kT[:, n * 64:(n + 1) * 64]
                qsl = qT[:, n * 64:(n + 1) * 64]
                Sd, Sr = S[d][:], ("S", d)
                vn = vb[nvbc[0] % NVB]
                vnr = ("vb", nvbc[0] % NVB)
                nvbc[0] += 1
                vs = vb[nvbc[0] % NVB]
                vsr = ("vb", nvbc[0] % NVB)
                nvbc[0] += 1
                ores = ("oacc", n)
                if delta:
                    b1 = C.bank()
                    C.mm(C.banks[b1][0:64, 0:128], ksl, Sd, True, True, ["kT", Sr], [("ps", b1)])
                    C.act(vs[:], vtok[:, n, :], AF.Identity, ["vtok", ("bet", d)], [vsr], scale=bet[d][:, n:n + 1])
                else:
                    C.act(vs[:], vtok[:, n, :], AF.Identity, ["vtok", ("EKG", d)], [vsr], scale=EKG[d][:, n:n + 1])
                b3 = C.bank()
                C.mm(C.banks[b3][0:64, 0:128], qsl, Sd, True, True, ["qT", Sr], [("ps", b3)])
                yield
                if delta:
                    C.stt("dve", vs[:], C.banks[b1][0:64, 0:128], NBEG[d][:, n:n + 1], vs[:], ALU.mult, ALU.add,
                          [("ps", b1), ("NBEG", d), vsr], [vsr])
                C.stt("dve", oacc[:, n, :], C.banks[b3][0:64, 0:128], EG[d][:, n:n + 1], oacc[:, n, :], ALU.mult, ALU.add,
                      [("ps", b3), ("EG", d), ores], [ores])
                C.act(Sd, Sd, AF.Identity, [Sr, ("CD", d)], [Sr], scale=CD[d][:, n:n + 1])
                yield
                if delta:
                    b2 = C.bank()
                    C.mm(C.banks[b2][0:64, 0:128], TTs[:, 2 * n + d, :], vs[:], True, True, [("TT", n, d), vsr], [("ps", b2)])
                    yield
                    C.cp("act", vn[:], C.banks[b2][0:64, 0:128], [("ps", b2)], [vnr])
                    yield
                    C.ts("dve", vs[:], C.banks[b2][0:64, 0:128], EKG[d][:, n:n + 1], None, ALU.mult, None, [("ps", b2), ("EKG", d)], [vsr])
                    yield
                    vnap = vn[:]
                else:
                    vnap = vtok[:, n, :]
                    vnr = "vtok"
                b4 = C.bank()
                C.mm(C.banks[b4][0:64, 0:128], QKTs[:, 2 * n + d, :], vnap, True, True, [("QKT", n, d), vnr], [("ps", b4)])
                b5 = C.bank()
                C.mm(C.banks[b5][:, 0:128], ktok[:, n, :], vs[:], True, True, ["ktok", vsr], [("ps", b5)])
                yield
                C.tt("dve", oacc[:, n, :], C.banks[b4][0:64, 0:128], oacc[:, n, :], ALU.add, [ores, ("ps", b4)], [ores])
                C.tt("dve", Sd, C.banks[b5][:, 0:128], Sd, ALU.add, [Sr, ("ps", b5)], [Sr])

            for step in range(NCH):
                gens = [scan_step(step, 0), scan_step(step, 1)]
                while gens:
                    nxt = []
                    for g in gens:
                        try:
                            next(g)
                            nxt.append(g)
                        except StopIteration:
                            pass
                    gens = nxt
            if EV_STOP <= 6:
                return
            ores_all = [("oacc", n) for n in range(NCH)]
            for n0 in range(0, NCH, 4):
                sl = oacc[:, n0:n0 + 4, :].rearrange("p a b -> p (a b)")
                C.act(tmp[0:64, :], sl, AF.Square, ores_all[n0:n0 + 4], ["tmp"])
                P.op("dve", (lambda o, i: (lambda e: e.reduce_sum(out=o, in_=i, axis=AX.X)))(
                    st[:, n0:n0 + 4], tmp[0:64, :].rearrange("p (a b) -> p a b", b=128)), ["tmp"], ["st"])
                P.op("dve", (lambda o, i: (lambda e: e.reduce_sum(out=o, in_=i, axis=AX.X)))(
                    st[:, NCH + n0:NCH + n0 + 4], oacc[:, n0:n0 + 4, :]), ores_all[n0:n0 + 4], ["st"])
            ssq = st[:, 0:NCH]
            ssum = st[:, NCH:2 * NCH]
            rstd = st[:, 2 * NCH:3 * NCH]
            mean = st[:, 3 * NCH:4 * NCH]
            C.ts("dve", mean, ssum, 1.0 / 128.0, None, ALU.mult, None, ["st"], ["st"])
            if delta:
                C.ts("dve", rstd, ssq, 1.0 / 128.0, NORM_EPS, ALU.mult, ALU.add, ["st"], ["st"])
            else:
                C.tt("dve", rstd, mean, mean, ALU.mult, ["st"], ["st"])
                C.stt("dve", rstd, ssq, 1.0 / 128.0, rstd, ALU.mult, ALU.subtract, ["st"], ["st"])
                C.ts("dve", rstd, rstd, NORM_EPS, None, ALU.add, None, ["st"], ["st"])
            C.act(rstd, rstd, AF.Sqrt, ["st"], ["st"])
            P.op("dve", (lambda o: (lambda e: e.reciprocal(out=o, in_=o)))(rstd), ["st"], ["st"])
            for n in range(NCH):
                if delta:
                    C.ts("dve", oacc[:, n, :], oacc[:, n, :], rstd[:, n:n + 1], None, ALU.mult, None,
                         [("oacc", n), "st"], [("oacc", n)])
                else:
                    C.ts("dve", oacc[:, n, :], oacc[:, n, :], mean[:, n:n + 1], rstd[:, n:n + 1], ALU.subtract, ALU.mult,
                         [("oacc", n), "st"], [("oacc", n)])
            for n0 in range(0, NCH, 8):
                nn = min(8, NCH - n0)
                bk = C.bank()
                for q in range(nn):
                    C.tr(C.banks[bk][:, q * 64:(q + 1) * 64], oacc[:, n0 + q, :], I64, [("oacc", n0 + q), "mk"], [("ps", bk)])
                t0 = n0 * 64
                tn = nn * 64
                if delta:
                    C.stt("dve", yT[:, t0:t0 + tn], C.banks[bk][:, 0:tn], nw[:, 0:1], szT[:, t0:t0 + tn], ALU.mult, ALU.mult,
                          [("ps", bk), "nw", "szT"], ["yT"])
                else:
                    C.tt("dve", yT[:, t0:t0 + tn], C.banks[bk][:, 0:tn], szT[:, t0:t0 + tn], ALU.mult, [("ps", bk), "szT"], ["yT"])
            C.ld(C.dram["YT"][b, hd * 128:(hd + 1) * 128, :], yT[:], ["yT"], [("Yw", b, hd)])


def build(stages, dbg=()):
    nc = bass.Bass("TRN2", target_bir_lowering=False)
    C = Ctx(nc)
    I = {}

    def inp(name, shape):
        I[name] = nc.dram_tensor(name, list(shape), F32, kind="ExternalInput").ap()

    inp("x", (NB, L, D)); inp("ctx", (NB, CT, D)); inp("cvec", (3, D))
    inp("mod_w", (2, D, 6 * D)); inp("mod_b", (2, 6 * D)); inp("ln_g", (2, 2, D)); inp("ln_b", (2, 2, D))
    inp("e_w_in", (1, D, EVEN_IN)); inp("e_conv", (1, 3, 1536)); inp("e_a_log", (1, 2, 4)); inp("e_dt_bias", (1, 2, 4))
    inp("e_norm_w", (1, 128)); inp("e_ret_decay", (1, 2, 4)); inp("e_w_out", (1, D, D))
    inp("o_w_qkv", (1, D, 3 * D)); inp("o_lambda", (1, 4, 64)); inp("o_subln_w", (1, 128)); inp("o_w_out", (1, D, D))
    inp("f_w_gate", (2, D, DFF)); inp("f_w_up", (2, D, DFF)); inp("f_conv", (2, 3, 3, DFF)); inp("f_w_down", (2, DFF, D))
    inp("ident", (128, 128)); inp("rope2_cos", (128, L)); inp("rope2_sin", (128, L))
    inp("rope1_cos", (128, L)); inp("rope1_sin", (128, L)); inp("masks", (8, 64, 64))
    I["out"] = nc.dram_tensor("out", [NB, L, D], F32, kind="ExternalOutput").ap()
    C.I = I
    def kind_of(nm):
        if nm in dbg:
            return "ExternalOutput"
        if nm + "in" in dbg:
            return "ExternalInput"
        return "Internal"

    C.dram["modv"] = nc.dram_tensor("modv", [2, 3, 6 * D], F32, kind=kind_of("modv")).ap()
    for nm in ("XA", "XB"):
        C.dram[nm] = nc.dram_tensor(nm, [NB, T, D], F32, kind=kind_of(nm)).ap()
    C.dram["HT"] = nc.dram_tensor("HT", [NB, D, T], BF16, kind=kind_of("HT")).ap()
    C.dram["YT"] = nc.dram_tensor("YT", [NB, D, T], BF16, kind=kind_of("YT")).ap()
    C.dram["GT"] = nc.dram_tensor("GT", [NB, DFF, T], BF16, kind=kind_of("GT")).ap()
    C.dram["LAD"] = nc.dram_tensor("LAD", [NB, 24, T], F32, kind=kind_of("LAD")).ap()
    S = stages
    if "mod" in S:
        stage_mod(C)
    if "modT0" in S:
        stage_modT0(C)
    if "even" in S:
        stage_even(C)
    if "pa0" in S:
        stage_projln(C, 0, "a", C.dram["YT"], 8, I["e_w_out"][0], "input", "XA", NT, (0, "f"))
    if "ffn0" in S:
        stage_ffn1(C, 0, True)
    if "pf0" in S:
        stage_projln(C, 0, "f", C.dram["GT"], 22, I["f_w_down"][0], "XA", "XB", NT, (1, "a"))
    if "odd" in S:
        stage_odd(C)
    if "pa1" in S:
        stage_projln(C, 1, "a", C.dram["YT"], 8, I["o_w_out"][0], "XB", "XA", 16, (1, "f"))
    if "ffn1" in S:
        stage_ffn1(C, 1, False)
    if "pf1" in S:
        stage_projln(C, 1, "f", C.dram["GT"], 22, I["f_w_down"][1], "XA", "out", 16, None)
    C.P.emit()
    return nc


def make_consts():
    c = {}
    c["ident"] = np.eye(128, dtype=np.float32)
    pos = np.arange(L)
    inv2 = 10000.0 ** (-np.arange(16, dtype=np.float64) / 16)
    ang2 = np.concatenate([(pos // 64)[:, None] * inv2[None], (pos % 64)[:, None] * inv2[None]], -1)
    idx2 = (np.arange(128) % 64) % 32
    c["rope2_cos"] = np.ascontiguousarray(np.cos(ang2)[:, idx2].T.astype(np.float32))
    c["rope2_sin"] = np.ascontiguousarray(np.sin(ang2)[:, idx2].T.astype(np.float32))
    inv1 = 10000.0 ** (-np.arange(64, dtype=np.float64) / 64)
    ang1 = pos[:, None] * inv1[None]
    idx1 = np.arange(128) % 64
    c["rope1_cos"] = np.ascontiguousarray(np.cos(ang1)[:, idx1].T.astype(np.float32))
    c["rope1_sin"] = np.ascontiguousarray(np.sin(ang1)[:, idx1].T.astype(np.float32))
    c["masks"] = make_masks()
    return c


def make_masks():
    m = np.zeros((8, 64, 64), np.float32)
    a = np.arange(64)
    le = (a[:, None] <= a[None, :]).astype(np.float32)
    m[0] = le
    m[1] = le.T
    m[2] = np.where(a[None, :] <= a[:, None], 0.0, -NEG)
    m[3] = np.where(a[None, :] >= a[:, None], 0.0, -NEG)
    m[4] = (a[None, :] < a[:, None]).astype(np.float32)
    m[5] = (a[None, :] > a[:, None]).astype(np.float32)
    m[6] = np.eye(64, dtype=np.float32)
    m[7] = 1.0
    return m


ALL_STAGES = ["mod", "modT0", "even", "pa0", "ffn0", "pf0", "odd", "pa1", "ffn1", "pf1"]
_NC_CACHE = {}


def kernel(**inputs):
    n = 8
    if "nc" not in _NC_CACHE:
        _NC_CACHE["nc"] = build(ALL_STAGES)
    nc = _NC_CACHE["nc"]
    consts = make_consts()
    shared = {k: np.ascontiguousarray(np.asarray(v, dtype=np.float32)) for k, v in inputs.items()
              if k not in ("x", "c", "ctx", "c_ctx")}
    x = np.asarray(inputs["x"], dtype=np.float32)
    ctx = np.asarray(inputs["ctx"], dtype=np.float32)
    c = np.asarray(inputs["c"], dtype=np.float32)
    c_ctx = np.asarray(inputs["c_ctx"], dtype=np.float32)
    in_maps = []
    for i in range(n):
        m = dict(shared)
        m.update(consts)
        m["x"] = np.ascontiguousarray(x[NB * i:NB * (i + 1)])
        m["ctx"] = np.ascontiguousarray(ctx[NB * i:NB * (i + 1)])
        m["cvec"] = np.ascontiguousarray(np.stack([c[NB * i], c[NB * i + 1], c_ctx]))
        in_maps.append(m)
    res = run_bass_kernel_spmd(nc, in_maps, core_ids=list(range(n)))
    return np.concatenate([r["out"] for r in res.results], axis=0).astype(np.float32)
```

```python
import math
from contextlib import ExitStack

import numpy as np
import concourse.bass as bass
import concourse.mybir as mybir
from concourse.bass_utils import run_bass_kernel_spmd

F32 = mybir.dt.float32
BF16 = mybir.dt.bfloat16
AF = mybir.ActivationFunctionType
ALU = mybir.AluOpType
AX = mybir.AxisListType

SAME_ENGINE_SYNC = True
N_DMA_SEMS = 24
N_HW_SEMS = 16

NB = 2
L = 2048
CT = 256
T = L + CT
D = 1024
DFF = 2816
NT = T // 128
ALPHA = 4.0 ** 0.25
LN_EPS = 1e-5
NORM_EPS = 1e-6
LAMBDA_INIT = 0.8 - 0.6 * math.exp(-0.3)
EVEN_IN = 4112
NEG = -30000.0
SB_BASE = 16640


class Op:
    __slots__ = ("eng", "fn", "deps", "signal", "count", "dma", "sem", "target", "idx")


class Prog:
    ENGS = ("pe", "act", "dve", "pool", "sp")

    def __init__(self, nc):
        self.nc = nc
        self.ops = []
        self.last_w = {}
        self.readers = {}
        self.dma_rr = 0
        self.dma_rr_sw = 0
        self.dma_sem_total = [0] * N_DMA_SEMS
        self.dma_sem_lastop = [None] * N_DMA_SEMS
        self.last_eng = {}
        self.barrier_deps = []

    def barrier(self):
        deps = list(self.last_eng.values())
        deps += [o for o in self.dma_sem_lastop if o is not None]
        self.barrier_deps = deps
        self.last_w = {}
        self.readers = {}

    def _add(self, eng, fn, reads, writes, dma):
        op = Op()
        op.eng = eng
        op.fn = fn
        op.signal = False
        op.count = 0
        op.dma = dma
        op.sem = None
        op.target = 0
        op.idx = len(self.ops)
        deps = {}
        ps_reads = [r for r in reads if isinstance(r, tuple) and r and r[0] == "ps"]
        if ps_reads:
            writes = list(writes) + [r for r in ps_reads if r not in writes]
        for d in self.barrier_deps:
            deps[d.idx] = d
        for r in reads:
            w = self.last_w.get(r)
            if w is not None:
                deps[w.idx] = w
        for r in writes:
            w = self.last_w.get(r)
            if w is not None:
                deps[w.idx] = w
            for rd in self.readers.get(r, ()):
                deps[rd.idx] = rd
        if dma:
            if eng == "pool":
                i = N_HW_SEMS + self.dma_rr_sw
                self.dma_rr_sw = (self.dma_rr_sw + 1) % (N_DMA_SEMS - N_HW_SEMS)
            else:
                i = self.dma_rr
                self.dma_rr = (self.dma_rr + 1) % N_HW_SEMS
            prev = self.dma_sem_lastop[i]
            if prev is not None:
                deps[prev.idx] = prev
            self.dma_sem_total[i] += 16
            op.sem = i
            op.target = self.dma_sem_total[i]
            self.dma_sem_lastop[i] = op
        else:
            self.last_eng[eng] = op
        op.deps = list(deps.values())
        for r in reads:
            self.readers.setdefault(r, []).append(op)
        for r in writes:
            self.last_w[r] = op
            self.readers[r] = []
        self.ops.append(op)
        return op

    def op(self, eng, fn, reads=(), writes=()):
        return self._add(eng, fn, reads, writes, False)

    def dma(self, eng, fn, reads=(), writes=()):
        return self._add(eng, fn, reads, writes, True)

    @staticmethod
    def _skip(d, op):
        return d.eng == op.eng and (d.eng == "pe" or not SAME_ENGINE_SYNC) and not op.dma

    def emit(self):
        nc = self.nc
        for op in self.ops:
            for d in op.deps:
                if d.dma or self._skip(d, op):
                    continue
                d.signal = True
        cnt = {e: 0 for e in self.ENGS}
        for op in self.ops:
            if op.dma:
                continue
            if op.signal:
                cnt[op.eng] += 1
                op.count = cnt[op.eng]
        streams = {e: [o for o in self.ops if o.eng == e] for e in self.ENGS}
        with ExitStack() as es:
            esem = {e: es.enter_context(nc.semaphore("s_" + e)) for e in self.ENGS}
            dsem = [es.enter_context(nc.semaphore("d%d" % i)) for i in range(N_DMA_SEMS)]
            block = es.enter_context(nc.Block())
            all_dma_final = [(dsem[i], self.dma_sem_total[i]) for i in range(N_DMA_SEMS)
                             if self.dma_sem_total[i] > 0]

            def run_stream(e, eng):
                waited = {}
                for op in streams[e]:
                    for d in op.deps:
                        if d.dma:
                            key, val, sem = ("d", d.sem), d.target, dsem[d.sem]
                        else:
                            if self._skip(d, op):
                                continue
                            key, val, sem = ("e", d.eng), d.count, esem[d.eng]
                        if waited.get(key, 0) >= val:
                            continue
                        waited[key] = val
                        eng.wait_ge(sem, val)
                    ins = op.fn(eng)
                    if op.dma:
                        ins.then_inc(dsem[op.sem], 16)
                    elif op.signal:
                        ins.then_inc(esem[e], 1)
                if e == "sp":
                    for sem, val in all_dma_final:
                        eng.wait_ge(sem, val)

            @block.tensor
            def _(eng):
                run_stream("pe", eng)

            @block.scalar
            def _(eng):
                run_stream("act", eng)

            @block.vector
            def _(eng):
                run_stream("dve", eng)

            @block.gpsimd
            def _(eng):
                run_stream("pool", eng)

            @block.sync
            def _(eng):
                run_stream("sp", eng)


class Ctx:
    def __init__(self, nc):
        self.nc = nc
        self.P = Prog(nc)
        self.sb_off = SB_BASE
        self.sb_n = 0
        self.banks = [nc.alloc_psum_tensor("bank%d" % i, [128, 512], F32) for i in range(8)]
        self.bank_rr = 0
        self.dram = {}

    def sb_reset(self):
        self.P.barrier()
        self.sb_off = SB_BASE

    def sb(self, name, shape, dt):
        nbytes = int(np.prod(shape[1:])) * (2 if dt == BF16 else 4)
        nbytes = (nbytes + 63) // 64 * 64
        self.sb_n += 1
        t = self.nc.alloc_sbuf_tensor_at("%s_%d" % (name, self.sb_n), list(shape), dt, offset=self.sb_off)
        self.sb_off += nbytes
        assert self.sb_off <= 229376, (name, self.sb_off)
        return t

    def sb_at(self, name, shape, dt, off):
        self.sb_n += 1
        return self.nc.alloc_sbuf_tensor_at("%s_%d" % (name, self.sb_n), list(shape), dt, offset=off)

    def bank(self):
        i = self.bank_rr
        self.bank_rr = (self.bank_rr + 1) % 8
        return i

    def mm(self, out, lhsT, rhs, start, stop, reads, writes):
        self.P.op("pe", lambda e: e.matmul(out, lhsT=lhsT, rhs=rhs, start=start, stop=stop), reads, writes)

    def tr(self, out, in_, ident, reads, writes):
        self.P.op("pe", lambda e: e.transpose(out, in_, ident), reads, writes)

    def act(self, out, in_, func, reads, writes, bias=None, scale=None):
        kw = {}
        if bias is not None:
            kw["bias"] = bias
        if scale is not None:
            kw["scale"] = scale
        self.P.op("act", lambda e: e.activation(out=out, in_=in_, func=func, **kw), reads, writes)

    def tt(self, eng, out, in0, in1, op, reads, writes):
        self.P.op(eng, lambda e: e.tensor_tensor(out=out, in0=in0, in1=in1, op=op), reads, writes)

    def ts(self, eng, out, in0, s1, s2, op0, op1, reads, writes):
        if op1 is None:
            self.P.op(eng, lambda e: e.tensor_scalar(out=out, in0=in0, scalar1=s1, scalar2=None, op0=op0), reads, writes)
        else:
            self.P.op(eng, lambda e: e.tensor_scalar(out=out, in0=in0, scalar1=s1, scalar2=s2, op0=op0, op1=op1), reads, writes)

    def stt(self, eng, out, in0, scalar, in1, op0, op1, reads, writes):
        self.P.op(eng, lambda e: e.scalar_tensor_tensor(out=out, in0=in0, scalar=scalar, in1=in1, op0=op0, op1=op1), reads, writes)

    def cp(self, eng, out, in_, reads, writes):
        if eng == "act":
            self.P.op("act", lambda e: e.copy(out=out, in_=in_), reads, writes)
        else:
            self.P.op(eng, lambda e: e.tensor_copy(out=out, in_=in_), reads, writes)

    def ld(self, out, in_, reads, writes, q="sp", slow=False):
        if slow:
            self.P.dma(q, lambda e: e.dma_start(out=out, in_=in_, allow_slow_non_contiguous=True), reads, writes)
        else:
            self.P.dma(q, lambda e: e.dma_start(out=out, in_=in_), reads, writes)


def pp_view(vec_ap):
    return vec_ap.rearrange("(k p) -> p k", p=128)


def stage_mod(C):
    C.sb_reset()
    I = C.I
    cT = C.sb("cT", [128, 8, 3], F32)
    for r in range(3):
        for k0 in range(0, 8, 4):
            C.ld(cT[:, k0:k0 + 4, r], pp_view(I["cvec"][r])[:, k0:k0 + 4], [], ["cT"], slow=True)
    C.act(cT[:], cT[:], AF.Silu, ["cT"], ["cT"])
    wt = [C.sb("modw%d" % i, [128, 8, 512], F32) for i in range(2)]
    mb = C.sb("modb", [3, 6144], F32)
    res = C.sb("modres", [3, 6144], F32)
    n = 0
    for li in range(2):
        C.ld(mb[:], I["mod_b"][li].partition_broadcast(3), [], ["mb"])
        wv = I["mod_w"][li].rearrange("(k p) n -> p k n", p=128)
        for j in range(12):
            w = wt[n % 2]
            wk = ("modw", n % 2)
            n += 1
            C.ld(w[:], wv[:, :, j * 512:(j + 1) * 512], [], [wk])
            bk = C.bank()
            for k in range(8):
                C.mm(C.banks[bk][0:3, :], cT[:, k, :], w[:, k, :], k == 0, k == 7, ["cT", wk], [("ps", bk)])
            C.tt("dve", res[:, j * 512:(j + 1) * 512], C.banks[bk][0:3, :], mb[:, j * 512:(j + 1) * 512], ALU.add,
                 [("ps", bk), "mb"], ["modres"])
        C.ld(C.dram["modv"][li], res[:], ["modres"], [("modv", li)])


def load_mod_pp(C, li, r, which, name):
    base = 0 if which == "a" else 3 * D
    sh = C.sb(name + "sh", [128, 8], F32)
    sc = C.sb(name + "sc", [128, 8], F32)
    mv = C.dram["modv"][li, r]
    for k0 in range(0, 8, 4):
        C.ld(sh[:, k0:k0 + 4], pp_view(mv[base:base + D])[:, k0:k0 + 4], [("modv", li)], [name + "sh"], slow=True)
        C.ld(sc[:, k0:k0 + 4], pp_view(mv[base + D:base + 2 * D])[:, k0:k0 + 4], [("modv", li)], [name + "sc"], slow=True)
    C.ts("dve", sc[:], sc[:], 1.0, None, ALU.add, None, [name + "sc"], [name + "sc"])
    return sh, sc


def emit_modT(C, xt, xres, sh, sc, mres, hdst, hres, tag, slot):
    ht = C.hT_tiles[slot]
    hk = ("hTt", slot)
    for half in range(2):
        bk = C.bank()
        for q in range(4):
            k = half * 4 + q
            C.tr(C.banks[bk][:, q * 128:(q + 1) * 128], xt[:, k * 128:(k + 1) * 128], C.ident[:], [xres, "ident"], [("ps", bk)])
        for q in range(4):
            k = half * 4 + q
            C.act(ht[:, k, :], C.banks[bk][:, q * 128:(q + 1) * 128], AF.Identity, [("ps", bk)] + mres, [hk],
                  bias=sh[:, k:k + 1], scale=sc[:, k:k + 1])
    C.ld(hdst, ht[:], [hk], [hres])


def x_in_ap(C, src, b, t):
    if src == "input":
        if t < 16:
            return C.I["x"][b, t * 128:(t + 1) * 128, :]
        return C.I["ctx"][b, (t - 16) * 128:(t - 15) * 128, :]
    return C.dram[src][b, t * 128:(t + 1) * 128, :]


def hT_dst(C, b, t):
    return C.dram["HT"][b].rearrange("(k p) t -> p k t", p=128)[:, :, t * 128:(t + 1) * 128]


def stage_modT0(C):
    C.sb_reset()
    C.ident = C.sb("ident", [128, 128], F32)
    C.ld(C.ident[:], C.I["ident"], [], ["ident"])
    C.hT_tiles = [C.sb("hTt%d" % i, [128, 8, 128], BF16) for i in range(2)]
    xts = [C.sb("xt%d" % i, [128, D], F32) for i in range(3)]
    n = 0
    for b in range(NB):
        sh, sc = load_mod_pp(C, 0, b, "a", "m%d" % b)
        shc, scc = load_mod_pp(C, 0, 2, "a", "mc%d" % b)
        for t in range(NT):
            xt = xts[n % 3]
            xk = ("xt", n % 3)
            C.ld(xt[:], x_in_ap(C, "input", b, t), [], [xk])
            if t < 16:
                emit_modT(C, xt, xk, sh, sc, ["m%dsh" % b, "m%dsc" % b], hT_dst(C, b, t), ("HT", b, t), "m0", n % 2)
            else:
                emit_modT(C, xt, xk, shc, scc, ["mc%dsh" % b, "mc%dsc" % b], hT_dst(C, b, t), ("HT", b, t), "m0", n % 2)
            n += 1


def stage_projln(C, li, which, ysrc, KC, W, xsrc, xdst, ntiles, nxt):
    C.sb_reset()
    C.ident = C.sb("ident", [128, 128], F32)
    C.ld(C.ident[:], C.I["ident"], [], ["ident"])
    C.hT_tiles = [C.sb("hTt%d" % i, [128, 8, 128], BF16) for i in range(2)]
    wsb = C.sb("wsb", [128, KC, D], BF16)
    wv = W.rearrange("(k p) n -> p k n", p=128)
    for k0 in range(0, KC, 2):
        k1 = min(KC, k0 + 2)
        C.ld(wsb[:, k0:k1, :], wv[:, k0:k1, :], [], [("wsb", k0)], q="pool")
    wres = [("wsb", k0) for k0 in range(0, KC, 2)]
    lnrow = 0 if which == "a" else 1
    lng = C.sb("lng", [128, D], F32)
    lnb = C.sb("lnb", [128, D], F32)
    C.ld(lng[:], C.I["ln_g"][li, lnrow].partition_broadcast(128), [], ["lng"])
    C.ld(lnb[:], C.I["ln_b"][li, lnrow].partition_broadcast(128), [], ["lnb"])
    goff = 2 * D if which == "a" else 5 * D
    gates = {}
    for r in ([0, 1, 2] if ntiles > 16 else [0, 1]):
        g = C.sb("gate%d" % r, [128, D], F32)
        C.ld(g[:], C.dram["modv"][li, r, goff:goff + D].partition_broadcast(128), [("modv", li)], [("gate", r)])
        gates[r] = g
    mods = {}
    if nxt is not None:
        for r in ([0, 1, 2] if ntiles > 16 else [0, 1]):
            mods[r] = load_mod_pp(C, nxt[0], r, nxt[1], "nm%d" % r)
    yts = [C.sb("yt%d" % i, [128, KC, 512], BF16) for i in range(2)]
    xts = [C.sb("xt%d" % i, [128, D], F32) for i in range(2)]
    zs = [C.sb("z%d" % i, [128, D], F32) for i in range(2)]
    xos = [C.sb("xo%d" % i, [128, D], F32) for i in range(2)]
    st = [C.sb("st%d" % i, [128, 2, 6], F32) for i in range(2)]
    mv = [C.sb("mv%d" % i, [128, 4], F32) for i in range(2)]
    n = 0
    ng = 0
    for b in range(NB):
        yv = ysrc[b].rearrange("(k p) t -> p k t", p=128)
        for g0 in range(0, ntiles, 4):
            g1 = min(ntiles, g0 + 4)
            yt = yts[ng % 2]
            yk = ("yt", ng % 2)
            ng += 1
            ntok = (g1 - g0) * 128
            for k0 in range(0, KC, 8):
                k1 = min(KC, k0 + 8)
                C.ld(yt[:, k0:k1, 0:ntok], yv[:, k0:k1, g0 * 128:g1 * 128], [("Y", b)], [yk])
            for t in range(g0, g1):
                s = n % 2
                n += 1
                r = b if t < 16 else 2
                xt, z, xo = xts[s], zs[s], xos[s]
                xk, zk, xok, stk = ("xt", s), ("z", s), ("xo", s), ("st", s)
                C.ld(xt[:], x_in_ap(C, xsrc, b, t), [("X", xsrc, b)], [xk])
                bks = [C.bank(), C.bank()]
                for nn in range(2):
                    for k in range(KC):
                        C.mm(C.banks[bks[nn]][:, :], yt[:, k, (t - g0) * 128:(t - g0 + 1) * 128],
                             wsb[:, k, nn * 512:(nn + 1) * 512], k == 0, k == KC - 1, [yk] + wres, [("ps", bks[nn])])
                for nn in range(2):
                    C.tt("dve", z[:, nn * 512:(nn + 1) * 512], C.banks[bks[nn]][:, :], gates[r][:, nn * 512:(nn + 1) * 512],
                         ALU.mult, [("ps", bks[nn]), ("gate", r)], [zk])
                C.stt("dve", z[:], xt[:], ALPHA, z[:], ALU.mult, ALU.add, [xk, zk], [zk])
                for nn in range(2):
                    C.P.op("dve", (lambda o, i: (lambda e: e.bn_stats(out=o, in_=i)))(st[s][:, nn, :], z[:, nn * 512:(nn + 1) * 512]),
                           [zk], [stk])
                C.P.op("dve", (lambda o, i: (lambda e: e.bn_aggr(out=o, in_=i)))(mv[s][:, 0:2], st[s][:].rearrange("p a b -> p (a b)")),
                       [stk], [stk])
                C.ts("dve", mv[s][:, 2:3], mv[s][:, 1:2], LN_EPS, None, ALU.add, None, [stk], [stk])
                C.act(mv[s][:, 2:3], mv[s][:, 2:3], AF.Sqrt, [stk], [stk])
                C.P.op("dve", (lambda o: (lambda e: e.reciprocal(out=o, in_=o)))(mv[s][:, 2:3]), [stk], [stk])
                C.stt("dve", mv[s][:, 3:4], mv[s][:, 0:1], -1.0, mv[s][:, 2:3], ALU.mult, ALU.mult, [stk], [stk])
                C.act(xo[:], z[:], AF.Identity, [zk, stk], [xok], bias=mv[s][:, 3:4], scale=mv[s][:, 2:3])
                C.tt("dve", xo[:], xo[:], lng[:], ALU.mult, [xok, "lng"], [xok])
                C.tt("dve", xo[:], xo[:], lnb[:], ALU.add, [xok, "lnb"], [xok])
                if xdst == "out":
                    C.ld(C.I["out"][b, t * 128:(t + 1) * 128, :], xo[:], [xok], [("OUT", b, t)])
                else:
                    C.ld(C.dram[xdst][b, t * 128:(t + 1) * 128, :], xo[:], [xok], [("Xw", xdst, b, t)])
                if nxt is not None:
                    sh, sc = mods[r]
                    emit_modT(C, xo, xok, sh, sc, ["nm%dsh" % r, "nm%dsc" % r], hT_dst(C, b, t), ("HT", b, t), "pl", s)


def stage_ffn1(C, li, with_ctx):
    C.sb_reset()
    ntok = T if with_ctx else L
    hsb = C.sb("hsb", [128, 8, T], BF16)
    cw = C.sb("cw", [128, 22, 9], F32)
    for i in range(3):
        for j in range(3):
            for c0 in range(0, 22, 4):
                c1 = min(22, c0 + 4)
                C.ld(cw[:, c0:c1, 3 * i + j], C.I["f_conv"][li, i, j].rearrange("(c p) -> p c", p=128)[:, c0:c1], [], ["cw"], slow=True)
    identf = C.sb("identf", [128, 128], F32)
    identb = C.sb("identb", [128, 128], BF16)
    C.ld(identf[:], C.I["ident"], [], ["identf"])
    C.cp("dve", identb[:], identf[:], ["identf"], ["identb"])
    wgs = [C.sb("wg%d" % i, [128, 8, 128], BF16) for i in range(2)]
    wus = [C.sb("wu%d" % i, [128, 8, 128], BF16) for i in range(2)]
    dgs = [C.sb("dg%d" % i, [128, 9, 128], BF16) for i in range(2)]
    gpads = [C.sb("gpad%d" % i, [128, 34, 66], BF16) for i in range(2)]
    gctxs = [C.sb("gctx%d" % i, [128, CT + 2], BF16) for i in range(2)]
    us = [C.sb("u%d" % i, [128, T], F32) for i in range(2)]
    svs = [C.sb("sv%d" % i, [128, T], F32) for i in range(2)]
    gbs = [C.sb("gb%d" % i, [128, T], BF16) for i in range(2)]
    for i in range(2):
        C.P.op("pool", (lambda o: (lambda e: e.memset(o, 0.0)))(gpads[i][:].rearrange("p a b -> p (a b)")), [], [("gpad", i)])
        C.P.op("pool", (lambda o: (lambda e: e.memset(o, 0.0)))(gctxs[i][:]), [], [("gctx", i)])
    wgv = C.I["f_w_gate"][li].rearrange("(k p) n -> p k n", p=128)
    wuv = C.I["f_w_up"][li].rearrange("(k p) n -> p k n", p=128)
    n = 0
    for b in range(NB):
        hv = C.dram["HT"][b].rearrange("(k p) t -> p k t", p=128)
        for k in range(8):
            C.ld(hsb[:, k, 0:ntok], hv[:, k, 0:ntok], [("HT", b)], ["hsb"])
        for c in range(22):
            s = n % 2
            n += 1
            wg, wu, dg, gpad, gctx, u, sv, gb = wgs[s], wus[s], dgs[s], gpads[s], gctxs[s], us[s], svs[s], gbs[s]
            C.ld(wg[:], wgv[:, :, c * 128:(c + 1) * 128], [], [("wg", s)], q="pool")
            C.ld(wu[:], wuv[:, :, c * 128:(c + 1) * 128], [], [("wu", s)], q="pool")
            for tap in range(9):
                C.ts("dve", dg[:, tap, :], identb[:], cw[:, c, tap:tap + 1], None, ALU.mult, None, ["identb", "cw"], [("dg", s)])
            for tt_ in range(4):
                bk = C.bank()
                for k in range(8):
                    C.mm(C.banks[bk][:, :], wg[:, k, :], hsb[:, k, tt_ * 512:(tt_ + 1) * 512], k == 0, k == 7, ["hsb", ("wg", s)], [("ps", bk)])
                C.cp("act", gpad[:, 1 + 8 * tt_:9 + 8 * tt_, 1:65], C.banks[bk][:, :].rearrange("p (r w) -> p r w", w=64),
                     [("ps", bk)], [("gpad", s)])
            if with_ctx:
                bk = C.bank()
                for k in range(8):
                    C.mm(C.banks[bk][:, 0:CT], wg[:, k, :], hsb[:, k, L:T], k == 0, k == 7, ["hsb", ("wg", s)], [("ps", bk)])
                C.cp("act", gctx[:, 1:CT + 1], C.banks[bk][:, 0:CT], [("ps", bk)], [("gctx", s)])
            tiles = [(i * 512, 512) for i in range(4)] + ([(L, CT)] if with_ctx else [])
            for (t0, tn) in tiles:
                bk = C.bank()
                for k in range(8):
                    C.mm(C.banks[bk][:, 0:tn], wu[:, k, :], hsb[:, k, t0:t0 + tn], k == 0, k == 7, ["hsb", ("wu", s)], [("ps", bk)])
                C.cp("act", u[:, t0:t0 + tn], C.banks[bk][:, 0:tn], [("ps", bk)], [("u", s)])
            for tt_ in range(4):
                bk = C.bank()
                for tap in range(9):
                    i, j = tap // 3, tap % 3
                    C.mm(C.banks[bk][:, :], dg[:, tap, :], gpad[:, 8 * tt_ + i:8 * tt_ + i + 8, j:j + 64], tap == 0, tap == 8,
                         [("dg", s), ("gpad", s)], [("ps", bk)])
                C.act(sv[:, tt_ * 512:(tt_ + 1) * 512], C.banks[bk][:, :], AF.Silu, [("ps", bk)], [("sv", s)])
            if with_ctx:
                bk = C.bank()
                for j in range(3):
                    C.mm(C.banks[bk][:, 0:CT], dg[:, 3 + j, :], gctx[:, j:j + CT], j == 0, j == 2, [("dg", s), ("gctx", s)], [("ps", bk)])
                C.act(sv[:, L:T], C.banks[bk][:, 0:CT], AF.Silu, [("ps", bk)], [("sv", s)])
            C.tt("dve", gb[:, 0:ntok], sv[:, 0:ntok], u[:, 0:ntok], ALU.mult, [("sv", s), ("u", s)], [("gb", s)])
            C.ld(C.dram["GT"][b, c * 128:(c + 1) * 128, 0:ntok], gb[:, 0:ntok], [("gb", s)], [("G", b, c)])


def rot_weights(C, wr, w, hd_half, res_in, res_out):
    nblk = 128 // (2 * hd_half)
    for c in range(nblk):
        b0 = c * 2 * hd_half
        C.ts("pool", wr[:, :, b0:b0 + hd_half], w[:, :, b0 + hd_half:b0 + 2 * hd_half], -1.0, None, ALU.mult, None, [res_in], [res_out])
        C.cp("pool", wr[:, :, b0 + hd_half:b0 + 2 * hd_half], w[:, :, b0:b0 + hd_half], [res_in], [res_out])


def stage_odd(C):
    C.sb_reset()
    I = C.I
    hsb = C.sb("hsb", [128, 8, T], BF16)
    vsb = C.sb("vsb", [128, NT, D], BF16)
    cosT = C.sb("cosT", [128, L], F32)
    sinT = C.sb("sinT", [128, L], F32)
    C.ld(cosT[:], I["rope2_cos"], [], ["cosT"])
    C.ld(sinT[:], I["rope2_sin"], [], ["sinT"])
    ones_b = C.sb("ones_b", [128, 128], BF16)
    ones_f = C.sb("ones_f", [128, 128], F32)
    C.P.op("pool", lambda e: e.memset(ones_b[:], 1.0), [], ["ones_b"])
    C.P.op("pool", lambda e: e.memset(ones_f[:], 1.0), [], ["ones_f"])
    lpb = C.sb("lpb", [128, 4, 64], F32)
    C.ld(lpb[:].rearrange("p a b -> p (a b)"), I["o_lambda"][0].rearrange("a b -> (a b)").partition_broadcast(128), [], ["lpb"])
    lt = C.sb("lt", [128, 2, 64], F32)
    lam = C.sb("lam", [128, 4], F32)
    C.tt("dve", lt[:, 0, :], lpb[:, 0, :], lpb[:, 1, :], ALU.mult, ["lpb"], ["lt"])
    C.tt("dve", lt[:, 1, :], lpb[:, 2, :], lpb[:, 3, :], ALU.mult, ["lpb"], ["lt"])
    C.P.op("dve", lambda e: e.reduce_sum(out=lam[:, 0:2], in_=lt[:], axis=AX.X), ["lt"], ["lam"])
    C.act(lam[:, 0:2], lam[:, 0:2], AF.Exp, ["lam"], ["lam"])
    C.tt("dve", lam[:, 2:3], lam[:, 1:2], lam[:, 0:1], ALU.subtract, ["lam"], ["lam"])
    C.ts("dve", lam[:, 2:3], lam[:, 2:3], -LAMBDA_INIT, None, ALU.add, None, ["lam"], ["lam"])
    sw = C.sb("sw", [128, 1], F32)
    C.ld(sw[:], I["o_subln_w"][0].rearrange("(p o) -> p o", o=1), [], ["sw"], slow=True)
    C.ts("dve", sw[:], sw[:], 1.0 - LAMBDA_INIT, None, ALU.mult, None, ["sw"], ["sw"])
    wv_all = C.sb("wv_all", [128, 8, D], BF16)
    wqs = [C.sb("wq%d" % i, [128, 8, 128], BF16) for i in range(2)]
    wqr = [C.sb("wqr%d" % i, [128, 8, 128], BF16) for i in range(2)]
    wks = [C.sb("wk%d" % i, [128, 8, 128], BF16) for i in range(2)]
    wkr = [C.sb("wkr%d" % i, [128, 8, 128], BF16) for i in range(2)]
    qT = [C.sb("qT%d" % i, [128, L], BF16) for i in range(2)]
    kT = [C.sb("kT%d" % i, [128, T], BF16) for i in range(2)]
    t1 = [C.sb("t1_%d" % i, [128, 512], F32) for i in range(2)]
    t2 = [C.sb("t2_%d" % i, [128, 512], F32) for i in range(2)]
    pT = [C.sb("pT%d" % i, [128, 512], BF16) for i in range(6)]
    accs = [C.sb("acc%d" % i, [128, 512], F32) for i in range(2)]
    ep = [C.sb("ep%d" % i, [128, 512], F32) for i in range(5)]
    yb = [C.sb("yb%d" % i, [128, 512], BF16) for i in range(2)]
    wqkv = I["o_w_qkv"][0].rearrange("(k p) n -> p k n", p=128)
    npt = 0
    nrot = 0
    ny = 0
    for b in range(NB):
        hv = C.dram["HT"][b].rearrange("(k p) t -> p k t", p=128)
        for k in range(8):
            C.ld(hsb[:, k, :], hv[:, k, :], [("HT", b)], ["hsb"])
        for k0 in range(0, 8, 2):
            C.ld(wv_all[:, k0:k0 + 2, :], wqkv[:, k0:k0 + 2, 2 * D:3 * D], [], ["wv_all"], q="pool")
        for t in range(NT):
            for nn in range(2):
                bk = C.bank()
                for k in range(8):
                    C.mm(C.banks[bk][:, :], hsb[:, k, t * 128:(t + 1) * 128], wv_all[:, k, nn * 512:(nn + 1) * 512], k == 0, k == 7,
                         ["hsb", "wv_all"], [("ps", bk)])
                C.cp("act" if nn == 0 else "dve", vsb[:, t, nn * 512:(nn + 1) * 512], C.banks[bk][:, :], [("ps", bk)], ["vsb"])
        for hd in range(8):
            s = hd % 2
            C.ld(wqs[s][:], wqkv[:, :, hd * 128:(hd + 1) * 128], [], [("wq", s)], q="pool")
            C.ld(wks[s][:], wqkv[:, :, D + hd * 128:D + (hd + 1) * 128], [], [("wk", s)], q="pool")
            rot_weights(C, wqr[s], wqs[s], 32, ("wq", s), ("wqr", s))
            rot_weights(C, wkr[s], wks[s], 32, ("wk", s), ("wkr", s))
            for (w, wr, dst, dres, wres) in ((wqs[s], wqr[s], qT[s], ("qT", s), [("wq", s), ("wqr", s)]),
                                            (wks[s], wkr[s], kT[s], ("kT", s), [("wk", s), ("wkr", s)])):
                for tt_ in range(4):
                    t0 = tt_ * 512
                    b1, b2 = C.bank(), C.bank()
                    for k in range(8):
                        C.mm(C.banks[b1][:, :], w[:, k, :], hsb[:, k, t0:t0 + 512], k == 0, k == 7, ["hsb"] + wres, [("ps", b1)])
                    for k in range(8):
                        C.mm(C.banks[b2][:, :], wr[:, k, :], hsb[:, k, t0:t0 + 512], k == 0, k == 7, ["hsb"] + wres, [("ps", b2)])
                    r = nrot % 2
                    nrot += 1
                    C.tt("dve", t1[r][:], C.banks[b1][:, :], cosT[:, t0:t0 + 512], ALU.mult, [("ps", b1), "cosT"], [("t1", r)])
                    C.tt("dve", t2[r][:], C.banks[b2][:, :], sinT[:, t0:t0 + 512], ALU.mult, [("ps", b2), "sinT"], [("t2", r)])
                    C.tt("dve", dst[:, t0:t0 + 512], t1[r][:], t2[r][:], ALU.add, [("t1", r), ("t2", r)], [dres])
            bk = C.bank()
            for k in range(8):
                C.mm(C.banks[bk][:, 0:CT], wks[s][:, k, :], hsb[:, k, L:T], k == 0, k == 7, ["hsb", ("wk", s)], [("ps", bk)])
            C.cp("act", kT[s][:, L:T], C.banks[bk][:, 0:CT], [("ps", bk)], [("kT", s)])
            for qb in range(4):
                po = [0, 1]
                pss = [6, 7]

                def emit_S(kt, cs=(0, 1)):
                    for c in cs:
                        bs = 2 + (kt % 2) * 2 + c
                        C.mm(C.banks[bs][:, :], kT[s][c * 64:(c + 1) * 64, kt * 128:(kt + 1) * 128],
                             qT[s][c * 64:(c + 1) * 64, qb * 512:(qb + 1) * 512], True, True, [("kT", s), ("qT", s)], [("ps", bs)])

                emit_S(0, (0,))
                C.mm(C.banks[7][0:8, 0:8], ones_f[:, 0:8], ones_f[:, 0:8], True, True, ["ones_f"], [("ps", 7)])
                emit_S(0, (1,))
                for kt in range(NT):
                    pis = []
                    for c in range(2):
                        bs = 2 + (kt % 2) * 2 + c
                        pi = npt % 6
                        npt += 1
                        pis.append(pi)
                        C.act(pT[pi][:], C.banks[bs][:, :], AF.Exp, [("ps", bs)], [("pT", pi)], scale=0.125)
                    for c in range(2):
                        pi = pis[c]
                        if kt + 1 < NT:
                            emit_S(kt + 1, (c,))
                        C.mm(C.banks[po[c]][:, :], vsb[:, kt, hd * 128:(hd + 1) * 128], pT[pi][:], kt == 0, kt == NT - 1,
                             ["vsb", ("pT", pi)], [("ps", po[c])])
                    for c in range(2):
                        pi = pis[c]
                        if kt == 0:
                            C.cp("dve", accs[c][:], pT[pi][:], [("pT", pi)], [("acc", c)])
                        else:
                            C.tt("dve", accs[c][:], accs[c][:], pT[pi][:], ALU.add, [("pT", pi), ("acc", c)], [("acc", c)])
                for c in range(2):
                    C.mm(C.banks[pss[c]][:, :], ones_f[:], accs[c][:], True, True, ["ones_f", ("acc", c)], [("ps", pss[c])])
                e0, e1, e2, e3, e4 = ep
                C.P.op("dve", (lambda o, i: (lambda e: e.reciprocal(out=o, in_=i)))(e0[:], C.banks[pss[0]][:, :]), [("ps", pss[0])], ["e0"])
                C.tt("dve", e1[:], C.banks[po[0]][:, :], e0[:], ALU.mult, [("ps", po[0]), "e0"], ["e1"])
                C.P.op("dve", (lambda o, i: (lambda e: e.reciprocal(out=o, in_=i)))(e2[:], C.banks[pss[1]][:, :]), [("ps", pss[1])], ["e2"])
                C.tt("dve", e3[:], C.banks[po[1]][:, :], e2[:], ALU.mult, [("ps", po[1]), "e2"], ["e3"])
                C.stt("dve", e1[:], e3[:], lam[:, 2:3], e1[:], ALU.mult, ALU.add, ["e3", "e1", "lam"], ["e1"])
                C.act(e4[:], e1[:], AF.Square, ["e1"], ["e4"])
                bq = 2
                C.mm(C.banks[bq][:, :], ones_f[:], e4[:], True, True, ["ones_f", "e4"], [("ps", bq)])
                C.ts("dve", e0[:], C.banks[bq][:, :], 1.0 / 128.0, NORM_EPS, ALU.mult, ALU.add, [("ps", bq)], ["e0"])
                C.act(e0[:], e0[:], AF.Ln, ["e0"], ["e0"])
                C.act(e0[:], e0[:], AF.Exp, ["e0"], ["e0"], scale=-0.5)
                yi = ny % 2
                ny += 1
                C.stt("dve", yb[yi][:], e1[:], sw[:, 0:1], e0[:], ALU.mult, ALU.mult, ["e1", "e0", "sw"], [("yb", yi)])
                C.ld(C.dram["YT"][b, hd * 128:(hd + 1) * 128, qb * 512:(qb + 1) * 512], yb[yi][:], [("yb", yi)], [("Yw", b, hd, qb)])


NCH = T // 64
DK_SCALE = 128.0 ** -0.5


import os


def stage_even(C):
    EV_NB = int(os.environ.get('EV_NB', NB))
    EV_STOP = int(os.environ.get('EV_STOP', 99))
    EV_SUB = int(os.environ.get('EV_SUB', 99))
    EV_HEADS = [int(x) for x in os.environ.get('EV_HEADS', '0,1,2,3,4,5,6,7').split(',')]
    C.sb_reset()
    I = C.I
    P = C.P
    mk = C.sb("mk", [64, 8, 64], F32)
    for i in range(7):
        C.ld(mk[:, i, :], I["masks"][i], [], ["mk"])
    U = [mk[:, 0, :], mk[:, 1, :]]
    POS = [mk[:, 2, :], mk[:, 3, :]]
    STRICT = [mk[:, 4, :], mk[:, 5, :]]
    I64 = mk[:, 6, :]
    ident = C.sb("ident", [128, 128], F32)
    C.ld(ident[:], I["ident"], [], ["ident"])
    ones64 = C.sb("ones64", [64, 128], F32)
    ones128 = C.sb("ones128", [128, 128], F32)
    P.op("pool", lambda e: e.memset(ones64[:], 1.0), [], ["ones64"])
    P.op("pool", lambda e: e.memset(ones128[:], 1.0), [], ["ones128"])
    cosT = C.sb("cosT", [128, L], F32)
    sinT = C.sb("sinT", [128, L], F32)
    C.ld(cosT[:], I["rope1_cos"], [], ["cosT"])
    C.ld(sinT[:], I["rope1_sin"], [], ["sinT"])
    cwe = C.sb("cwe", [128, 12, 3], F32)
    for tap in range(3):
        for g0 in range(0, 12, 4):
            C.ld(cwe[:, g0:g0 + 4, tap], I["e_conv"][0, tap].rearrange("(g p) -> p g", p=128)[:, g0:g0 + 4], [], ["cwe"], slow=True)
    nw = C.sb("nw", [128, 1], F32)
    C.ld(nw[:], I["e_norm_w"][0].rearrange("(p o) -> p o", o=1), [], ["nw"], slow=True)
    sc8 = C.sb("sc8", [8, 4], F32)
    C.ld(sc8[:, 0:1], I["e_a_log"][0].rearrange("d (h o) -> (d h) o", o=1), [], ["sc8"], slow=True)
    C.ld(sc8[:, 1:2], I["e_dt_bias"][0].rearrange("d (h o) -> (d h) o", o=1), [], ["sc8"], slow=True)
    C.ld(sc8[:, 2:3], I["e_ret_decay"][0].rearrange("d (h o) -> (d h) o", o=1), [], ["sc8"], slow=True)
    C.act(sc8[:, 0:1], sc8[:, 0:1], AF.Exp, ["sc8"], ["sc8"])
    C.ts("dve", sc8[:, 0:1], sc8[:, 0:1], -1.0, None, ALU.mult, None, ["sc8"], ["sc8"])
    C.act(sc8[:, 2:3], sc8[:, 2:3], AF.Exp, ["sc8"], ["sc8"])
    C.ts("dve", sc8[:, 2:3], sc8[:, 2:3], -1.0, None, ALU.mult, None, ["sc8"], ["sc8"])

    hsb_off = C.sb_off
    hsb = C.sb("hsb", [128, 8, T], BF16)
    wab = C.sb("wab", [128, 8, 16], BF16)
    wts = [C.sb("wt%d" % i, [128, 8, 128], BF16) for i in range(4)]
    wrs = [C.sb("wr%d" % i, [128, 8, 128], BF16) for i in range(2)]
    raw = C.sb("raw", [128, T], F32)
    qT = C.sb("qT", [128, T], F32)
    kT = C.sb("kT", [128, T], F32)
    vT = C.sb("vT", [128, T], F32)
    szT = C.sb("szT", [128, T], BF16)
    tmp = C.sb("tmp", [128, 512], F32)
    tmp2 = C.sb("tmp2", [128, 512], F32)
    ktok = C.sb("ktok", [64, NCH, 128], F32)
    vtok = C.sb("vtok", [64, NCH, 128], F32)
    oacc_off = C.sb_off
    oacc = C.sb("oacc", [64, NCH, 128], F32)
    ab = C.sb_at("ab", [8, T], F32, oacc_off)
    ab2 = C.sb_at("ab2", [8, T], F32, oacc_off + T * 4)
    TTs = C.sb_at("TTs", [64, 2 * NCH, 64], F32, hsb_off)
    QKTs = C.sb_at("QKTs", [64, 2 * NCH, 64], F32, hsb_off + 2 * NCH * 64 * 4)
    yT = C.sb("yT", [128, T], BF16)
    la = [C.sb("la%d" % d, [64, NCH], F32) for d in range(2)]
    bet = [C.sb("bet%d" % d, [64, NCH], F32) for d in range(2)]
    G = [C.sb("G%d" % d, [64, NCH], F32) for d in range(2)]
    EG = [C.sb("EG%d" % d, [64, NCH], F32) for d in range(2)]
    EKG = [C.sb("EKG%d" % d, [64, NCH], F32) for d in range(2)]
    NBEG = [C.sb("NBEG%d" % d, [64, NCH], F32) for d in range(2)]
    CD = [C.sb("CD%d" % d, [128, NCH], F32) for d in range(2)]
    S = [C.sb("S%d" % d, [128, 128], F32) for d in range(2)]
    st = C.sb("st", [64, 4 * NCH], F32)
    lat = C.sb("lat", [NCH, 128], F32)
    NSM = 72
    sm = [C.sb("sm%d" % i, [64, 64], F32) for i in range(NSM)]
    NPM = 24
    Pm = [C.sb("Pm%d" % i, [64, 64], F32) for i in range(NPM)]
    NVB = 8
    vb = [C.sb("vb%d" % i, [64, 128], F32) for i in range(NVB)]
    smi = [0]

    tb = [0, 0]
    psA = {"banks": [2, 3, 4, 5], "bi": -1, "j": 8, "round": -1}
    psB = {"banks": [6, 7], "bi": -1, "j": 8, "round": -1}
    rnd = [0]

    def _pslot(st):
        if st["round"] != rnd[0] or st["j"] >= 8:
            st["round"] = rnd[0]
            st["bi"] = (st["bi"] + 1) % len(st["banks"])
            st["j"] = 0
        bk, j = st["banks"][st["bi"]], st["j"]
        st["j"] += 1
        return C.banks[bk][0:64, j * 64:(j + 1) * 64], ("ps", bk)

    def pslot():
        return _pslot(psA)

    def pslotB():
        return _pslot(psB)

    def pmat():
        i = tb[1] % NPM
        tb[1] += 1
        return Pm[i][:], ("Pm", i)

    def small():
        i = smi[0] % NSM
        smi[0] += 1
        return sm[i][:], ("sm", i)

    win = I["e_w_in"][0].rearrange("(k p) n -> p k n", p=128)
    LAD = C.dram["LAD"]

    def proj(w, t0, tn, wres):
        bk = C.bank()
        for k in range(8):
            C.mm(C.banks[bk][:, 0:tn], w[:, k, :], hsb[:, k, t0:t0 + tn], k == 0, k == 7, ["hsb"] + wres, [("ps", bk)])
        return bk

    tiles = [(i * 512, 512) for i in range(4)] + [(L, CT)]
    wloaded = [False]

    def load_head_weights(hd_):
        h4_ = hd_ % 4
        if hd_ < 4:
            cols_ = [h4_ * 128, 512 + h4_ * 128, 1024 + h4_ * 128, 1536 + h4_ * 128]
        else:
            cols_ = [2064 + h4_ * 128, 2576 + h4_ * 128, 3088 + h4_ * 128, 3600 + h4_ * 128]
        for i_ in range(4):
            C.ld(wts[i_][:], win[:, :, cols_[i_]:cols_[i_] + 128], [], [("wt", i_)], q="pool")

    for b in range(EV_NB):
        hv = C.dram["HT"][b].rearrange("(k p) t -> p k t", p=128)
        P.barrier()
        for k in range(8):
            C.ld(hsb[:, k, :], hv[:, k, :], [("HT", b)], ["hsb"])
        C.ld(wab[:], win[:, :, 2048:2064], [], ["wab"], q="pool")
        for (t0, tn) in tiles:
            bk = C.bank()
            for k in range(8):
                C.mm(C.banks[bk][0:8, 0:tn], wab[:, k, 0:8], hsb[:, k, t0:t0 + tn], k == 0, k == 7, ["hsb", "wab"], [("ps", bk)])
            C.act(ab[:, t0:t0 + tn], C.banks[bk][0:8, 0:tn], AF.Exp, [("ps", bk), "sc8"], ["ab"], bias=sc8[:, 1:2])
            bk = C.bank()
            for k in range(8):
                C.mm(C.banks[bk][0:8, 0:tn], wab[:, k, 8:16], hsb[:, k, t0:t0 + tn], k == 0, k == 7, ["hsb", "wab"], [("ps", bk)])
            C.act(ab2[:, t0:t0 + tn], C.banks[bk][0:8, 0:tn], AF.Sigmoid, [("ps", bk)], ["ab2"])
        C.ts("dve", ab[:], ab[:], 1.0, None, ALU.add, None, ["ab"], ["ab"])
        C.act(ab[:], ab[:], AF.Ln, ["ab"], ["ab"])
        C.ts("dve", ab[:], ab[:], sc8[:, 0:1], None, ALU.mult, None, ["ab", "sc8"], ["ab"])
        C.ld(LAD[b, 0:8, :], ab[:], ["ab"], [("LAD", b, 0)])
        C.ld(LAD[b, 8:16, :], ab2[:], ["ab2"], [("LAD", b, 1)])
        C.ts("dve", ab2[:], ab2[:], 0.0, None, ALU.mult, None, ["ab2"], ["ab2"])
        C.ts("dve", ab2[:], ab2[:], sc8[:, 2:3], None, ALU.add, None, ["ab2", "sc8"], ["ab2"])
        C.ld(LAD[b, 16:24, :], ab2[:], ["ab2"], [("LAD", b, 2)])
        ladres = [("LAD", b, 0), ("LAD", b, 1), ("LAD", b, 2)]
        if EV_STOP <= 1:
            return

        for hd in EV_HEADS:
            P.barrier()
            if hd != EV_HEADS[0]:
                for k in range(8):
                    C.ld(hsb[:, k, :], hv[:, k, :], [("HT", b)], ["hsb"])
            delta = hd < 4
            h4 = hd % 4
            if not wloaded[0]:
                load_head_weights(hd)
            wloaded[0] = False
            dsts = [qT, kT, vT]
            dres = ["qT", "kT", "vT"]
            if delta:
                for i in range(3):
                    for (t0, tn) in tiles:
                        bk = proj(wts[i], t0, tn, [("wt", i)])
                        C.cp("act", raw[:, t0:t0 + tn], C.banks[bk][:, 0:tn], [("ps", bk)], ["raw"])
                    gidx = i * 4 + h4
                    d_ = dsts[i]
                    for (s0, sn) in ((0, L), (L, CT)):
                        C.ts("dve", d_[:, s0:s0 + sn], raw[:, s0:s0 + sn], cwe[:, gidx, 1:2], None, ALU.mult, None, ["raw", "cwe"], [dres[i]])
                        C.stt("dve", d_[:, s0 + 1:s0 + sn], raw[:, s0:s0 + sn - 1], cwe[:, gidx, 0:1], d_[:, s0 + 1:s0 + sn], ALU.mult, ALU.add,
                              ["raw", "cwe", dres[i]], [dres[i]])
                        C.stt("dve", d_[:, s0:s0 + sn - 1], raw[:, s0 + 1:s0 + sn], cwe[:, gidx, 2:3], d_[:, s0:s0 + sn - 1], ALU.mult, ALU.add,
                              ["raw", "cwe", dres[i]], [dres[i]])
                    C.act(d_[:], d_[:], AF.Silu, [dres[i]], [dres[i]])
                    if i < 2:
                        for (t0, tn) in tiles:
                            C.act(tmp[:, 0:tn], d_[:, t0:t0 + tn], AF.Square, [dres[i]], ["tmp"])
                            bk = C.bank()
                            C.mm(C.banks[bk][:, 0:tn], ones128[:], tmp[:, 0:tn], True, True, ["ones128", "tmp"], [("ps", bk)])
                            C.ts("dve", tmp2[:, 0:tn], C.banks[bk][:, 0:tn], NORM_EPS, None, ALU.add, None, [("ps", bk)], ["tmp2"])
                            C.act(tmp2[:, 0:tn], tmp2[:, 0:tn], AF.Sqrt, ["tmp2"], ["tmp2"])
                            P.op("dve", (lambda o: (lambda e: e.reciprocal(out=o, in_=o)))(tmp2[:, 0:tn]), ["tmp2"], ["tmp2"])
                            C.stt("dve", d_[:, t0:t0 + tn], d_[:, t0:t0 + tn], DK_SCALE if i == 0 else 1.0, tmp2[:, 0:tn], ALU.mult, ALU.mult,
                                  [dres[i], "tmp2"], [dres[i]])
            else:
                for i in range(2):
                    rot_weights(C, wrs[i], wts[i], 64, ("wt", i), ("wr", i))
                    d_ = dsts[i]
                    sc_ = 1.0 if i == 0 else DK_SCALE
                    for (t0, tn) in tiles[:4]:
                        b1 = proj(wts[i], t0, tn, [("wt", i)])
                        b2 = proj(wrs[i], t0, tn, [("wr", i)])
                        C.tt("dve", tmp[:], C.banks[b1][:, :], cosT[:, t0:t0 + 512], ALU.mult, [("ps", b1), "cosT"], ["tmp"])
                        C.tt("dve", tmp2[:], C.banks[b2][:, :], sinT[:, t0:t0 + 512], ALU.mult, [("ps", b2), "sinT"], ["tmp2"])
                        if i == 0:
                            C.tt("dve", d_[:, t0:t0 + 512], tmp[:], tmp2[:], ALU.add, ["tmp", "tmp2"], [dres[i]])
                        else:
                            C.tt("dve", tmp[:], tmp[:], tmp2[:], ALU.add, ["tmp", "tmp2"], ["tmp"])
                            C.act(d_[:, t0:t0 + 512], tmp[:], AF.Identity, ["tmp"], [dres[i]], scale=sc_)
                    bk = proj(wts[i], L, CT, [("wt", i)])
                    C.act(d_[:, L:T], C.banks[bk][:, 0:CT], AF.Identity, [("ps", bk)], [dres[i]], scale=sc_)
                for (t0, tn) in tiles:
                    bk = proj(wts[2], t0, tn, [("wt", 2)])
                    C.cp("act", vT[:, t0:t0 + tn], C.banks[bk][:, 0:tn], [("ps", bk)], ["vT"])
            for (t0, tn) in tiles:
                bk = proj(wts[3], t0, tn, [("wt", 3)])
                C.act(szT[:, t0:t0 + tn], C.banks[bk][:, 0:tn], AF.Silu, [("ps", bk)], ["szT"])
            if EV_STOP <= 2:
                return
            for (src, sres, dst, dr) in ((kT, "kT", ktok, "ktok"), (vT, "vT", vtok, "vtok")):
                for n0 in range(0, NCH, 4):
                    bk = C.bank()
                    for q in range(4):
                        n = n0 + q
                        C.tr(C.banks[bk][0:64, q * 128:(q + 1) * 128], src[:, n * 64:(n + 1) * 64], ident[:], [sres, "ident"], [("ps", bk)])
                    C.cp("act" if (n0 // 4) % 2 == 0 else "dve", dst[:, n0:n0 + 4, :].rearrange("p a b -> p (a b)"), C.banks[bk][0:64, :],
                         [("ps", bk)], [dr])
            if EV_STOP <= 3:
                return
            for d in range(2):
                row = (d * 4 + h4) if delta else (16 + d * 4 + h4)
                C.ld(lat[:, 0:64], LAD[b, row].rearrange("(n i) -> n i", i=64), ladres, ["lat"])
                if delta:
                    C.ld(lat[:, 64:128], LAD[b, 8 + d * 4 + h4].rearrange("(n i) -> n i", i=64), ladres, ["lat"])
                bkt = C.bank()
                C.tr(C.banks[bkt][0:64, 0:NCH], lat[:, 0:64], ident[0:NCH, 0:NCH], ["lat", "ident"], [("ps", bkt)])
                if delta:
                    C.tr(C.banks[bkt][0:64, 64:64 + NCH], lat[:, 64:128], ident[0:NCH, 0:NCH], ["lat", "ident"], [("ps", bkt)])
                C.cp("dve", la[d][:], C.banks[bkt][0:64, 0:NCH], [("ps", bkt)], [("la", d)])
                if delta:
                    C.cp("dve", bet[d][:], C.banks[bkt][0:64, 64:64 + NCH], [("ps", bkt)], [("bet", d)])
                bk = C.bank()
                C.mm(C.banks[bk][0:64, 0:NCH], U[d], la[d][:], True, True, ["mk", ("la", d)], [("ps", bk)])
                C.cp("dve", G[d][:], C.banks[bk][0:64, 0:NCH], [("ps", bk)], [("G", d)])
                C.act(EG[d][:], C.banks[bk][0:64, 0:NCH], AF.Exp, [("ps", bk)], [("EG", d)])
                bk2 = C.bank()
                C.mm(C.banks[bk2][:, 0:NCH], ones64[:], la[d][:], True, True, ["ones64", ("la", d)], [("ps", bk2)])
                C.act(CD[d][:], C.banks[bk2][:, 0:NCH], AF.Exp, [("ps", bk2)], [("CD", d)])
                C.tt("dve", EKG[d][:], C.banks[bk2][0:64, 0:NCH], G[d][:], ALU.subtract, [("ps", bk2), ("G", d)], [("EKG", d)])
                C.act(EKG[d][:], EKG[d][:], AF.Exp, [("EKG", d)], [("EKG", d)])
                if delta:
                    C.stt("dve", NBEG[d][:], bet[d][:], -1.0, EG[d][:], ALU.mult, ALU.mult, [("bet", d), ("EG", d)], [("NBEG", d)])
            if EV_STOP <= 4:
                return
            P.barrier()
            hi_ = EV_HEADS.index(hd)
            nxt_hd = EV_HEADS[hi_ + 1] if hi_ + 1 < len(EV_HEADS) else (EV_HEADS[0] if b + 1 < EV_NB else None)
            if nxt_hd is not None:
                load_head_weights(nxt_hd)
                wloaded[0] = True
            for n0 in range(0, NCH, 4):
                P.op("pool", (lambda o: (lambda e: e.memset(o, 0.0)))(oacc[:, n0:n0 + 4, :].rearrange("p a b -> p (a b)")), [],
                     [("oacc", n) for n in range(n0, n0 + 4)])
            def chain(n, d, kk_ap, kk_r, qk_ap, qk_r):
                labc, labr = small()
                C.act(labc, ones64[:, 0:64], AF.Identity, ["ones64", ("la", d)], [labr], scale=la[d][:, n:n + 1])
                yield
                pg, pgr = pslot()
                C.mm(pg, labc, U[d], True, False, [labr, "mk"], [pgr])
                C.mm(pg, I64, POS[d], False, True, ["mk"], [pgr])
                yield
                dec, decr = small()
                C.act(dec, pg, AF.Exp, [pgr, ("G", d)], [decr], bias=G[d][:, n:n + 1], scale=-1.0)
                yield
                qk, qkr = small()
                C.tt("dve", qk, qk_ap, dec, ALU.mult, [qk_r, decr], [qkr])
                if delta:
                    decs, decsr = small()
                    C.tt("pool", decs, dec, STRICT[d], ALU.mult, [decr, "mk"], [decsr])
                yield
                pt, ptr = pslot()
                C.tr(pt, qk, I64, [qkr, "mk"], [ptr])
                if delta:
                    Y, Yr = small()
                    C.stt("dve", Y, kk_ap, bet[d][:, n:n + 1], decs, ALU.mult, ALU.mult, [kk_r, ("bet", d), decsr], [Yr])
                yield
                C.cp("act", QKTs[:, 2 * n + d, :], pt, [ptr], [("QKT", n, d)])
                if not delta:
                    return
                px, pxr = pslot()
                C.tr(px, Y, I64, [Yr, "mk"], [pxr])
                yield
                X, Xr = small()
                C.cp("act", X, px, [pxr], [Xr])
                yield
                Pc, Pr = pmat()
                C.tt("pool", Pc, I64, X, ALU.subtract, ["mk", Xr], [Pr])
                yield
                for k in range(5):
                    py, pyr = pslot()
                    C.mm(py, X, Y, True, True, [Xr, Yr], [pyr])
                    if k < 4:
                        px2, px2r = pslotB()
                        C.mm(px2, Y, X, True, True, [Xr, Yr], [px2r])
                    yield
                    Y2, Y2r = small()
                    C.cp("act", Y2, py, [pyr], [Y2r])
                    if k < 4:
                        X2, X2r = small()
                        C.cp("dve", X2, px2, [px2r], [X2r])
                    yield
                    pp, ppr = pslotB()
                    C.mm(pp, Y2, Pc, True, True, [Y2r, Pr], [ppr])
                    yield
                    if k < 4:
                        Pn, Pnr = pmat()
                    else:
                        Pn, Pnr = TTs[:, 2 * n + d, :], ("TT", n, d)
                    C.tt("dve", Pn, pp, Pc, ALU.add, [Pr, ppr], [Pnr])
                    Pc, Pr = Pn, Pnr
                    Y, Yr = Y2, Y2r
                    if k < 4:
                        X, Xr = X2, X2r
                    yield

            GC = int(os.environ.get("EV_GC", 4))
            for gi, n0 in enumerate(range(0, NCH, GC)):
                gens = []
                for q in range(GC):
                    n = n0 + q
                    ksl = kT[:, n * 64:(n + 1) * 64]
                    qsl = qT[:, n * 64:(n + 1) * 64]
                    bkq = gi % 2
                    kk_ap, kk_r = C.banks[bkq][0:64, q * 64:(q + 1) * 64], ("ps", bkq)
                    qk_ap, qk_r = C.banks[bkq][0:64, (4 + q) * 64:(5 + q) * 64], ("ps", bkq)
                    if delta:
                        C.mm(kk_ap, ksl, ksl, True, True, ["kT"], [kk_r])
                    C.mm(qk_ap, qsl, ksl, True, True, ["kT", "qT"], [qk_r])
                    for d in range(2):
                        gens.append(chain(n, d, kk_ap, kk_r, qk_ap, qk_r))
                lev = 0
                while gens:
                    nxt = []
                    lev += 1
                    rnd[0] += 1
                    if lev > int(os.environ.get("EV_LEV", 999)):
                        break
                    for g in gens:
                        try:
                            next(g)
                            nxt.append(g)
                        except StopIteration:
                            pass
                    gens = nxt
            if EV_STOP <= 5:
                return
            for d in range(2):
                P.op("pool", (lambda o: (lambda e: e.memset(o, 0.0)))(S[d][:]), [], [("S", d)])
            order = [list(range(32, 36)) + list(range(0, 32)), list(range(35, 31, -1)) + list(range(31, -1, -1))]
            nvbc = [0]

            def scan_step(step, d):
                n = order[d][step]
                ksl = kT[:, n * 64:(n + 1) * 64]
                qsl = qT[:, n * 64:(n + 1) * 64]
                Sd, Sr = S[d][:], ("S", d)
                vn = vb[nvbc[0] % NVB]
                vnr = ("vb", nvbc[0] % NVB)
                nvbc[0] += 1
                vs = vb[nvbc[0] % NVB]
                vsr = ("vb", nvbc[0] % NVB)
                nvbc[0] += 1
                ores = ("oacc", n)
                if delta:
                    b1 = C.bank()
                    C.mm(C.banks[b1][0:64, 0:128], ksl, Sd, True, True, ["kT", Sr], [("ps", b1)])
                    C.act(vs[:], vtok[:, n, :], AF.Identity, ["vtok", ("bet", d)], [vsr], scale=bet[d][:, n:n + 1])
                else:
                    C.act(vs[:], vtok[:, n, :], AF.Identity, ["vtok", ("EKG", d)], [vsr], scale=EKG[d][:, n:n + 1])
                b3 = C.bank()
                C.mm(C.banks[b3][0:64, 0:128], qsl, Sd, True, True, ["qT", Sr], [("ps", b3)])
                yield
                if delta:
                    C.stt("dve", vs[:], C.banks[b1][0:64, 0:128], NBEG[d][:, n:n + 1], vs[:], ALU.mult, ALU.add,
                          [("ps", b1), ("NBEG", d), vsr], [vsr])
                C.stt("dve", oacc[:, n, :], C.banks[b3][0:64, 0:128], EG[d][:, n:n + 1], oacc[:, n, :], ALU.mult, ALU.add,
                      [("ps", b3), ("EG", d), ores], [ores])
                C.act(Sd, Sd, AF.Identity, [Sr, ("CD", d)], [Sr], scale=CD[d][:, n:n + 1])
                yield
                if delta:
                    b2 = C.bank()
                    C.mm(C.banks[b2][0:64, 0:128], TTs[:, 2 * n + d, :], vs[:], True, True, [("TT", n, d), vsr], [("ps", b2)])
                    yield
                    C.cp("act", vn[:], C.banks[b2][0:64, 0:128], [("ps", b2)], [vnr])
                    yield
                    C.ts("dve", vs[:], C.banks[b2][0:64, 0:128], EKG[d][:, n:n + 1], None, ALU.mult, None, [("ps", b2), ("EKG", d)], [vsr])
                    yield
                    vnap = vn[:]
                else:
                    vnap = vtok[:, n, :]
                    vnr = "vtok"
                b4 = C.bank()
                C.mm(C.banks[b4][0:64, 0:128], QKTs[:, 2 * n + d, :], vnap, True, True, [("QKT", n, d), vnr], [("ps", b4)])
                b5 = C.bank()
                C.mm(C.banks[b5][:, 0:128], ktok[:, n, :], vs[:], True, True, ["ktok", vsr], [("ps", b5)])
                yield
                C.tt("dve", oacc[:, n, :], C.banks[b4][0:64, 0:128], oacc[:, n, :], ALU.add, [ores, ("ps", b4)], [ores])
                C.tt("dve", Sd, C.banks[b5][:, 0:128], Sd, ALU.add, [Sr, ("ps", b5)], [Sr])

            for step in range(NCH):
                gens = [scan_step(step, 0), scan_step(step, 1)]
                while gens:
                    nxt = []
                    for g in gens:
                        try:
                            next(g)
                            nxt.append(g)
                        except StopIteration:
                            pass
                    gens = nxt
            if EV_STOP <= 6:
                return
            ores_all = [("oacc", n) for n in range(NCH)]
            for n0 in range(0, NCH, 4):
                sl = oacc[:, n0:n0 + 4, :].rearrange("p a b -> p (a b)")
                C.act(tmp[0:64, :], sl, AF.Square, ores_all[n0:n0 + 4], ["tmp"])
                P.op("dve", (lambda o, i: (lambda e: e.reduce_sum(out=o, in_=i, axis=AX.X)))(
                    st[:, n0:n0 + 4], tmp[0:64, :].rearrange("p (a b) -> p a b", b=128)), ["tmp"], ["st"])
                P.op("dve", (lambda o, i: (lambda e: e.reduce_sum(out=o, in_=i, axis=AX.X)))(
                    st[:, NCH + n0:NCH + n0 + 4], oacc[:, n0:n0 + 4, :]), ores_all[n0:n0 + 4], ["st"])
            ssq = st[:, 0:NCH]
            ssum = st[:, NCH:2 * NCH]
            rstd = st[:, 2 * NCH:3 * NCH]
            mean = st[:, 3 * NCH:4 * NCH]
            C.ts("dve", mean, ssum, 1.0 / 128.0, None, ALU.mult, None, ["st"], ["st"])
            if delta:
                C.ts("dve", rstd, ssq, 1.0 / 128.0, NORM_EPS, ALU.mult, ALU.add, ["st"], ["st"])
            else:
                C.tt("dve", rstd, mean, mean, ALU.mult, ["st"], ["st"])
                C.stt("dve", rstd, ssq, 1.0 / 128.0, rstd, ALU.mult, ALU.subtract, ["st"], ["st"])
                C.ts("dve", rstd, rstd, NORM_EPS, None, ALU.add, None, ["st"], ["st"])
            C.act(rstd, rstd, AF.Sqrt, ["st"], ["st"])
            P.op("dve", (lambda o: (lambda e: e.reciprocal(out=o, in_=o)))(rstd), ["st"], ["st"])
            for n in range(NCH):
                if delta:
                    C.ts("dve", oacc[:, n, :], oacc[:, n, :], rstd[:, n:n + 1], None, ALU.mult, None,
                         [("oacc", n), "st"], [("oacc", n)])
                else:
                    C.ts("dve", oacc[:, n, :], oacc[:, n, :], mean[:, n:n + 1], rstd[:, n:n + 1], ALU.subtract, ALU.mult,
                         [("oacc", n), "st"], [("oacc", n)])
            for n0 in range(0, NCH, 8):
                nn = min(8, NCH - n0)
                bk = C.bank()
                for q in range(nn):
                    C.tr(C.banks[bk][:, q * 64:(q + 1) * 64], oacc[:, n0 + q, :], I64, [("oacc", n0 + q), "mk"], [("ps", bk)])
                t0 = n0 * 64
                tn = nn * 64
                if delta:
                    C.stt("dve", yT[:, t0:t0 + tn], C.banks[bk][:, 0:tn], nw[:, 0:1], szT[:, t0:t0 + tn], ALU.mult, ALU.mult,
                          [("ps", bk), "nw", "szT"], ["yT"])
                else:
                    C.tt("dve", yT[:, t0:t0 + tn], C.banks[bk][:, 0:tn], szT[:, t0:t0 + tn], ALU.mult, [("ps", bk), "szT"], ["yT"])
            C.ld(C.dram["YT"][b, hd * 128:(hd + 1) * 128, :], yT[:], ["yT"], [("Yw", b, hd)])


def build(stages, dbg=()):
    nc = bass.Bass("TRN2", target_bir_lowering=False)
    C = Ctx(nc)
    I = {}

    def inp(name, shape):
        I[name] = nc.dram_tensor(name, list(shape), F32, kind="ExternalInput").ap()

    inp("x", (NB, L, D)); inp("ctx", (NB, CT, D)); inp("cvec", (3, D))
    inp("mod_w", (2, D, 6 * D)); inp("mod_b", (2, 6 * D)); inp("ln_g", (2, 2, D)); inp("ln_b", (2, 2, D))
    inp("e_w_in", (1, D, EVEN_IN)); inp("e_conv", (1, 3, 1536)); inp("e_a_log", (1, 2, 4)); inp("e_dt_bias", (1, 2, 4))
    inp("e_norm_w", (1, 128)); inp("e_ret_decay", (1, 2, 4)); inp("e_w_out", (1, D, D))
    inp("o_w_qkv", (1, D, 3 * D)); inp("o_lambda", (1, 4, 64)); inp("o_subln_w", (1, 128)); inp("o_w_out", (1, D, D))
    inp("f_w_gate", (2, D, DFF)); inp("f_w_up", (2, D, DFF)); inp("f_conv", (2, 3, 3, DFF)); inp("f_w_down", (2, DFF, D))
    inp("ident", (128, 128)); inp("rope2_cos", (128, L)); inp("rope2_sin", (128, L))
    inp("rope1_cos", (128, L)); inp("rope1_sin", (128, L)); inp("masks", (8, 64, 64))
    I["out"] = nc.dram_tensor("out", [NB, L, D], F32, kind="ExternalOutput").ap()
    C.I = I
    def kind_of(nm):
        if nm in dbg:
            return "ExternalOutput"
        if nm + "in" in dbg:
            return "ExternalInput"
        return "Internal"

    C.dram["modv"] = nc.dram_tensor("modv", [2, 3, 6 * D], F32, kind=kind_of("modv")).ap()
    for nm in ("XA", "XB"):
        C.dram[nm] = nc.dram_tensor(nm, [NB, T, D], F32, kind=kind_of(nm)).ap()
    C.dram["HT"] = nc.dram_tensor("HT", [NB, D, T], BF16, kind=kind_of("HT")).ap()
    C.dram["YT"] = nc.dram_tensor("YT", [NB, D, T], BF16, kind=kind_of("YT")).ap()
    C.dram["GT"] = nc.dram_tensor("GT", [NB, DFF, T], BF16, kind=kind_of("GT")).ap()
    C.dram["LAD"] = nc.dram_tensor("LAD", [NB, 24, T], F32, kind=kind_of("LAD")).ap()
    S = stages
    if "mod" in S:
        stage_mod(C)
    if "modT0" in S:
        stage_modT0(C)
    if "even" in S:
        stage_even(C)
    if "pa0" in S:
        stage_projln(C, 0, "a", C.dram["YT"], 8, I["e_w_out"][0], "input", "XA", NT, (0, "f"))
    if "ffn0" in S:
        stage_ffn1(C, 0, True)
    if "pf0" in S:
        stage_projln(C, 0, "f", C.dram["GT"], 22, I["f_w_down"][0], "XA", "XB", NT, (1, "a"))
    if "odd" in S:
        stage_odd(C)
    if "pa1" in S:
        stage_projln(C, 1, "a", C.dram["YT"], 8, I["o_w_out"][0], "XB", "XA", 16, (1, "f"))
    if "ffn1" in S:
        stage_ffn1(C, 1, False)
    if "pf1" in S:
        stage_projln(C, 1, "f", C.dram["GT"], 22, I["f_w_down"][1], "XA", "out", 16, None)
    C.P.emit()
    return nc


def make_consts():
    c = {}
    c["ident"] = np.eye(128, dtype=np.float32)
    pos = np.arange(L)
    inv2 = 10000.0 ** (-np.arange(16, dtype=np.float64) / 16)
    ang2 = np.concatenate([(pos // 64)[:, None] * inv2[None], (pos % 64)[:, None] * inv2[None]], -1)
    idx2 = (np.arange(128) % 64) % 32
    c["rope2_cos"] = np.ascontiguousarray(np.cos(ang2)[:, idx2].T.astype(np.float32))
    c["rope2_sin"] = np.ascontiguousarray(np.sin(ang2)[:, idx2].T.astype(np.float32))
    inv1 = 10000.0 ** (-np.arange(64, dtype=np.float64) / 64)
    ang1 = pos[:, None] * inv1[None]
    idx1 = np.arange(128) % 64
    c["rope1_cos"] = np.ascontiguousarray(np.cos(ang1)[:, idx1].T.astype(np.float32))
    c["rope1_sin"] = np.ascontiguousarray(np.sin(ang1)[:, idx1].T.astype(np.float32))
    c["masks"] = make_masks()
    return c


def make_masks():
    m = np.zeros((8, 64, 64), np.float32)
    a = np.arange(64)
    le = (a[:, None] <= a[None, :]).astype(np.float32)
    m[0] = le
    m[1] = le.T
    m[2] = np.where(a[None, :] <= a[:, None], 0.0, -NEG)
    m[3] = np.where(a[None, :] >= a[:, None], 0.0, -NEG)
    m[4] = (a[None, :] < a[:, None]).astype(np.float32)
    m[5] = (a[None, :] > a[:, None]).astype(np.float32)
    m[6] = np.eye(64, dtype=np.float32)
    m[7] = 1.0
    return m


ALL_STAGES = ["mod", "modT0", "even", "pa0", "ffn0", "pf0", "odd", "pa1", "ffn1", "pf1"]
_NC_CACHE = {}


def kernel(**inputs):
    n = 8
    if "nc" not in _NC_CACHE:
        _NC_CACHE["nc"] = build(ALL_STAGES)
    nc = _NC_CACHE["nc"]
    consts = make_consts()
    shared = {k: np.ascontiguousarray(np.asarray(v, dtype=np.float32)) for k, v in inputs.items()
              if k not in ("x", "c", "ctx", "c_ctx")}
    x = np.asarray(inputs["x"], dtype=np.float32)
    ctx = np.asarray(inputs["ctx"], dtype=np.float32)
    c = np.asarray(inputs["c"], dtype=np.float32)
    c_ctx = np.asarray(inputs["c_ctx"], dtype=np.float32)
    in_maps = []
    for i in range(n):
        m = dict(shared)
        m.update(consts)
        m["x"] = np.ascontiguousarray(x[NB * i:NB * (i + 1)])
        m["ctx"] = np.ascontiguousarray(ctx[NB * i:NB * (i + 1)])
        m["cvec"] = np.ascontiguousarray(np.stack([c[NB * i], c[NB * i + 1], c_ctx]))
        in_maps.append(m)
    res = run_bass_kernel_spmd(nc, in_maps, core_ids=list(range(n)))
    return np.concatenate([r["out"] for r in res.results], axis=0).astype(np.float32)
```

```python
import math
from contextlib import ExitStack

import numpy as np
import concourse.bass as bass
import concourse.mybir as mybir
from concourse.bass_utils import run_bass_kernel_spmd

F32 = mybir.dt.float32
BF16 = mybir.dt.bfloat16
AF = mybir.ActivationFunctionType
ALU = mybir.AluOpType
AX = mybir.AxisListType

SAME_ENGINE_SYNC = True
N_DMA_SEMS = 24
N_HW_SEMS = 16

NB = 2
L = 2048
CT = 256
T = L + CT
D = 1024
DFF = 2816
NT = T // 128
ALPHA = 4.0 ** 0.25
LN_EPS = 1e-5
NORM_EPS = 1e-6
LAMBDA_INIT = 0.8 - 0.6 * math.exp(-0.3)
EVEN_IN = 4112
NEG = -30000.0
SB_BASE = 16640


class Op:
    __slots__ = ("eng", "fn", "deps", "signal", "count", "dma", "sem", "target", "idx")


class Prog:
    ENGS = ("pe", "act", "dve", "pool", "sp")

    def __init__(self, nc):
        self.nc = nc
        self.ops = []
        self.last_w = {}
        self.readers = {}
        self.dma_rr = 0
        self.dma_rr_sw = 0
        self.dma_sem_total = [0] * N_DMA_SEMS
        self.dma_sem_lastop = [None] * N_DMA_SEMS
        self.last_eng = {}
        self.barrier_deps = []

    def barrier(self):
        deps = list(self.last_eng.values())
        deps += [o for o in self.dma_sem_lastop if o is not None]
        self.barrier_deps = deps
        self.last_w = {}
        self.readers = {}

    def _add(self, eng, fn, reads, writes, dma):
        op = Op()
        op.eng = eng
        op.fn = fn
        op.signal = False
        op.count = 0
        op.dma = dma
        op.sem = None
        op.target = 0
        op.idx = len(self.ops)
        deps = {}
        ps_reads = [r for r in reads if isinstance(r, tuple) and r and r[0] == "ps"]
        if ps_reads:
            writes = list(writes) + [r for r in ps_reads if r not in writes]
        for d in self.barrier_deps:
            deps[d.idx] = d
        for r in reads:
            w = self.last_w.get(r)
            if w is not None:
                deps[w.idx] = w
        for r in writes:
            w = self.last_w.get(r)
            if w is not None:
                deps[w.idx] = w
            for rd in self.readers.get(r, ()):
                deps[rd.idx] = rd
        if dma:
            if eng == "pool":
                i = N_HW_SEMS + self.dma_rr_sw
                self.dma_rr_sw = (self.dma_rr_sw + 1) % (N_DMA_SEMS - N_HW_SEMS)
            else:
                i = self.dma_rr
                self.dma_rr = (self.dma_rr + 1) % N_HW_SEMS
            prev = self.dma_sem_lastop[i]
            if prev is not None:
                deps[prev.idx] = prev
            self.dma_sem_total[i] += 16
            op.sem = i
            op.target = self.dma_sem_total[i]
            self.dma_sem_lastop[i] = op
        else:
            self.last_eng[eng] = op
        op.deps = list(deps.values())
        for r in reads:
            self.readers.setdefault(r, []).append(op)
        for r in writes:
            self.last_w[r] = op
            self.readers[r] = []
        self.ops.append(op)
        return op

    def op(self, eng, fn, reads=(), writes=()):
        return self._add(eng, fn, reads, writes, False)

    def dma(self, eng, fn, reads=(), writes=()):
        return self._add(eng, fn, reads, writes, True)

    @staticmethod
    def _skip(d, op):
        return d.eng == op.eng and (d.eng == "pe" or not SAME_ENGINE_SYNC) and not op.dma

    def emit(self):
        nc = self.nc
        for op in self.ops:
            for d in op.deps:
                if d.dma or self._skip(d, op):
                    continue
                d.signal = True
        cnt = {e: 0 for e in self.ENGS}
        for op in self.ops:
            if op.dma:
                continue
            if op.signal:
                cnt[op.eng] += 1
                op.count = cnt[op.eng]
        streams = {e: [o for o in self.ops if o.eng == e] for e in self.ENGS}
        with ExitStack() as es:
            esem = {e: es.enter_context(nc.semaphore("s_" + e)) for e in self.ENGS}
            dsem = [es.enter_context(nc.semaphore("d%d" % i)) for i in range(N_DMA_SEMS)]
            block = es.enter_context(nc.Block())
            all_dma_final = [(dsem[i], self.dma_sem_total[i]) for i in range(N_DMA_SEMS)
                             if self.dma_sem_total[i] > 0]

            def run_stream(e, eng):
                waited = {}
                for op in streams[e]:
                    for d in op.deps:
                        if d.dma:
                            key, val, sem = ("d", d.sem), d.target, dsem[d.sem]
                        else:
                            if self._skip(d, op):
                                continue
                            key, val, sem = ("e", d.eng), d.count, esem[d.eng]
                        if waited.get(key, 0) >= val:
                            continue
                        waited[key] = val
                        eng.wait_ge(sem, val)
                    ins = op.fn(eng)
                    if op.dma:
                        ins.then_inc(dsem[op.sem], 16)
                    elif op.signal:
                        ins.then_inc(esem[e], 1)
                if e == "sp":
                    for sem, val in all_dma_final:
                        eng.wait_ge(sem, val)

            @block.tensor
            def _(eng):
                run_stream("pe", eng)

            @block.scalar
            def _(eng):
                run_stream("act", eng)

            @block.vector
            def _(eng):
                run_stream("dve", eng)

            @block.gpsimd
            def _(eng):
                run_stream("pool", eng)

            @block.sync
            def _(eng):
                run_stream("sp", eng)


class Ctx:
    def __init__(self, nc):
        self.nc = nc
        self.P = Prog(nc)
        self.sb_off = SB_BASE
        self.sb_n = 0
        self.banks = [nc.alloc_psum_tensor("bank%d" % i, [128, 512], F32) for i in range(8)]
        self.bank_rr = 0
        self.dram = {}

    def sb_reset(self):
        self.P.barrier()
        self.sb_off = SB_BASE

    def sb(self, name, shape, dt):
        nbytes = int(np.prod(shape[1:])) * (2 if dt == BF16 else 4)
        nbytes = (nbytes + 63) // 64 * 64
        self.sb_n += 1
        t = self.nc.alloc_sbuf_tensor_at("%s_%d" % (name, self.sb_n), list(shape), dt, offset=self.sb_off)
        self.sb_off += nbytes
        assert self.sb_off <= 229376, (name, self.sb_off)
        return t

    def sb_at(self, name, shape, dt, off):
        self.sb_n += 1
        return self.nc.alloc_sbuf_tensor_at("%s_%d" % (name, self.sb_n), list(shape), dt, offset=off)

    def bank(self):
        i = self.bank_rr
        self.bank_rr = (self.bank_rr + 1) % 8
        return i

    def mm(self, out, lhsT, rhs, start, stop, reads, writes):
        self.P.op("pe", lambda e: e.matmul(out, lhsT=lhsT, rhs=rhs, start=start, stop=stop), reads, writes)

    def tr(self, out, in_, ident, reads, writes):
        self.P.op("pe", lambda e: e.transpose(out, in_, ident), reads, writes)

    def act(self, out, in_, func, reads, writes, bias=None, scale=None):
        kw = {}
        if bias is not None:
            kw["bias"] = bias
        if scale is not None:
            kw["scale"] = scale
        self.P.op("act", lambda e: e.activation(out=out, in_=in_, func=func, **kw), reads, writes)

    def tt(self, eng, out, in0, in1, op, reads, writes):
        self.P.op(eng, lambda e: e.tensor_tensor(out=out, in0=in0, in1=in1, op=op), reads, writes)

    def ts(self, eng, out, in0, s1, s2, op0, op1, reads, writes):
        if op1 is None:
            self.P.op(eng, lambda e: e.tensor_scalar(out=out, in0=in0, scalar1=s1, scalar2=None, op0=op0), reads, writes)
        else:
            self.P.op(eng, lambda e: e.tensor_scalar(out=out, in0=in0, scalar1=s1, scalar2=s2, op0=op0, op1=op1), reads, writes)

    def stt(self, eng, out, in0, scalar, in1, op0, op1, reads, writes):
        self.P.op(eng, lambda e: e.scalar_tensor_tensor(out=out, in0=in0, scalar=scalar, in1=in1, op0=op0, op1=op1), reads, writes)

    def cp(self, eng, out, in_, reads, writes):
        if eng == "act":
            self.P.op("act", lambda e: e.copy(out=out, in_=in_), reads, writes)
        else:
            self.P.op(eng, lambda e: e.tensor_copy(out=out, in_=in_), reads, writes)

    def ld(self, out, in_, reads, writes, q="sp", slow=False):
        if slow:
            self.P.dma(q, lambda e: e.dma_start(out=out, in_=in_, allow_slow_non_contiguous=True), reads, writes)
        else:
            self.P.dma(q, lambda e: e.dma_start(out=out, in_=in_), reads, writes)


def pp_view(vec_ap):
    return vec_ap.rearrange("(k p) -> p k", p=128)


def stage_mod(C):
    C.sb_reset()
    I = C.I
    cT = C.sb("cT", [128, 8, 3], F32)
    for r in range(3):
        for k0 in range(0, 8, 4):
            C.ld(cT[:, k0:k0 + 4, r], pp_view(I["cvec"][r])[:, k0:k0 + 4], [], ["cT"], slow=True)
    C.act(cT[:], cT[:], AF.Silu, ["cT"], ["cT"])
    wt = [C.sb("modw%d" % i, [128, 8, 512], F32) for i in range(2)]
    mb = C.sb("modb", [3, 6144], F32)
    res = C.sb("modres", [3, 6144], F32)
    n = 0
    for li in range(2):
        C.ld(mb[:], I["mod_b"][li].partition_broadcast(3), [], ["mb"])
        wv = I["mod_w"][li].rearrange("(k p) n -> p k n", p=128)
        for j in range(12):
            w = wt[n % 2]
            wk = ("modw", n % 2)
            n += 1
            C.ld(w[:], wv[:, :, j * 512:(j + 1) * 512], [], [wk])
            bk = C.bank()
            for k in range(8):
                C.mm(C.banks[bk][0:3, :], cT[:, k, :], w[:, k, :], k == 0, k == 7, ["cT", wk], [("ps", bk)])
            C.tt("dve", res[:, j * 512:(j + 1) * 512], C.banks[bk][0:3, :], mb[:, j * 512:(j + 1) * 512], ALU.add,
                 [("ps", bk), "mb"], ["modres"])
        C.ld(C.dram["modv"][li], res[:], ["modres"], [("modv", li)])


def load_mod_pp(C, li, r, which, name):
    base = 0 if which == "a" else 3 * D
    sh = C.sb(name + "sh", [128, 8], F32)
    sc = C.sb(name + "sc", [128, 8], F32)
    mv = C.dram["modv"][li, r]
    for k0 in range(0, 8, 4):
        C.ld(sh[:, k0:k0 + 4], pp_view(mv[base:base + D])[:, k0:k0 + 4], [("modv", li)], [name + "sh"], slow=True)
        C.ld(sc[:, k0:k0 + 4], pp_view(mv[base + D:base + 2 * D])[:, k0:k0 + 4], [("modv", li)], [name + "sc"], slow=True)
    C.ts("dve", sc[:], sc[:], 1.0, None, ALU.add, None, [name + "sc"], [name + "sc"])
    return sh, sc


def emit_modT(C, xt, xres, sh, sc, mres, hdst, hres, tag, slot):
    ht = C.hT_tiles[slot]
    hk = ("hTt", slot)
    for half in range(2):
        bk = C.bank()
        for q in range(4):
            k = half * 4 + q
            C.tr(C.banks[bk][:, q * 128:(q + 1) * 128], xt[:, k * 128:(k + 1) * 128], C.ident[:], [xres, "ident"], [("ps", bk)])
        for q in range(4):
            k = half * 4 + q
            C.act(ht[:, k, :], C.banks[bk][:, q * 128:(q + 1) * 128], AF.Identity, [("ps", bk)] + mres, [hk],
                  bias=sh[:, k:k + 1], scale=sc[:, k:k + 1])
    C.ld(hdst, ht[:], [hk], [hres])


def x_in_ap(C, src, b, t):
    if src == "input":
        if t < 16:
            return C.I["x"][b, t * 128:(t + 1) * 128, :]
        return C.I["ctx"][b, (t - 16) * 128:(t - 15) * 128, :]
    return C.dram[src][b, t * 128:(t + 1) * 128, :]


def hT_dst(C, b, t):
    return C.dram["HT"][b].rearrange("(k p) t -> p k t", p=128)[:, :, t * 128:(t + 1) * 128]


def stage_modT0(C):
    C.sb_reset()
    C.ident = C.sb("ident", [128, 128], F32)
    C.ld(C.ident[:], C.I["ident"], [], ["ident"])
    C.hT_tiles = [C.sb("hTt%d" % i, [128, 8, 128], BF16) for i in range(2)]
    xts = [C.sb("xt%d" % i, [128, D], F32) for i in range(3)]
    n = 0
    for b in range(NB):
        sh, sc = load_mod_pp(C, 0, b, "a", "m%d" % b)
        shc, scc = load_mod_pp(C, 0, 2, "a", "mc%d" % b)
        for t in range(NT):
            xt = xts[n % 3]
            xk = ("xt", n % 3)
            C.ld(xt[:], x_in_ap(C, "input", b, t), [], [xk])
            if t < 16:
                emit_modT(C, xt, xk, sh, sc, ["m%dsh" % b, "m%dsc" % b], hT_dst(C, b, t), ("HT", b, t), "m0", n % 2)
            else:
                emit_modT(C, xt, xk, shc, scc, ["mc%dsh" % b, "mc%dsc" % b], hT_dst(C, b, t), ("HT", b, t), "m0", n % 2)
            n += 1


def stage_projln(C, li, which, ysrc, KC, W, xsrc, xdst, ntiles, nxt):
    C.sb_reset()
    C.ident = C.sb("ident", [128, 128], F32)
    C.ld(C.ident[:], C.I["ident"], [], ["ident"])
    C.hT_tiles = [C.sb("hTt%d" % i, [128, 8, 128], BF16) for i in range(2)]
    wsb = C.sb("wsb", [128, KC, D], BF16)
    wv = W.rearrange("(k p) n -> p k n", p=128)
    for k0 in range(0, KC, 2):
        k1 = min(KC, k0 + 2)
        C.ld(wsb[:, k0:k1, :], wv[:, k0:k1, :], [], [("wsb", k0)], q="pool")
    wres = [("wsb", k0) for k0 in range(0, KC, 2)]
    lnrow = 0 if which == "a" else 1
    lng = C.sb("lng", [128, D], F32)
    lnb = C.sb("lnb", [128, D], F32)
    C.ld(lng[:], C.I["ln_g"][li, lnrow].partition_broadcast(128), [], ["lng"])
    C.ld(lnb[:], C.I["ln_b"][li, lnrow].partition_broadcast(128), [], ["lnb"])
    goff = 2 * D if which == "a" else 5 * D
    gates = {}
    for r in ([0, 1, 2] if ntiles > 16 else [0, 1]):
        g = C.sb("gate%d" % r, [128, D], F32)
        C.ld(g[:], C.dram["modv"][li, r, goff:goff + D].partition_broadcast(128), [("modv", li)], [("gate", r)])
        gates[r] = g
    mods = {}
    if nxt is not None:
        for r in ([0, 1, 2] if ntiles > 16 else [0, 1]):
            mods[r] = load_mod_pp(C, nxt[0], r, nxt[1], "nm%d" % r)
    yts = [C.sb("yt%d" % i, [128, KC, 512], BF16) for i in range(2)]
    xts = [C.sb("xt%d" % i, [128, D], F32) for i in range(2)]
    zs = [C.sb("z%d" % i, [128, D], F32) for i in range(2)]
    xos = [C.sb("xo%d" % i, [128, D], F32) for i in range(2)]
    st = [C.sb("st%d" % i, [128, 2, 6], F32) for i in range(2)]
    mv = [C.sb("mv%d" % i, [128, 4], F32) for i in range(2)]
    n = 0
    ng = 0
    seq = [(b_, t_) for b_ in range(NB) for t_ in range(ntiles)]

    def emit_xload(idx):
        b_, t_ = seq[idx]
        C.ld(xts[idx % 2][:], x_in_ap(C, xsrc, b_, t_), [("X", xsrc, b_)], [("xt", idx % 2)])

    for b in range(NB):
        yv = ysrc[b].rearrange("(k p) t -> p k t", p=128)
        for g0 in range(0, ntiles, 4):
            g1 = min(ntiles, g0 + 4)
            yt = yts[ng % 2]
            yk = ("yt", ng % 2)
            ng += 1
            ntok = (g1 - g0) * 128
            for k0 in range(0, KC, 8):
                k1 = min(KC, k0 + 8)
                C.ld(yt[:, k0:k1, 0:ntok], yv[:, k0:k1, g0 * 128:g1 * 128], [("Y", b)], [yk])
            for t in range(g0, g1):
                s = n % 2
                n += 1
                r = b if t < 16 else 2
                xt, z, xo = xts[s], zs[s], xos[s]
                xk, zk, xok, stk = ("xt", s), ("z", s), ("xo", s), ("st", s)
                if n == 1:
                    emit_xload(0)
                bks = [C.bank(), C.bank()]
                for nn in range(2):
                    for k in range(KC):
                        C.mm(C.banks[bks[nn]][:, :], yt[:, k, (t - g0) * 128:(t - g0 + 1) * 128],
                             wsb[:, k, nn * 512:(nn + 1) * 512], k == 0, k == KC - 1, [yk] + wres, [("ps", bks[nn])])
                if n < len(seq):
                    emit_xload(n)
                for nn in range(2):
                    C.tt("dve", z[:, nn * 512:(nn + 1) * 512], C.banks[bks[nn]][:, :], gates[r][:, nn * 512:(nn + 1) * 512],
                         ALU.mult, [("ps", bks[nn]), ("gate", r)], [zk])
                C.stt("dve", z[:], xt[:], ALPHA, z[:], ALU.mult, ALU.add, [xk, zk], [zk])
                for nn in range(2):
                    C.P.op("dve", (lambda o, i: (lambda e: e.bn_stats(out=o, in_=i)))(st[s][:, nn, :], z[:, nn * 512:(nn + 1) * 512]),
                           [zk], [stk])
                C.P.op("dve", (lambda o, i: (lambda e: e.bn_aggr(out=o, in_=i)))(mv[s][:, 0:2], st[s][:].rearrange("p a b -> p (a b)")),
                       [stk], [stk])
                C.ts("dve", mv[s][:, 2:3], mv[s][:, 1:2], LN_EPS, None, ALU.add, None, [stk], [stk])
                C.act(mv[s][:, 2:3], mv[s][:, 2:3], AF.Sqrt, [stk], [stk])
                C.P.op("dve", (lambda o: (lambda e: e.reciprocal(out=o, in_=o)))(mv[s][:, 2:3]), [stk], [stk])
                C.stt("dve", mv[s][:, 3:4], mv[s][:, 0:1], -1.0, mv[s][:, 2:3], ALU.mult, ALU.mult, [stk], [stk])
                C.act(xo[:], z[:], AF.Identity, [zk, stk], [xok], bias=mv[s][:, 3:4], scale=mv[s][:, 2:3])
                C.tt("dve", xo[:], xo[:], lng[:], ALU.mult, [xok, "lng"], [xok])
                C.tt("dve", xo[:], xo[:], lnb[:], ALU.add, [xok, "lnb"], [xok])
                if xdst == "out":
                    C.ld(C.I["out"][b, t * 128:(t + 1) * 128, :], xo[:], [xok], [("OUT", b, t)])
                else:
                    C.ld(C.dram[xdst][b, t * 128:(t + 1) * 128, :], xo[:], [xok], [("Xw", xdst, b, t)])
                if nxt is not None:
                    sh, sc = mods[r]
                    emit_modT(C, xo, xok, sh, sc, ["nm%dsh" % r, "nm%dsc" % r], hT_dst(C, b, t), ("HT", b, t), "pl", s)


def stage_ffn1(C, li, with_ctx):
    C.sb_reset()
    ntok = T if with_ctx else L
    hsb = C.sb("hsb", [128, 8, T], BF16)
    cw = C.sb("cw", [128, 22, 9], F32)
    for i in range(3):
        for j in range(3):
            for c0 in range(0, 22, 4):
                c1 = min(22, c0 + 4)
                C.ld(cw[:, c0:c1, 3 * i + j], C.I["f_conv"][li, i, j].rearrange("(c p) -> p c", p=128)[:, c0:c1], [], ["cw"], slow=True)
    identf = C.sb("identf", [128, 128], F32)
    identb = C.sb("identb", [128, 128], BF16)
    C.ld(identf[:], C.I["ident"], [], ["identf"])
    C.cp("dve", identb[:], identf[:], ["identf"], ["identb"])
    wgs = [C.sb("wg%d" % i, [128, 8, 128], BF16) for i in range(2)]
    wus = [C.sb("wu%d" % i, [128, 8, 128], BF16) for i in range(2)]
    dgs = [C.sb("dg%d" % i, [128, 9, 128], BF16) for i in range(2)]
    gpads = [C.sb("gpad%d" % i, [128, 34, 66], BF16) for i in range(2)]
    gctxs = [C.sb("gctx%d" % i, [128, CT + 2], BF16) for i in range(2)]
    us = [C.sb("u%d" % i, [128, T], F32) for i in range(2)]
    svs = [C.sb("sv%d" % i, [128, T], F32) for i in range(2)]
    gbs = [C.sb("gb%d" % i, [128, T], BF16) for i in range(2)]
    for i in range(2):
        C.P.op("pool", (lambda o: (lambda e: e.memset(o, 0.0)))(gpads[i][:].rearrange("p a b -> p (a b)")), [], [("gpad", i)])
        C.P.op("pool", (lambda o: (lambda e: e.memset(o, 0.0)))(gctxs[i][:]), [], [("gctx", i)])
    wgv = C.I["f_w_gate"][li].rearrange("(k p) n -> p k n", p=128)
    wuv = C.I["f_w_up"][li].rearrange("(k p) n -> p k n", p=128)
    n = 0
    for b in range(NB):
        hv = C.dram["HT"][b].rearrange("(k p) t -> p k t", p=128)
        for k in range(8):
            C.ld(hsb[:, k, 0:ntok], hv[:, k, 0:ntok], [("HT", b)], ["hsb"])
        for c in range(22):
            s = n % 2
            n += 1
            wg, wu, dg, gpad, gctx, u, sv, gb = wgs[s], wus[s], dgs[s], gpads[s], gctxs[s], us[s], svs[s], gbs[s]
            C.ld(wg[:], wgv[:, :, c * 128:(c + 1) * 128], [], [("wg", s)], q="pool")
            C.ld(wu[:], wuv[:, :, c * 128:(c + 1) * 128], [], [("wu", s)], q="pool")
            for tap in range(9):
                C.ts("dve", dg[:, tap, :], identb[:], cw[:, c, tap:tap + 1], None, ALU.mult, None, ["identb", "cw"], [("dg", s)])
            for tt_ in range(4):
                bk = C.bank()
                for k in range(8):
                    C.mm(C.banks[bk][:, :], wg[:, k, :], hsb[:, k, tt_ * 512:(tt_ + 1) * 512], k == 0, k == 7, ["hsb", ("wg", s)], [("ps", bk)])
                C.cp("act", gpad[:, 1 + 8 * tt_:9 + 8 * tt_, 1:65], C.banks[bk][:, :].rearrange("p (r w) -> p r w", w=64),
                     [("ps", bk)], [("gpad", s)])
            if with_ctx:
                bk = C.bank()
                for k in range(8):
                    C.mm(C.banks[bk][:, 0:CT], wg[:, k, :], hsb[:, k, L:T], k == 0, k == 7, ["hsb", ("wg", s)], [("ps", bk)])
                C.cp("act", gctx[:, 1:CT + 1], C.banks[bk][:, 0:CT], [("ps", bk)], [("gctx", s)])
            tiles = [(i * 512, 512) for i in range(4)] + ([(L, CT)] if with_ctx else [])
            for (t0, tn) in tiles:
                bk = C.bank()
                for k in range(8):
                    C.mm(C.banks[bk][:, 0:tn], wu[:, k, :], hsb[:, k, t0:t0 + tn], k == 0, k == 7, ["hsb", ("wu", s)], [("ps", bk)])
                C.cp("act", u[:, t0:t0 + tn], C.banks[bk][:, 0:tn], [("ps", bk)], [("u", s)])
            for tt_ in range(4):
                bk = C.bank()
                for tap in range(9):
                    i, j = tap // 3, tap % 3
                    C.mm(C.banks[bk][:, :], dg[:, tap, :], gpad[:, 8 * tt_ + i:8 * tt_ + i + 8, j:j + 64], tap == 0, tap == 8,
                         [("dg", s), ("gpad", s)], [("ps", bk)])
                C.act(sv[:, tt_ * 512:(tt_ + 1) * 512], C.banks[bk][:, :], AF.Silu, [("ps", bk)], [("sv", s)])
            if with_ctx:
                bk = C.bank()
                for j in range(3):
                    C.mm(C.banks[bk][:, 0:CT], dg[:, 3 + j, :], gctx[:, j:j + CT], j == 0, j == 2, [("dg", s), ("gctx", s)], [("ps", bk)])
                C.act(sv[:, L:T], C.banks[bk][:, 0:CT], AF.Silu, [("ps", bk)], [("sv", s)])
            C.tt("dve", gb[:, 0:ntok], sv[:, 0:ntok], u[:, 0:ntok], ALU.mult, [("sv", s), ("u", s)], [("gb", s)])
            C.ld(C.dram["GT"][b, c * 128:(c + 1) * 128, 0:ntok], gb[:, 0:ntok], [("gb", s)], [("G", b, c)])


def rot_weights(C, wr, w, hd_half, res_in, res_out):
    nblk = 128 // (2 * hd_half)
    for c in range(nblk):
        b0 = c * 2 * hd_half
        C.ts("pool", wr[:, :, b0:b0 + hd_half], w[:, :, b0 + hd_half:b0 + 2 * hd_half], -1.0, None, ALU.mult, None, [res_in], [res_out])
        C.cp("pool", wr[:, :, b0 + hd_half:b0 + 2 * hd_half], w[:, :, b0:b0 + hd_half], [res_in], [res_out])


def stage_odd(C):
    C.sb_reset()
    I = C.I
    hsb = C.sb("hsb", [128, 8, T], BF16)
    vsb = C.sb("vsb", [128, NT, D], BF16)
    cosT = C.sb("cosT", [128, L], F32)
    sinT = C.sb("sinT", [128, L], F32)
    C.ld(cosT[:], I["rope2_cos"], [], ["cosT"])
    C.ld(sinT[:], I["rope2_sin"], [], ["sinT"])
    ones_b = C.sb("ones_b", [128, 128], BF16)
    ones_f = C.sb("ones_f", [128, 128], F32)
    C.P.op("pool", lambda e: e.memset(ones_b[:], 1.0), [], ["ones_b"])
    C.P.op("pool", lambda e: e.memset(ones_f[:], 1.0), [], ["ones_f"])
    lpb = C.sb("lpb", [128, 4, 64], F32)
    C.ld(lpb[:].rearrange("p a b -> p (a b)"), I["o_lambda"][0].rearrange("a b -> (a b)").partition_broadcast(128), [], ["lpb"])
    lt = C.sb("lt", [128, 2, 64], F32)
    lam = C.sb("lam", [128, 4], F32)
    C.tt("dve", lt[:, 0, :], lpb[:, 0, :], lpb[:, 1, :], ALU.mult, ["lpb"], ["lt"])
    C.tt("dve", lt[:, 1, :], lpb[:, 2, :], lpb[:, 3, :], ALU.mult, ["lpb"], ["lt"])
    C.P.op("dve", lambda e: e.reduce_sum(out=lam[:, 0:2], in_=lt[:], axis=AX.X), ["lt"], ["lam"])
    C.act(lam[:, 0:2], lam[:, 0:2], AF.Exp, ["lam"], ["lam"])
    C.tt("dve", lam[:, 2:3], lam[:, 1:2], lam[:, 0:1], ALU.subtract, ["lam"], ["lam"])
    C.ts("dve", lam[:, 2:3], lam[:, 2:3], -LAMBDA_INIT, None, ALU.add, None, ["lam"], ["lam"])
    sw = C.sb("sw", [128, 1], F32)
    C.ld(sw[:], I["o_subln_w"][0].rearrange("(p o) -> p o", o=1), [], ["sw"], slow=True)
    C.ts("dve", sw[:], sw[:], 1.0 - LAMBDA_INIT, None, ALU.mult, None, ["sw"], ["sw"])
    wv_all = C.sb("wv_all", [128, 8, D], BF16)
    wqs = [C.sb("wq%d" % i, [128, 8, 128], BF16) for i in range(2)]
    wqr = [C.sb("wqr%d" % i, [128, 8, 128], BF16) for i in range(2)]
    wks = [C.sb("wk%d" % i, [128, 8, 128], BF16) for i in range(2)]
    wkr = [C.sb("wkr%d" % i, [128, 8, 128], BF16) for i in range(2)]
    qT = [C.sb("qT%d" % i, [128, L], BF16) for i in range(2)]
    kT = [C.sb("kT%d" % i, [128, T], BF16) for i in range(2)]
    t1 = [C.sb("t1_%d" % i, [128, 512], F32) for i in range(2)]
    t2 = [C.sb("t2_%d" % i, [128, 512], F32) for i in range(2)]
    pT = [C.sb("pT%d" % i, [128, 512], BF16) for i in range(6)]
    accs = [C.sb("acc%d" % i, [128, 512], F32) for i in range(2)]
    ep = [C.sb("ep%d" % i, [128, 512], F32) for i in range(5)]
    yb = [C.sb("yb%d" % i, [128, 512], BF16) for i in range(2)]
    wqkv = I["o_w_qkv"][0].rearrange("(k p) n -> p k n", p=128)
    npt = 0
    nrot = 0
    ny = 0
    for b in range(NB):
        hv = C.dram["HT"][b].rearrange("(k p) t -> p k t", p=128)
        for k in range(8):
            C.ld(hsb[:, k, :], hv[:, k, :], [("HT", b)], ["hsb"])
        for k0 in range(0, 8, 2):
            C.ld(wv_all[:, k0:k0 + 2, :], wqkv[:, k0:k0 + 2, 2 * D:3 * D], [], ["wv_all"], q="pool")
        for t in range(NT):
            for nn in range(2):
                bk = C.bank()
                for k in range(8):
                    C.mm(C.banks[bk][:, :], hsb[:, k, t * 128:(t + 1) * 128], wv_all[:, k, nn * 512:(nn + 1) * 512], k == 0, k == 7,
                         ["hsb", "wv_all"], [("ps", bk)])
                C.cp("act" if nn == 0 else "dve", vsb[:, t, nn * 512:(nn + 1) * 512], C.banks[bk][:, :], [("ps", bk)], ["vsb"])
        for hd in range(8):
            s = hd % 2
            C.ld(wqs[s][:], wqkv[:, :, hd * 128:(hd + 1) * 128], [], [("wq", s)], q="pool")
            C.ld(wks[s][:], wqkv[:, :, D + hd * 128:D + (hd + 1) * 128], [], [("wk", s)], q="pool")
            rot_weights(C, wqr[s], wqs[s], 32, ("wq", s), ("wqr", s))
            rot_weights(C, wkr[s], wks[s], 32, ("wk", s), ("wkr", s))
            for (w, wr, dst, dres, wres) in ((wqs[s], wqr[s], qT[s], ("qT", s), [("wq", s), ("wqr", s)]),
                                            (wks[s], wkr[s], kT[s], ("kT", s), [("wk", s), ("wkr", s)])):
                for tt_ in range(4):
                    t0 = tt_ * 512
                    b1, b2 = C.bank(), C.bank()
                    for k in range(8):
                        C.mm(C.banks[b1][:, :], w[:, k, :], hsb[:, k, t0:t0 + 512], k == 0, k == 7, ["hsb"] + wres, [("ps", b1)])
                    for k in range(8):
                        C.mm(C.banks[b2][:, :], wr[:, k, :], hsb[:, k, t0:t0 + 512], k == 0, k == 7, ["hsb"] + wres, [("ps", b2)])
                    r = nrot % 2
                    nrot += 1
                    C.tt("dve", t1[r][:], C.banks[b1][:, :], cosT[:, t0:t0 + 512], ALU.mult, [("ps", b1), "cosT"], [("t1", r)])
                    C.tt("dve", t2[r][:], C.banks[b2][:, :], sinT[:, t0:t0 + 512], ALU.mult, [("ps", b2), "sinT"], [("t2", r)])
                    C.tt("dve", dst[:, t0:t0 + 512], t1[r][:], t2[r][:], ALU.add, [("t1", r), ("t2", r)], [dres])
            bk = C.bank()
            for k in range(8):
                C.mm(C.banks[bk][:, 0:CT], wks[s][:, k, :], hsb[:, k, L:T], k == 0, k == 7, ["hsb", ("wk", s)], [("ps", bk)])
            C.cp("act", kT[s][:, L:T], C.banks[bk][:, 0:CT], [("ps", bk)], [("kT", s)])
            for qb in range(4):
                po = [0, 1]
                pss = [6, 7]

                def emit_S(kt, cs=(0, 1)):
                    for c in cs:
                        bs = 2 + (kt % 2) * 2 + c
                        C.mm(C.banks[bs][:, :], kT[s][c * 64:(c + 1) * 64, kt * 128:(kt + 1) * 128],
                             qT[s][c * 64:(c + 1) * 64, qb * 512:(qb + 1) * 512], True, True, [("kT", s), ("qT", s)], [("ps", bs)])

                emit_S(0, (0,))
                C.mm(C.banks[7][0:8, 0:8], ones_f[:, 0:8], ones_f[:, 0:8], True, True, ["ones_f"], [("ps", 7)])
                emit_S(0, (1,))
                for kt in range(NT):
                    pis = []
                    for c in range(2):
                        bs = 2 + (kt % 2) * 2 + c
                        pi = npt % 6
                        npt += 1
                        pis.append(pi)
                        C.act(pT[pi][:], C.banks[bs][:, :], AF.Exp, [("ps", bs)], [("pT", pi)], scale=0.125)
                    for c in range(2):
                        pi = pis[c]
                        if kt + 1 < NT:
                            emit_S(kt + 1, (c,))
                        C.mm(C.banks[po[c]][:, :], vsb[:, kt, hd * 128:(hd + 1) * 128], pT[pi][:], kt == 0, kt == NT - 1,
                             ["vsb", ("pT", pi)], [("ps", po[c])])
                    for c in range(2):
                        pi = pis[c]
                        if kt == 0:
                            C.cp("dve", accs[c][:], pT[pi][:], [("pT", pi)], [("acc", c)])
                        else:
                            C.tt("dve", accs[c][:], accs[c][:], pT[pi][:], ALU.add, [("pT", pi), ("acc", c)], [("acc", c)])
                for c in range(2):
                    C.mm(C.banks[pss[c]][:, :], ones_f[:], accs[c][:], True, True, ["ones_f", ("acc", c)], [("ps", pss[c])])
                e0, e1, e2, e3, e4 = ep
                C.P.op("dve", (lambda o, i: (lambda e: e.reciprocal(out=o, in_=i)))(e0[:], C.banks[pss[0]][:, :]), [("ps", pss[0])], ["e0"])
                C.tt("dve", e1[:], C.banks[po[0]][:, :], e0[:], ALU.mult, [("ps", po[0]), "e0"], ["e1"])
                C.P.op("dve", (lambda o, i: (lambda e: e.reciprocal(out=o, in_=i)))(e2[:], C.banks[pss[1]][:, :]), [("ps", pss[1])], ["e2"])
                C.tt("dve", e3[:], C.banks[po[1]][:, :], e2[:], ALU.mult, [("ps", po[1]), "e2"], ["e3"])
                C.stt("dve", e1[:], e3[:], lam[:, 2:3], e1[:], ALU.mult, ALU.add, ["e3", "e1", "lam"], ["e1"])
                C.act(e4[:], e1[:], AF.Square, ["e1"], ["e4"])
                bq = 2
                C.mm(C.banks[bq][:, :], ones_f[:], e4[:], True, True, ["ones_f", "e4"], [("ps", bq)])
                C.ts("dve", e0[:], C.banks[bq][:, :], 1.0 / 128.0, NORM_EPS, ALU.mult, ALU.add, [("ps", bq)], ["e0"])
                C.act(e0[:], e0[:], AF.Ln, ["e0"], ["e0"])
                C.act(e0[:], e0[:], AF.Exp, ["e0"], ["e0"], scale=-0.5)
                yi = ny % 2
                ny += 1
                C.stt("dve", yb[yi][:], e1[:], sw[:, 0:1], e0[:], ALU.mult, ALU.mult, ["e1", "e0", "sw"], [("yb", yi)])
                C.ld(C.dram["YT"][b, hd * 128:(hd + 1) * 128, qb * 512:(qb + 1) * 512], yb[yi][:], [("yb", yi)], [("Yw", b, hd, qb)])


NCH = T // 64
DK_SCALE = 128.0 ** -0.5


import os


def stage_even(C):
    EV_NB = int(os.environ.get('EV_NB', NB))
    EV_STOP = int(os.environ.get('EV_STOP', 99))
    EV_SUB = int(os.environ.get('EV_SUB', 99))
    EV_HEADS = [int(x) for x in os.environ.get('EV_HEADS', '0,1,2,3,4,5,6,7').split(',')]
    C.sb_reset()
    I = C.I
    P = C.P
    mk = C.sb("mk", [64, 8, 64], F32)
    for i in range(7):
        C.ld(mk[:, i, :], I["masks"][i], [], ["mk"])
    U = [mk[:, 0, :], mk[:, 1, :]]
    POS = [mk[:, 2, :], mk[:, 3, :]]
    STRICT = [mk[:, 4, :], mk[:, 5, :]]
    I64 = mk[:, 6, :]
    ident = C.sb("ident", [128, 128], F32)
    C.ld(ident[:], I["ident"], [], ["ident"])
    ones64 = C.sb("ones64", [64, 128], F32)
    ones128 = C.sb("ones128", [128, 128], F32)
    P.op("pool", lambda e: e.memset(ones64[:], 1.0), [], ["ones64"])
    P.op("pool", lambda e: e.memset(ones128[:], 1.0), [], ["ones128"])
    cosT = C.sb("cosT", [128, L], F32)
    sinT = C.sb("sinT", [128, L], F32)
    C.ld(cosT[:], I["rope1_cos"], [], ["cosT"])
    C.ld(sinT[:], I["rope1_sin"], [], ["sinT"])
    cwe = C.sb("cwe", [128, 12, 3], F32)
    for tap in range(3):
        for g0 in range(0, 12, 4):
            C.ld(cwe[:, g0:g0 + 4, tap], I["e_conv"][0, tap].rearrange("(g p) -> p g", p=128)[:, g0:g0 + 4], [], ["cwe"], slow=True)
    nw = C.sb("nw", [128, 1], F32)
    C.ld(nw[:], I["e_norm_w"][0].rearrange("(p o) -> p o", o=1), [], ["nw"], slow=True)
    sc8 = C.sb("sc8", [8, 4], F32)
    C.ld(sc8[:, 0:1], I["e_a_log"][0].rearrange("d (h o) -> (d h) o", o=1), [], ["sc8"], slow=True)
    C.ld(sc8[:, 1:2], I["e_dt_bias"][0].rearrange("d (h o) -> (d h) o", o=1), [], ["sc8"], slow=True)
    C.ld(sc8[:, 2:3], I["e_ret_decay"][0].rearrange("d (h o) -> (d h) o", o=1), [], ["sc8"], slow=True)
    C.act(sc8[:, 0:1], sc8[:, 0:1], AF.Exp, ["sc8"], ["sc8"])
    C.ts("dve", sc8[:, 0:1], sc8[:, 0:1], -1.0, None, ALU.mult, None, ["sc8"], ["sc8"])
    C.act(sc8[:, 2:3], sc8[:, 2:3], AF.Exp, ["sc8"], ["sc8"])
    C.ts("dve", sc8[:, 2:3], sc8[:, 2:3], -1.0, None, ALU.mult, None, ["sc8"], ["sc8"])

    hsb_off = C.sb_off
    hsb = C.sb("hsb", [128, 8, T], BF16)
    wab = C.sb("wab", [128, 8, 16], BF16)
    wts = [C.sb("wt%d" % i, [128, 8, 128], BF16) for i in range(4)]
    wrs = [C.sb("wr%d" % i, [128, 8, 128], BF16) for i in range(2)]
    raw = C.sb("raw", [128, T], F32)
    qT = C.sb("qT", [128, T], F32)
    kT = C.sb("kT", [128, T], F32)
    vT = C.sb("vT", [128, T], F32)
    szT = C.sb("szT", [128, T], BF16)
    tmp = C.sb("tmp", [128, 512], F32)
    tmp2 = C.sb("tmp2", [128, 512], F32)
    ktok = C.sb("ktok", [64, NCH, 128], F32)
    vtok = C.sb("vtok", [64, NCH, 128], F32)
    oacc_off = C.sb_off
    oacc = C.sb("oacc", [64, NCH, 128], F32)
    ab = C.sb_at("ab", [8, T], F32, oacc_off)
    ab2 = C.sb_at("ab2", [8, T], F32, oacc_off + T * 4)
    TTs = C.sb_at("TTs", [64, 2 * NCH, 64], F32, hsb_off)
    QKTs = C.sb_at("QKTs", [64, 2 * NCH, 64], F32, hsb_off + 2 * NCH * 64 * 4)
    yT = C.sb("yT", [128, T], BF16)
    la = [C.sb("la%d" % d, [64, NCH], F32) for d in range(2)]
    bet = [C.sb("bet%d" % d, [64, NCH], F32) for d in range(2)]
    G = [C.sb("G%d" % d, [64, NCH], F32) for d in range(2)]
    EG = [C.sb("EG%d" % d, [64, NCH], F32) for d in range(2)]
    EKG = [C.sb("EKG%d" % d, [64, NCH], F32) for d in range(2)]
    NBEG = [C.sb("NBEG%d" % d, [64, NCH], F32) for d in range(2)]
    CD = [C.sb("CD%d" % d, [128, NCH], F32) for d in range(2)]
    S = [C.sb("S%d" % d, [128, 128], F32) for d in range(2)]
    st = C.sb("st", [64, 4 * NCH], F32)
    lat = C.sb("lat", [NCH, 128], F32)
    NSM = 72
    sm = [C.sb("sm%d" % i, [64, 64], F32) for i in range(NSM)]
    NPM = 24
    Pm = [C.sb("Pm%d" % i, [64, 64], F32) for i in range(NPM)]
    NVB = 8
    vb = [C.sb("vb%d" % i, [64, 128], F32) for i in range(NVB)]
    smi = [0]

    tb = [0, 0]
    psA = {"banks": [2, 3, 4, 5], "bi": -1, "j": 8, "round": -1}
    psB = {"banks": [6, 7], "bi": -1, "j": 8, "round": -1}
    rnd = [0]

    def _pslot(st):
        if st["round"] != rnd[0] or st["j"] >= 8:
            st["round"] = rnd[0]
            st["bi"] = (st["bi"] + 1) % len(st["banks"])
            st["j"] = 0
        bk, j = st["banks"][st["bi"]], st["j"]
        st["j"] += 1
        return C.banks[bk][0:64, j * 64:(j + 1) * 64], ("ps", bk)

    def pslot():
        return _pslot(psA)

    def pslotB():
        return _pslot(psB)

    def pmat():
        i = tb[1] % NPM
        tb[1] += 1
        return Pm[i][:], ("Pm", i)

    def small():
        i = smi[0] % NSM
        smi[0] += 1
        return sm[i][:], ("sm", i)

    win = I["e_w_in"][0].rearrange("(k p) n -> p k n", p=128)
    LAD = C.dram["LAD"]

    def proj(w, t0, tn, wres):
        bk = C.bank()
        for k in range(8):
            C.mm(C.banks[bk][:, 0:tn], w[:, k, :], hsb[:, k, t0:t0 + tn], k == 0, k == 7, ["hsb"] + wres, [("ps", bk)])
        return bk

    tiles = [(i * 512, 512) for i in range(4)] + [(L, CT)]
    wloaded = [False]

    def load_head_weights(hd_):
        h4_ = hd_ % 4
        if hd_ < 4:
            cols_ = [h4_ * 128, 512 + h4_ * 128, 1024 + h4_ * 128, 1536 + h4_ * 128]
        else:
            cols_ = [2064 + h4_ * 128, 2576 + h4_ * 128, 3088 + h4_ * 128, 3600 + h4_ * 128]
        for i_ in range(4):
            C.ld(wts[i_][:], win[:, :, cols_[i_]:cols_[i_] + 128], [], [("wt", i_)], q="pool")

    for b in range(EV_NB):
        hv = C.dram["HT"][b].rearrange("(k p) t -> p k t", p=128)
        P.barrier()
        for k in range(8):
            C.ld(hsb[:, k, :], hv[:, k, :], [("HT", b)], ["hsb"])
        C.ld(wab[:], win[:, :, 2048:2064], [], ["wab"], q="pool")
        for (t0, tn) in tiles:
            bk = C.bank()
            for k in range(8):
                C.mm(C.banks[bk][0:8, 0:tn], wab[:, k, 0:8], hsb[:, k, t0:t0 + tn], k == 0, k == 7, ["hsb", "wab"], [("ps", bk)])
            C.act(ab[:, t0:t0 + tn], C.banks[bk][0:8, 0:tn], AF.Exp, [("ps", bk), "sc8"], ["ab"], bias=sc8[:, 1:2])
            bk = C.bank()
            for k in range(8):
                C.mm(C.banks[bk][0:8, 0:tn], wab[:, k, 8:16], hsb[:, k, t0:t0 + tn], k == 0, k == 7, ["hsb", "wab"], [("ps", bk)])
            C.act(ab2[:, t0:t0 + tn], C.banks[bk][0:8, 0:tn], AF.Sigmoid, [("ps", bk)], ["ab2"])
        C.ts("dve", ab[:], ab[:], 1.0, None, ALU.add, None, ["ab"], ["ab"])
        C.act(ab[:], ab[:], AF.Ln, ["ab"], ["ab"])
        C.ts("dve", ab[:], ab[:], sc8[:, 0:1], None, ALU.mult, None, ["ab", "sc8"], ["ab"])
        C.ld(LAD[b, 0:8, :], ab[:], ["ab"], [("LAD", b, 0)])
        C.ld(LAD[b, 8:16, :], ab2[:], ["ab2"], [("LAD", b, 1)])
        C.ts("dve", ab2[:], ab2[:], 0.0, None, ALU.mult, None, ["ab2"], ["ab2"])
        C.ts("dve", ab2[:], ab2[:], sc8[:, 2:3], None, ALU.add, None, ["ab2", "sc8"], ["ab2"])
        C.ld(LAD[b, 16:24, :], ab2[:], ["ab2"], [("LAD", b, 2)])
        ladres = [("LAD", b, 0), ("LAD", b, 1), ("LAD", b, 2)]
        if EV_STOP <= 1:
            return

        for hd in EV_HEADS:
            P.barrier()
            if hd != EV_HEADS[0]:
                for k in range(8):
                    C.ld(hsb[:, k, :], hv[:, k, :], [("HT", b)], ["hsb"])
            delta = hd < 4
            h4 = hd % 4
            if not wloaded[0]:
                load_head_weights(hd)
            wloaded[0] = False
            dsts = [qT, kT, vT]
            dres = ["qT", "kT", "vT"]
            if delta:
                for i in range(3):
                    for (t0, tn) in tiles:
                        bk = proj(wts[i], t0, tn, [("wt", i)])
                        C.cp("act", raw[:, t0:t0 + tn], C.banks[bk][:, 0:tn], [("ps", bk)], ["raw"])
                    gidx = i * 4 + h4
                    d_ = dsts[i]
                    for (s0, sn) in ((0, L), (L, CT)):
                        C.ts("dve", d_[:, s0:s0 + sn], raw[:, s0:s0 + sn], cwe[:, gidx, 1:2], None, ALU.mult, None, ["raw", "cwe"], [dres[i]])
                        C.stt("dve", d_[:, s0 + 1:s0 + sn], raw[:, s0:s0 + sn - 1], cwe[:, gidx, 0:1], d_[:, s0 + 1:s0 + sn], ALU.mult, ALU.add,
                              ["raw", "cwe", dres[i]], [dres[i]])
                        C.stt("dve", d_[:, s0:s0 + sn - 1], raw[:, s0 + 1:s0 + sn], cwe[:, gidx, 2:3], d_[:, s0:s0 + sn - 1], ALU.mult, ALU.add,
                              ["raw", "cwe", dres[i]], [dres[i]])
                    C.act(d_[:], d_[:], AF.Silu, [dres[i]], [dres[i]])
                    if i < 2:
                        for (t0, tn) in tiles:
                            C.act(tmp[:, 0:tn], d_[:, t0:t0 + tn], AF.Square, [dres[i]], ["tmp"])
                            bk = C.bank()
                            C.mm(C.banks[bk][:, 0:tn], ones128[:], tmp[:, 0:tn], True, True, ["ones128", "tmp"], [("ps", bk)])
                            C.ts("dve", tmp2[:, 0:tn], C.banks[bk][:, 0:tn], NORM_EPS, None, ALU.add, None, [("ps", bk)], ["tmp2"])
                            C.act(tmp2[:, 0:tn], tmp2[:, 0:tn], AF.Sqrt, ["tmp2"], ["tmp2"])
                            P.op("dve", (lambda o: (lambda e: e.reciprocal(out=o, in_=o)))(tmp2[:, 0:tn]), ["tmp2"], ["tmp2"])
                            C.stt("dve", d_[:, t0:t0 + tn], d_[:, t0:t0 + tn], DK_SCALE if i == 0 else 1.0, tmp2[:, 0:tn], ALU.mult, ALU.mult,
                                  [dres[i], "tmp2"], [dres[i]])
            else:
                for i in range(2):
                    rot_weights(C, wrs[i], wts[i], 64, ("wt", i), ("wr", i))
                    d_ = dsts[i]
                    sc_ = 1.0 if i == 0 else DK_SCALE
                    for (t0, tn) in tiles[:4]:
                        b1 = proj(wts[i], t0, tn, [("wt", i)])
                        b2 = proj(wrs[i], t0, tn, [("wr", i)])
                        C.tt("dve", tmp[:], C.banks[b1][:, :], cosT[:, t0:t0 + 512], ALU.mult, [("ps", b1), "cosT"], ["tmp"])
                        C.tt("dve", tmp2[:], C.banks[b2][:, :], sinT[:, t0:t0 + 512], ALU.mult, [("ps", b2), "sinT"], ["tmp2"])
                        if i == 0:
                            C.tt("dve", d_[:, t0:t0 + 512], tmp[:], tmp2[:], ALU.add, ["tmp", "tmp2"], [dres[i]])
                        else:
                            C.tt("dve", tmp[:], tmp[:], tmp2[:], ALU.add, ["tmp", "tmp2"], ["tmp"])
                            C.act(d_[:, t0:t0 + 512], tmp[:], AF.Identity, ["tmp"], [dres[i]], scale=sc_)
                    bk = proj(wts[i], L, CT, [("wt", i)])
                    C.act(d_[:, L:T], C.banks[bk][:, 0:CT], AF.Identity, [("ps", bk)], [dres[i]], scale=sc_)
                for (t0, tn) in tiles:
                    bk = proj(wts[2], t0, tn, [("wt", 2)])
                    C.cp("act", vT[:, t0:t0 + tn], C.banks[bk][:, 0:tn], [("ps", bk)], ["vT"])
            for (t0, tn) in tiles:
                bk = proj(wts[3], t0, tn, [("wt", 3)])
                C.act(szT[:, t0:t0 + tn], C.banks[bk][:, 0:tn], AF.Silu, [("ps", bk)], ["szT"])
            if EV_STOP <= 2:
                return
            for (src, sres, dst, dr) in ((kT, "kT", ktok, "ktok"), (vT, "vT", vtok, "vtok")):
                for n0 in range(0, NCH, 4):
                    bk = C.bank()
                    for q in range(4):
                        n = n0 + q
                        C.tr(C.banks[bk][0:64, q * 128:(q + 1) * 128], src[:, n * 64:(n + 1) * 64], ident[:], [sres, "ident"], [("ps", bk)])
                    C.cp("act" if (n0 // 4) % 2 == 0 else "dve", dst[:, n0:n0 + 4, :].rearrange("p a b -> p (a b)"), C.banks[bk][0:64, :],
                         [("ps", bk)], [dr])
            if EV_STOP <= 3:
                return
            for d in range(2):
                row = (d * 4 + h4) if delta else (16 + d * 4 + h4)
                C.ld(lat[:, 0:64], LAD[b, row].rearrange("(n i) -> n i", i=64), ladres, ["lat"])
                if delta:
                    C.ld(lat[:, 64:128], LAD[b, 8 + d * 4 + h4].rearrange("(n i) -> n i", i=64), ladres, ["lat"])
                bkt = C.bank()
                C.tr(C.banks[bkt][0:64, 0:NCH], lat[:, 0:64], ident[0:NCH, 0:NCH], ["lat", "ident"], [("ps", bkt)])
                if delta:
                    C.tr(C.banks[bkt][0:64, 64:64 + NCH], lat[:, 64:128], ident[0:NCH, 0:NCH], ["lat", "ident"], [("ps", bkt)])
                C.cp("dve", la[d][:], C.banks[bkt][0:64, 0:NCH], [("ps", bkt)], [("la", d)])
                if delta:
                    C.cp("dve", bet[d][:], C.banks[bkt][0:64, 64:64 + NCH], [("ps", bkt)], [("bet", d)])
                bk = C.bank()
                C.mm(C.banks[bk][0:64, 0:NCH], U[d], la[d][:], True, True, ["mk", ("la", d)], [("ps", bk)])
                C.cp("dve", G[d][:], C.banks[bk][0:64, 0:NCH], [("ps", bk)], [("G", d)])
                C.act(EG[d][:], C.banks[bk][0:64, 0:NCH], AF.Exp, [("ps", bk)], [("EG", d)])
                bk2 = C.bank()
                C.mm(C.banks[bk2][:, 0:NCH], ones64[:], la[d][:], True, True, ["ones64", ("la", d)], [("ps", bk2)])
                C.act(CD[d][:], C.banks[bk2][:, 0:NCH], AF.Exp, [("ps", bk2)], [("CD", d)])
                C.tt("dve", EKG[d][:], C.banks[bk2][0:64, 0:NCH], G[d][:], ALU.subtract, [("ps", bk2), ("G", d)], [("EKG", d)])
                C.act(EKG[d][:], EKG[d][:], AF.Exp, [("EKG", d)], [("EKG", d)])
                if delta:
                    C.stt("dve", NBEG[d][:], bet[d][:], -1.0, EG[d][:], ALU.mult, ALU.mult, [("bet", d), ("EG", d)], [("NBEG", d)])
            if EV_STOP <= 4:
                return
            P.barrier()
            hi_ = EV_HEADS.index(hd)
            nxt_hd = EV_HEADS[hi_ + 1] if hi_ + 1 < len(EV_HEADS) else (EV_HEADS[0] if b + 1 < EV_NB else None)
            if nxt_hd is not None:
                load_head_weights(nxt_hd)
                wloaded[0] = True
            for n0 in range(0, NCH, 4):
                P.op("pool", (lambda o: (lambda e: e.memset(o, 0.0)))(oacc[:, n0:n0 + 4, :].rearrange("p a b -> p (a b)")), [],
                     [("oacc", n) for n in range(n0, n0 + 4)])
            def chain(n, d, kk_ap, kk_r, qk_ap, qk_r):
                labc, labr = small()
                C.act(labc, ones64[:, 0:64], AF.Identity, ["ones64", ("la", d)], [labr], scale=la[d][:, n:n + 1])
                yield
                pg, pgr = pslot()
                C.mm(pg, labc, U[d], True, False, [labr, "mk"], [pgr])
                C.mm(pg, I64, POS[d], False, True, ["mk"], [pgr])
                yield
                dec, decr = small()
                C.act(dec, pg, AF.Exp, [pgr, ("G", d)], [decr], bias=G[d][:, n:n + 1], scale=-1.0)
                yield
                qk, qkr = small()
                C.tt("dve", qk, qk_ap, dec, ALU.mult, [qk_r, decr], [qkr])
                if delta:
                    decs, decsr = small()
                    C.tt("pool", decs, dec, STRICT[d], ALU.mult, [decr, "mk"], [decsr])
                yield
                pt, ptr = pslot()
                C.tr(pt, qk, I64, [qkr, "mk"], [ptr])
                if delta:
                    Y, Yr = small()
                    C.stt("dve", Y, kk_ap, bet[d][:, n:n + 1], decs, ALU.mult, ALU.mult, [kk_r, ("bet", d), decsr], [Yr])
                yield
                C.cp("act", QKTs[:, 2 * n + d, :], pt, [ptr], [("QKT", n, d)])
                if not delta:
                    return
                px, pxr = pslot()
                C.tr(px, Y, I64, [Yr, "mk"], [pxr])
                yield
                X, Xr = small()
                C.cp("act", X, px, [pxr], [Xr])
                yield
                Pc, Pr = pmat()
                C.tt("pool", Pc, I64, X, ALU.subtract, ["mk", Xr], [Pr])
                yield
                for k in range(5):
                    py, pyr = pslot()
                    C.mm(py, X, Y, True, True, [Xr, Yr], [pyr])
                    if k < 4:
                        px2, px2r = pslotB()
                        C.mm(px2, Y, X, True, True, [Xr, Yr], [px2r])
                    yield
                    Y2, Y2r = small()
                    C.cp("act", Y2, py, [pyr], [Y2r])
                    if k < 4:
                        X2, X2r = small()
                        C.cp("dve", X2, px2, [px2r], [X2r])
                    yield
                    pp, ppr = pslotB()
                    C.mm(pp, Y2, Pc, True, True, [Y2r, Pr], [ppr])
                    yield
                    if k < 4:
                        Pn, Pnr = pmat()
                    else:
                        Pn, Pnr = TTs[:, 2 * n + d, :], ("TT", n, d)
                    C.tt("dve", Pn, pp, Pc, ALU.add, [Pr, ppr], [Pnr])
                    Pc, Pr = Pn, Pnr
                    Y, Yr = Y2, Y2r
                    if k < 4:
                        X, Xr = X2, X2r
                    yield

            GC = int(os.environ.get("EV_GC", 4))
            for gi, n0 in enumerate(range(0, NCH, GC)):
                gens = []
                for q in range(GC):
                    n = n0 + q
                    ksl = kT[:, n * 64:(n + 1) * 64]
                    qsl = qT[:, n * 64:(n + 1) * 64]
                    bkq = gi % 2
                    kk_ap, kk_r = C.banks[bkq][0:64, q * 64:(q + 1) * 64], ("ps", bkq)
                    qk_ap, qk_r = C.banks[bkq][0:64, (4 + q) * 64:(5 + q) * 64], ("ps", bkq)
                    if delta:
                        C.mm(kk_ap, ksl, ksl, True, True, ["kT"], [kk_r])
                    C.mm(qk_ap, qsl, ksl, True, True, ["kT", "qT"], [qk_r])
                    for d in range(2):
                        gens.append(chain(n, d, kk_ap, kk_r, qk_ap, qk_r))
                lev = 0
                while gens:
                    nxt = []
                    lev += 1
                    rnd[0] += 1
                    if lev > int(os.environ.get("EV_LEV", 999)):
                        break
                    for g in gens:
                        try:
                            next(g)
                            nxt.append(g)
                        except StopIteration:
                            pass
                    gens = nxt
            if EV_STOP <= 5:
                return
            for d in range(2):
                P.op("pool", (lambda o: (lambda e: e.memset(o, 0.0)))(S[d][:]), [], [("S", d)])
            order = [list(range(32, 36)) + list(range(0, 32)), list(range(35, 31, -1)) + list(range(31, -1, -1))]
            nvbc = [0]

            def scan_step(step, d):
                n = order[d][step]
                ksl = kT[:, n * 64:(n + 1) * 64]
                qsl = qT[:, n * 64:(n + 1) * 64]
                Sd, Sr = S[d][:], ("S", d)
                vn = vb[nvbc[0] % NVB]
                vnr = ("vb", nvbc[0] % NVB)
                nvbc[0] += 1
                vs = vb[nvbc[0] % NVB]
                vsr = ("vb", nvbc[0] % NVB)
                nvbc[0] += 1
                ores = ("oacc", n)
                if delta:
                    b1 = C.bank()
                    C.mm(C.banks[b1][0:64, 0:128], ksl, Sd, True, True, ["kT", Sr], [("ps", b1)])
                    C.act(vs[:], vtok[:, n, :], AF.Identity, ["vtok", ("bet", d)], [vsr], scale=bet[d][:, n:n + 1])
                else:
                    C.act(vs[:], vtok[:, n, :], AF.Identity, ["vtok", ("EKG", d)], [vsr], scale=EKG[d][:, n:n + 1])
                b3 = C.bank()
                C.mm(C.banks[b3][0:64, 0:128], qsl, Sd, True, True, ["qT", Sr], [("ps", b3)])
                yield
                if delta:
                    C.stt("dve", vs[:], C.banks[b1][0:64, 0:128], NBEG[d][:, n:n + 1], vs[:], ALU.mult, ALU.add,
                          [("ps", b1), ("NBEG", d), vsr], [vsr])
                C.stt("dve", oacc[:, n, :], C.banks[b3][0:64, 0:128], EG[d][:, n:n + 1], oacc[:, n, :], ALU.mult, ALU.add,
                      [("ps", b3), ("EG", d), ores], [ores])
                C.act(Sd, Sd, AF.Identity, [Sr, ("CD", d)], [Sr], scale=CD[d][:, n:n + 1])
                yield
                if delta:
                    b2 = C.bank()
                    C.mm(C.banks[b2][0:64, 0:128], TTs[:, 2 * n + d, :], vs[:], True, True, [("TT", n, d), vsr], [("ps", b2)])
                    yield
                    C.cp("act", vn[:], C.banks[b2][0:64, 0:128], [("ps", b2)], [vnr])
                    yield
                    C.ts("dve", vs[:], C.banks[b2][0:64, 0:128], EKG[d][:, n:n + 1], None, ALU.mult, None, [("ps", b2), ("EKG", d)], [vsr])
                    yield
                    vnap = vn[:]
                else:
                    vnap = vtok[:, n, :]
                    vnr = "vtok"
                b4 = C.bank()
                C.mm(C.banks[b4][0:64, 0:128], QKTs[:, 2 * n + d, :], vnap, True, True, [("QKT", n, d), vnr], [("ps", b4)])
                b5 = C.bank()
                C.mm(C.banks[b5][:, 0:128], ktok[:, n, :], vs[:], True, True, ["ktok", vsr], [("ps", b5)])
                yield
                C.tt("dve", oacc[:, n, :], C.banks[b4][0:64, 0:128], oacc[:, n, :], ALU.add, [ores, ("ps", b4)], [ores])
                C.tt("dve", Sd, C.banks[b5][:, 0:128], Sd, ALU.add, [Sr, ("ps", b5)], [Sr])

            for step in range(NCH):
                gens = [scan_step(step, 0), scan_step(step, 1)]
                while gens:
                    nxt = []
                    for g in gens:
                        try:
                            next(g)
                            nxt.append(g)
                        except StopIteration:
                            pass
                    gens = nxt
            if EV_STOP <= 6:
                return
            ores_all = [("oacc", n) for n in range(NCH)]
            for n0 in range(0, NCH, 4):
                sl = oacc[:, n0:n0 + 4, :].rearrange("p a b -> p (a b)")
                C.act(tmp[0:64, :], sl, AF.Square, ores_all[n0:n0 + 4], ["tmp"])
                P.op("dve", (lambda o, i: (lambda e: e.reduce_sum(out=o, in_=i, axis=AX.X)))(
                    st[:, n0:n0 + 4], tmp[0:64, :].rearrange("p (a b) -> p a b", b=128)), ["tmp"], ["st"])
                P.op("dve", (lambda o, i: (lambda e: e.reduce_sum(out=o, in_=i, axis=AX.X)))(
                    st[:, NCH + n0:NCH + n0 + 4], oacc[:, n0:n0 + 4, :]), ores_all[n0:n0 + 4], ["st"])
            ssq = st[:, 0:NCH]
            ssum = st[:, NCH:2 * NCH]
            rstd = st[:, 2 * NCH:3 * NCH]
            mean = st[:, 3 * NCH:4 * NCH]
            C.ts("dve", mean, ssum, 1.0 / 128.0, None, ALU.mult, None, ["st"], ["st"])
            if delta:
                C.ts("dve", rstd, ssq, 1.0 / 128.0, NORM_EPS, ALU.mult, ALU.add, ["st"], ["st"])
            else:
                C.tt("dve", rstd, mean, mean, ALU.mult, ["st"], ["st"])
                C.stt("dve", rstd, ssq, 1.0 / 128.0, rstd, ALU.mult, ALU.subtract, ["st"], ["st"])
                C.ts("dve", rstd, rstd, NORM_EPS, None, ALU.add, None, ["st"], ["st"])
            C.act(rstd, rstd, AF.Sqrt, ["st"], ["st"])
            P.op("dve", (lambda o: (lambda e: e.reciprocal(out=o, in_=o)))(rstd), ["st"], ["st"])
            for n in range(NCH):
                if delta:
                    C.ts("dve", oacc[:, n, :], oacc[:, n, :], rstd[:, n:n + 1], None, ALU.mult, None,
                         [("oacc", n), "st"], [("oacc", n)])
                else:
                    C.ts("dve", oacc[:, n, :], oacc[:, n, :], mean[:, n:n + 1], rstd[:, n:n + 1], ALU.subtract, ALU.mult,
                         [("oacc", n), "st"], [("oacc", n)])
            for n0 in range(0, NCH, 8):
                nn = min(8, NCH - n0)
                bk = C.bank()
                for q in range(nn):
                    C.tr(C.banks[bk][:, q * 64:(q + 1) * 64], oacc[:, n0 + q, :], I64, [("oacc", n0 + q), "mk"], [("ps", bk)])
                t0 = n0 * 64
                tn = nn * 64
                if delta:
                    C.stt("dve", yT[:, t0:t0 + tn], C.banks[bk][:, 0:tn], nw[:, 0:1], szT[:, t0:t0 + tn], ALU.mult, ALU.mult,
                          [("ps", bk), "nw", "szT"], ["yT"])
                else:
                    C.tt("dve", yT[:, t0:t0 + tn], C.banks[bk][:, 0:tn], szT[:, t0:t0 + tn], ALU.mult, [("ps", bk), "szT"], ["yT"])
            C.ld(C.dram["YT"][b, hd * 128:(hd + 1) * 128, :], yT[:], ["yT"], [("Yw", b, hd)])


def build(stages, dbg=()):
    nc = bass.Bass("TRN2", target_bir_lowering=False)
    C = Ctx(nc)
    I = {}

    def inp(name, shape):
        I[name] = nc.dram_tensor(name, list(shape), F32, kind="ExternalInput").ap()

    inp("x", (NB, L, D)); inp("ctx", (NB, CT, D)); inp("cvec", (3, D))
    inp("mod_w", (2, D, 6 * D)); inp("mod_b", (2, 6 * D)); inp("ln_g", (2, 2, D)); inp("ln_b", (2, 2, D))
    inp("e_w_in", (1, D, EVEN_IN)); inp("e_conv", (1, 3, 1536)); inp("e_a_log", (1, 2, 4)); inp("e_dt_bias", (1, 2, 4))
    inp("e_norm_w", (1, 128)); inp("e_ret_decay", (1, 2, 4)); inp("e_w_out", (1, D, D))
    inp("o_w_qkv", (1, D, 3 * D)); inp("o_lambda", (1, 4, 64)); inp("o_subln_w", (1, 128)); inp("o_w_out", (1, D, D))
    inp("f_w_gate", (2, D, DFF)); inp("f_w_up", (2, D, DFF)); inp("f_conv", (2, 3, 3, DFF)); inp("f_w_down", (2, DFF, D))
    inp("ident", (128, 128)); inp("rope2_cos", (128, L)); inp("rope2_sin", (128, L))
    inp("rope1_cos", (128, L)); inp("rope1_sin", (128, L)); inp("masks", (8, 64, 64))
    I["out"] = nc.dram_tensor("out", [NB, L, D], F32, kind="ExternalOutput").ap()
    C.I = I
    def kind_of(nm):
        if nm in dbg:
            return "ExternalOutput"
        if nm + "in" in dbg:
            return "ExternalInput"
        return "Internal"

    C.dram["modv"] = nc.dram_tensor("modv", [2, 3, 6 * D], F32, kind=kind_of("modv")).ap()
    for nm in ("XA", "XB"):
        C.dram[nm] = nc.dram_tensor(nm, [NB, T, D], F32, kind=kind_of(nm)).ap()
    C.dram["HT"] = nc.dram_tensor("HT", [NB, D, T], BF16, kind=kind_of("HT")).ap()
    C.dram["YT"] = nc.dram_tensor("YT", [NB, D, T], BF16, kind=kind_of("YT")).ap()
    C.dram["GT"] = nc.dram_tensor("GT", [NB, DFF, T], BF16, kind=kind_of("GT")).ap()
    C.dram["LAD"] = nc.dram_tensor("LAD", [NB, 24, T], F32, kind=kind_of("LAD")).ap()
    S = stages
    if "mod" in S:
        stage_mod(C)
    if "modT0" in S:
        stage_modT0(C)
    if "even" in S:
        stage_even(C)
    if "pa0" in S:
        stage_projln(C, 0, "a", C.dram["YT"], 8, I["e_w_out"][0], "input", "XA", NT, (0, "f"))
    if "ffn0" in S:
        stage_ffn1(C, 0, True)
    if "pf0" in S:
        stage_projln(C, 0, "f", C.dram["GT"], 22, I["f_w_down"][0], "XA", "XB", NT, (1, "a"))
    if "odd" in S:
        stage_odd(C)
    if "pa1" in S:
        stage_projln(C, 1, "a", C.dram["YT"], 8, I["o_w_out"][0], "XB", "XA", 16, (1, "f"))
    if "ffn1" in S:
        stage_ffn1(C, 1, False)
    if "pf1" in S:
        stage_projln(C, 1, "f", C.dram["GT"], 22, I["f_w_down"][1], "XA", "out", 16, None)
    C.P.emit()
    return nc


def make_consts():
    c = {}
    c["ident"] = np.eye(128, dtype=np.float32)
    pos = np.arange(L)
    inv2 = 10000.0 ** (-np.arange(16, dtype=np.float64) / 16)
    ang2 = np.concatenate([(pos // 64)[:, None] * inv2[None], (pos % 64)[:, None] * inv2[None]], -1)
    idx2 = (np.arange(128) % 64) % 32
    c["rope2_cos"] = np.ascontiguousarray(np.cos(ang2)[:, idx2].T.astype(np.float32))
    c["rope2_sin"] = np.ascontiguousarray(np.sin(ang2)[:, idx2].T.astype(np.float32))
    inv1 = 10000.0 ** (-np.arange(64, dtype=np.float64) / 64)
    ang1 = pos[:, None] * inv1[None]
    idx1 = np.arange(128) % 64
    c["rope1_cos"] = np.ascontiguousarray(np.cos(ang1)[:, idx1].T.astype(np.float32))
    c["rope1_sin"] = np.ascontiguousarray(np.sin(ang1)[:, idx1].T.astype(np.float32))
    c["masks"] = make_masks()
    return c


def make_masks():
    m = np.zeros((8, 64, 64), np.float32)
    a = np.arange(64)
    le = (a[:, None] <= a[None, :]).astype(np.float32)
    m[0] = le
    m[1] = le.T
    m[2] = np.where(a[None, :] <= a[:, None], 0.0, -NEG)
    m[3] = np.where(a[None, :] >= a[:, None], 0.0, -NEG)
    m[4] = (a[None, :] < a[:, None]).astype(np.float32)
    m[5] = (a[None, :] > a[:, None]).astype(np.float32)
    m[6] = np.eye(64, dtype=np.float32)
    m[7] = 1.0
    return m


ALL_STAGES = ["mod", "modT0", "even", "pa0", "ffn0", "pf0", "odd", "pa1", "ffn1", "pf1"]
_NC_CACHE = {}


def kernel(**inputs):
    n = 8
    if "nc" not in _NC_CACHE:
        _NC_CACHE["nc"] = build(ALL_STAGES)
    nc = _NC_CACHE["nc"]
    consts = make_consts()
    shared = {k: np.ascontiguousarray(np.asarray(v, dtype=np.float32)) for k, v in inputs.items()
              if k not in ("x", "c", "ctx", "c_ctx")}
    x = np.asarray(inputs["x"], dtype=np.float32)
    ctx = np.asarray(inputs["ctx"], dtype=np.float32)
    c = np.asarray(inputs["c"], dtype=np.float32)
    c_ctx = np.asarray(inputs["c_ctx"], dtype=np.float32)
    in_maps = []
    for i in range(n):
        m = dict(shared)
        m.update(consts)
        m["x"] = np.ascontiguousarray(x[NB * i:NB * (i + 1)])
        m["ctx"] = np.ascontiguousarray(ctx[NB * i:NB * (i + 1)])
        m["cvec"] = np.ascontiguousarray(np.stack([c[NB * i], c[NB * i + 1], c_ctx]))
        in_maps.append(m)
    res = run_bass_kernel_spmd(nc, in_maps, core_ids=list(range(n)))
    return np.concatenate([r["out"] for r in res.results], axis=0).astype(np.float32)
```
